# Optimizing a Trainium2 kernel written in Bass

```python
import jax, jax.numpy as jnp
from jax import lax
import numpy as np

D_MODEL = 1024
BATCH = 8
SEQ = 4096
DEPTH = 1

ROPE_THETA = 500000.0
NORM_EPS = 1e-6
Q_BLOCK = 128

MLA_HEADS = 8
MLA_Q_RANK = 384
MLA_KV_RANK = 256
MLA_NOPE_DIM = 64
MLA_ROPE_DIM = 32
MLA_V_DIM = 64
MLA_WIDTH = MLA_HEADS * MLA_V_DIM

DIL_PATTERNS = ((128, 1), (512, 4), (2048, 16))
DIL_GROUPS = 3
DIL_HEADS_PER_GROUP = 8
DIL_HEADS = DIL_GROUPS * DIL_HEADS_PER_GROUP
DIL_HEAD_DIM = 64
DIL_ROPE_DIM = DIL_HEAD_DIM // 4
DIL_WIDTH = DIL_HEADS_PER_GROUP * DIL_HEAD_DIM

IN_SPLITS = (
    MLA_Q_RANK,
    MLA_KV_RANK,
    MLA_ROPE_DIM,
    3 * DIL_HEADS * DIL_HEAD_DIM,
    MLA_WIDTH,
    DIL_WIDTH,
    D_MODEL,
    D_MODEL,
)
IN_WIDTH = sum(IN_SPLITS)

kernel_name = "hybrid_mla_dilated_gated_parallel"


def rms_norm(x, g):
    xf = x.astype(jnp.float32)
    xf = xf * lax.rsqrt(jnp.mean(xf * xf, axis=-1, keepdims=True) + NORM_EPS)
    return (xf * g.astype(jnp.float32)).astype(x.dtype)


def rope_tables(positions, rot_dim):
    inv_freq = ROPE_THETA ** (-jnp.arange(0, rot_dim, 2, dtype=jnp.float32) / rot_dim)
    ang = positions.astype(jnp.float32)[..., None] * inv_freq
    return jnp.cos(ang), jnp.sin(ang)


def apply_rope(x, cos, sin):
    half = x.shape[-1] // 2
    xf = x.astype(jnp.float32)
    x1, x2 = xf[..., :half], xf[..., half:]
    return jnp.concatenate([x1 * cos - x2 * sin, x2 * cos + x1 * sin], axis=-1).astype(x.dtype)


def partial_rope(x, cos, sin, rot_dim):
    return jnp.concatenate([apply_rope(x[..., :rot_dim], cos, sin), x[..., rot_dim:]], axis=-1)


def mla_attention(c_q, c_kv, k_rope, q_norm_g, w_uq, kv_norm_g, w_ukv, cos, sin):
    B, S, _ = c_q.shape
    q = (rms_norm(c_q, q_norm_g) @ w_uq).reshape(B, S, MLA_HEADS, MLA_NOPE_DIM + MLA_ROPE_DIM)
    q_nope, q_rope = q[..., :MLA_NOPE_DIM], q[..., MLA_NOPE_DIM:]
    q_rope = apply_rope(q_rope, cos[:, :, None, :], sin[:, :, None, :])
    k_rope = apply_rope(k_rope, cos, sin)
    kv = (rms_norm(c_kv, kv_norm_g) @ w_ukv).reshape(B, S, MLA_HEADS, MLA_NOPE_DIM + MLA_V_DIM)
    k_nope, v = kv[..., :MLA_NOPE_DIM], kv[..., MLA_NOPE_DIM:]
    scale = (MLA_NOPE_DIM + MLA_ROPE_DIM) ** -0.5
    nb = S // Q_BLOCK
    qn_b = q_nope.reshape(B, nb, Q_BLOCK, MLA_HEADS, MLA_NOPE_DIM).transpose(1, 0, 2, 3, 4)
    qr_b = q_rope.reshape(B, nb, Q_BLOCK, MLA_HEADS, MLA_ROPE_DIM).transpose(1, 0, 2, 3, 4)
    key_idx = jnp.arange(S)

    def one_block(args):
        qn, qr, start = args
        s = (jnp.einsum('bqhd,bkhd->bhqk', qn, k_nope)
             + jnp.einsum('bqhr,bkr->bhqk', qr, k_rope)).astype(jnp.float32) * scale
        q_idx = start + jnp.arange(Q_BLOCK)
        s = jnp.where(key_idx[None, :] <= q_idx[:, None], s, -jnp.inf)
        p = jax.nn.softmax(s, axis=-1)
        return jnp.einsum('bhqk,bkhd->bqhd', p.astype(v.dtype), v)

    out = lax.map(one_block, (qn_b, qr_b, jnp.arange(nb) * Q_BLOCK))
    return out.transpose(1, 0, 2, 3, 4).reshape(B, S, MLA_WIDTH)


def dilated_group(q, k, v, window, dilation):
    B, S, H, Dh = q.shape
    L = S // dilation
    sub_window = window // dilation
    L_pad = -(-L // Q_BLOCK) * Q_BLOCK
    BD = B * dilation
    nb = L_pad // Q_BLOCK

    def to_sub(t):
        t = t.reshape(B, L, dilation, H, Dh).transpose(0, 2, 3, 1, 4).reshape(BD, H, L, Dh)
        return jnp.pad(t, ((0, 0), (0, 0), (0, L_pad - L), (0, 0)))

    qs, ks, vs = to_sub(q), to_sub(k), to_sub(v)
    qb = qs.reshape(BD, H, nb, Q_BLOCK, Dh)

    def band(t):
        tp = jnp.pad(t, ((0, 0), (0, 0), (Q_BLOCK, 0), (0, 0)))
        prev = tp[:, :, :L_pad].reshape(BD, H, nb, Q_BLOCK, Dh)
        cur = t.reshape(BD, H, nb, Q_BLOCK, Dh)
        return jnp.concatenate([prev, cur], axis=3)

    kb, vb = band(ks), band(vs)
    s = jnp.einsum('zhnqd,zhnkd->zhnqk', qb, kb).astype(jnp.float32) * (Dh ** -0.5)
    blk = jnp.arange(nb)[:, None, None] * Q_BLOCK
    qi = blk + jnp.arange(Q_BLOCK)[None, :, None]
    kj = blk - Q_BLOCK + jnp.arange(2 * Q_BLOCK)[None, None, :]
    dist = qi - kj
    mask = (dist >= 0) & (dist <= sub_window) & (kj >= 0)
    s = jnp.where(mask, s, -jnp.inf)
    m = jnp.max(s, axis=-1, keepdims=True)
    p = jnp.exp(s - m)
    denom = jnp.sum(p, axis=-1, keepdims=True)
    o = jnp.einsum('zhnqk,zhnkd->zhnqd', (p / denom).astype(v.dtype), vb)
    lse = (m + jnp.log(denom))[..., 0]
    o = o.reshape(BD, H, L_pad, Dh)[:, :, :L]
    o = o.reshape(B, dilation, H, L, Dh).transpose(0, 3, 1, 2, 4).reshape(B, S, H, Dh)
    lse = lse.reshape(BD, H, L_pad)[:, :, :L]
    lse = lse.reshape(B, dilation, H, L).transpose(0, 3, 1, 2).reshape(B, S, H)
    return o, lse


def dilated_attention(qkv, cos, sin):
    B, S, _ = qkv.shape
    qkv = qkv.reshape(B, S, 3, DIL_GROUPS, DIL_HEADS_PER_GROUP, DIL_HEAD_DIM)
    c, s_ = cos[:, :, None, None, :], sin[:, :, None, None, :]
    q = partial_rope(qkv[:, :, 0], c, s_, DIL_ROPE_DIM)
    k = partial_rope(qkv[:, :, 1], c, s_, DIL_ROPE_DIM)
    v = qkv[:, :, 2]
    outs, lses = [], []
    for g, (window, dilation) in enumerate(DIL_PATTERNS):
        o, lse = dilated_group(q[:, :, g], k[:, :, g], v[:, :, g], window, dilation)
        outs.append(o)
        lses.append(lse)
    alpha = jax.nn.softmax(jnp.stack(lses, axis=0), axis=0)
    out = jnp.sum(alpha[..., None].astype(v.dtype) * jnp.stack(outs, axis=0), axis=0)
    return out.reshape(B, S, DIL_WIDTH)


def setup_inputs(seed: int = 0) -> dict:
    key = jax.random.key(seed)
    ks = jax.random.split(key, 13)
    f32 = jnp.float32

    def w(k, shape, fan_in):
        return jax.random.normal(k, shape, f32) * (fan_in ** -0.5)

    def gain(k, n):
        return 1.0 + 0.02 * jax.random.normal(k, (DEPTH, n), f32)

    x = jax.random.normal(ks[0], (BATCH, SEQ, D_MODEL), f32)
    start = jax.random.randint(ks[1], (BATCH, 1), 0, 4096, dtype=jnp.int32)
    positions = start + jnp.arange(SEQ, dtype=jnp.int32)[None, :]
    return {
        "x": x,
        "positions": positions,
        "pre_norm_g": gain(ks[2], D_MODEL),
        "w_in": w(ks[3], (DEPTH, D_MODEL, IN_WIDTH), D_MODEL),
        "q_norm_g": gain(ks[4], MLA_Q_RANK),
        "w_uq": w(ks[5], (DEPTH, MLA_Q_RANK, MLA_HEADS * (MLA_NOPE_DIM + MLA_ROPE_DIM)), MLA_Q_RANK),
        "kv_norm_g": gain(ks[6], MLA_KV_RANK),
        "w_ukv": w(ks[7], (DEPTH, MLA_KV_RANK, MLA_HEADS * (MLA_NOPE_DIM + MLA_V_DIM)), MLA_KV_RANK),
        "w_proj_mla": w(ks[8], (DEPTH, MLA_WIDTH, D_MODEL), MLA_WIDTH),
        "w_proj_dil": w(ks[9], (DEPTH, DIL_WIDTH, D_MODEL), DIL_WIDTH),
        "w_out": w(ks[10], (DEPTH, D_MODEL, D_MODEL), D_MODEL),
        "post_norm_g": gain(ks[11], D_MODEL),
    }


def reference(x, positions, pre_norm_g, w_in, q_norm_g, w_uq, kv_norm_g, w_ukv,
              w_proj_mla, w_proj_dil, w_out, post_norm_g):
    cos_mla, sin_mla = rope_tables(positions, MLA_ROPE_DIM)
    cos_dil, sin_dil = rope_tables(positions, DIL_ROPE_DIM)
    split_at = [int(i) for i in np.cumsum(IN_SPLITS)[:-1]]
    for layer in range(DEPTH):
        h = rms_norm(x, pre_norm_g[layer])
        proj = h @ w_in[layer]
        c_q, c_kv, k_rope, qkv_dil, z_mla, z_dil, g_mla, g_dil = jnp.split(proj, split_at, axis=-1)
        y_mla = mla_attention(c_q, c_kv, k_rope, q_norm_g[layer], w_uq[layer],
                              kv_norm_g[layer], w_ukv[layer], cos_mla, sin_mla)
        y_dil = dilated_attention(qkv_dil, cos_dil, sin_dil)
        y_mla = (y_mla * jax.nn.silu(z_mla)) @ w_proj_mla[layer]
        y_dil = (y_dil * jax.nn.silu(z_dil)) @ w_proj_dil[layer]
        merged = jax.nn.sigmoid(g_mla) * y_mla + jax.nn.sigmoid(g_dil) * y_dil
        x = x + rms_norm(merged @ w_out[layer], post_norm_g[layer])
    return x
```

```python
import math
import numpy as np
import concourse.bass as bass
import concourse.mybir as mybir
from concourse.bass_utils import run_bass_kernel_spmd
from contextlib import ExitStack

F32 = mybir.dt.float32
BF16 = mybir.dt.bfloat16
I32 = mybir.dt.int32
AF = mybir.ActivationFunctionType
ALU = mybir.AluOpType

S = 4096
TT = 512
NT = S // TT
EPS = 1e-6
NEG = -30000.0
C1 = 6.28125
C2 = 2.0 * math.pi - 6.28125
O_CQ, O_CKV, O_KR, O_DIL, O_ZM, O_ZD, O_GM, O_GD = 0, 384, 640, 672, 5280, 5792, 6304, 7328
SC_MLA = 96.0 ** -0.5
SC_DIL = 64.0 ** -0.5


class Prog:
    ENG = ('pe', 'act', 'dve', 'pool', 'sp')

    def __init__(self, nc, es):
        self.nc = nc
        self.es = es
        self.streams = {e: [] for e in self.ENG}
        self.sem = {}
        self.cnt = {}
        self.dsems = []
        self.dead = False
        for e in self.ENG[:4]:
            self._newsem(e)

    def _newsem(self, name):
        self.sem[name] = self.es.enter_context(self.nc.semaphore(name))
        self.cnt[name] = 0
        return name

    def dsem(self, name):
        self.dsems.append(name)
        return self._newsem(name)

    def op(self, eng, fn, waits=(), sig=True):
        if self.dead:
            return (eng, self.cnt[eng])
        tok = None
        if sig:
            self.cnt[eng] += 1
            tok = (eng, self.cnt[eng])
        self.streams[eng].append((fn, [w for w in waits if w is not None], (eng, 1) if sig else None))
        return tok

    def dma(self, q, out, in_, sem, waits=()):
        if self.dead:
            return (sem, self.cnt[sem])
        self.cnt[sem] += 16
        tok = (sem, self.cnt[sem])
        self.streams[q].append((lambda e: e.dma_start(out=out, in_=in_), [w for w in waits if w is not None], (sem, 16)))
        return tok

    def wait(self, eng, waits):
        if self.dead:
            return
        self.streams[eng].append((None, [w for w in waits if w is not None], None))

    def barrier(self):
        toks = [(s, self.cnt[s]) for s in list(self.ENG[:4]) + self.dsems if self.cnt[s] > 0]
        for e in self.ENG:
            self.wait(e, toks)

    def mm(self, out, lhsT, rhs, start=True, stop=True, waits=(), sig=False):
        return self.op('pe', lambda e: e.matmul(out, lhsT=lhsT, rhs=rhs, start=start, stop=stop), waits, sig)

    def act(self, out, in_, func, scale=None, bias=None, accum=None, waits=(), sig=True):
        kw = {}
        if scale is not None:
            kw['scale'] = scale
        if bias is not None:
            kw['bias'] = bias
        if accum is not None:
            kw['accum_out'] = accum
        return self.op('act', lambda e: e.activation(out=out, in_=in_, func=func, **kw), waits, sig)

    def tt(self, eng, out, in0, in1, op, waits=(), sig=True):
        return self.op(eng, lambda e: e.tensor_tensor(out=out, in0=in0, in1=in1, op=op), waits, sig)

    def ts(self, eng, out, in0, s1, op0, s2=None, op1=None, waits=(), sig=True):
        if op1 is None:
            return self.op(eng, lambda e: e.tensor_scalar(out=out, in0=in0, scalar1=s1, scalar2=None, op0=op0), waits, sig)
        return self.op(eng, lambda e: e.tensor_scalar(out=out, in0=in0, scalar1=s1, scalar2=s2, op0=op0, op1=op1), waits, sig)

    def stt(self, out, in0, scalar, in1, op0, op1, waits=(), sig=True):
        return self.op('dve', lambda e: e.scalar_tensor_tensor(out=out, in0=in0, scalar=scalar, in1=in1, op0=op0, op1=op1), waits, sig)

    def copy(self, eng, out, in_, waits=(), sig=True):
        if eng == 'act':
            return self.act(out, in_, AF.Copy, waits=waits, sig=sig)
        return self.op(eng, lambda e: e.tensor_copy(out=out, in_=in_), waits, sig)

    def recip(self, out, in_, waits=(), sig=True):
        return self.op('dve', lambda e: e.reciprocal(out=out, in_=in_), waits, sig)

    def memset(self, eng, ap, val, waits=(), sig=True):
        return self.op(eng, lambda e: e.memset(ap, val), waits, sig)

    def check(self):
        pos = {e: 0 for e in self.ENG}
        val = {s: 0 for s in self.sem}
        progress = True
        while progress:
            progress = False
            for e in self.ENG:
                st = self.streams[e]
                while pos[e] < len(st):
                    fn, waits, inc = st[pos[e]]
                    if all(val[s] >= v for (s, v) in waits):
                        if inc is not None:
                            val[inc[0]] += inc[1]
                        pos[e] += 1
                        progress = True
                    else:
                        break
        bad = {e: (pos[e], len(self.streams[e])) for e in self.ENG if pos[e] < len(self.streams[e])}
        if bad:
            msg = []
            for e, (p, n) in bad.items():
                fn, waits, inc = self.streams[e][p]
                msg.append("%s stuck at %d/%d waits=%s have=%s" % (e, p, n, [w for w in waits if val[w[0]] < w[1]], [(w[0], val[w[0]]) for w in waits if val[w[0]] < w[1]]))
            raise RuntimeError("DEADLOCK: " + "; ".join(msg))

    def emit(self):
        self.check()
        nc = self.nc
        engobj = {'pe': 'tensor', 'act': 'scalar', 'dve': 'vector', 'pool': 'gpsimd', 'sp': 'sync'}
        with nc.Block() as block:
            for en in self.ENG:
                stream = self.streams[en]

                def body(engine, stream=stream):
                    waited = {}
                    for fn, waits, inc in stream:
                        for (s, v) in waits:
                            if waited.get(s, 0) >= v:
                                continue
                            engine.wait_ge(self.sem[s], v)
                            waited[s] = v
                        if fn is not None:
                            ins = fn(engine)
                            if inc is not None:
                                ins.then_inc(self.sem[inc[0]], inc[1])
                getattr(block, engobj[en])(body)


class Ring:
    def __init__(self, items):
        self.items = list(items)
        self.free = [[] for _ in self.items]
        self.busy = [False for _ in self.items]
        self.i = 0

    def get(self):
        k = self.i % len(self.items)
        self.i += 1
        assert not self.busy[k], "ring slot re-acquired before its release was emitted"
        self.busy[k] = True
        w = self.free[k]
        self.free[k] = []
        return k, self.items[k], w

    def rel(self, k, toks):
        self.free[k] = [t for t in toks if t is not None]
        self.busy[k] = False


class _Stop(Exception):
    pass


def build(debug=False, stop_after=99):
    nc = bass.Bass("TRN2", target_bir_lowering=False)

    def din(name, shape, dt=F32):
        return nc.dram_tensor(name, shape, dt, kind="ExternalInput").ap()

    xT = din("xT", [1024, S]); x = din("x", [S, 1024]); pos = din("pos", [1, S], I32)
    w_in = din("w_in", [1024, 8352]); w_uq = din("w_uq", [384, 768]); w_ukv = din("w_ukv", [256, 1024])
    w_pm = din("w_pm", [512, 1024]); w_pd = din("w_pd", [512, 1024]); w_out = din("w_out", [1024, 1024])
    ident_d = din("ident", [128, 128]); mask_d = din("maskab", [128, 256]); rot_d = din("rotm", [128, 224])
    freq_d = din("freqs", [128, 2]); gpre_d = din("gpre", [128, 8]); gq_d = din("gq", [128, 3])
    gkv_d = din("gkv", [128, 2]); gpost_d = din("gpost", [1, 1024])
    m01_d = din("mask01", [128, 256])
    y = nc.dram_tensor("y", [S, 1024], F32, kind="ExternalOutput").ap()
    sk = "ExternalOutput" if debug else "Internal"
    QM = nc.dram_tensor("QM", [8, 96, S], BF16, kind=sk).ap()
    KM = nc.dram_tensor("KM", [8, 96, S], BF16, kind=sk).ap()
    VM = nc.dram_tensor("VM", [8, 128, 32 * 128], BF16, kind=sk).ap()
    UM = nc.dram_tensor("UM", [1024, S], BF16, kind=sk).ap()
    WGb = nc.dram_tensor("WGb", [16, 128, 8, 128], BF16, kind="Internal").ap()
    WPMb = nc.dram_tensor("WPMb", [512, 1024], BF16, kind="Internal").ap()
    WPDb = nc.dram_tensor("WPDb", [512, 1024], BF16, kind="Internal").ap()
    WOb = nc.dram_tensor("WOb", [1024, 1024], BF16, kind="Internal").ap()

    w_in_v = w_in.rearrange("(c p) n -> p c n", p=128)
    xT_v = xT.rearrange("(c p) t -> p c t", p=128)

    with ExitStack() as es:
        P = Prog(nc, es)

        uid = [0]

        def sbt(stack, name, shape, dt=F32):
            uid[0] += 1
            return stack.enter_context(nc.sbuf_tensor('s%d_%s' % (uid[0], name), shape, dt))

        banks = [es.enter_context(nc.psum_tensor(f"ps{i}", [128, 512], F32)) for i in range(8)]

        hT = sbt(es, "hT", [128, 8, S], BF16)
        ident = sbt(es, "ident", [128, 128], BF16)
        maskab = sbt(es, "maskab", [128, 256], BF16)
        rotm = sbt(es, "rotm", [128, 224], BF16)
        ones = sbt(es, "ones", [128, 128], F32)
        freqs = sbt(es, "freqs", [128, 2], F32)
        gpre = sbt(es, "gpre", [128, 8], F32)
        gq = sbt(es, "gq", [128, 3], F32)
        gkv = sbt(es, "gkv", [128, 2], F32)
        epsb = sbt(es, "epsb", [128, 1], F32)
        hpib = sbt(es, "hpib", [128, 1], F32)

        dc = P.dsem("dcst")
        dcp = P.dsem("dcstp")
        P.dma('pool', ident[:], ident_d, dcp)
        P.dma('pool', maskab[:], mask_d, dcp)
        k_cstp = P.dma('pool', rotm[:], rot_d, dcp)
        P.dma('sp', freqs[:], freq_d, dc)
        P.dma('sp', gpre[:], gpre_d, dc)
        P.dma('sp', gq[:], gq_d, dc)
        k_cst = P.dma('sp', gkv[:], gkv_d, dc)
        k_ones = P.memset('dve', ones[:], 1.0)
        k_eps = P.memset('dve', epsb[:], EPS)
        k_hpi = P.memset('dve', hpib[:], math.pi / 2)
        CST = [k_cst, k_cstp, k_ones, k_eps, k_hpi]

        def gen_tables_a(stack, fcol, tag, pk_dt):
            CH = 1024
            posi = sbt(stack, "tb_posi", [128, CH], I32)
            ang = sbt(stack, "tb_ang", [128, CH])
            xx = sbt(stack, "tb_x", [128, CH])
            ki = sbt(stack, "tb_ki", [128, CH], I32)
            kf = sbt(stack, "tb_kf", [128, CH])
            gg = [sbt(stack, "tb_g0", [128, CH]), sbt(stack, "tb_g1", [128, CH])]
            pk = [sbt(stack, "tb_pk0", [128, CH], pk_dt), sbt(stack, "tb_pk1", [128, CH], pk_dt)]
            dsm = P.dsem("dtb" + tag)
            kd = None
            for q in range(4):
                kd = P.dma('sp', posi[q * 32:(q + 1) * 32, :], pos[:, q * CH:(q + 1) * CH].broadcast_to([32, CH]), dsm)
            k0 = P.copy('dve', xx[:], posi[:], [kd])
            k1 = P.ts('dve', ang[:], xx[:], freqs[:, fcol:fcol + 1], ALU.mult, waits=[k0] + CST)
            last = [k1]
            khs = []
            for which in (0, 1):
                ka = P.ts('dve', xx[:], ang[:], 1.0 / (2 * math.pi), ALU.mult, 0.25 * which, ALU.add, waits=last)
                kb = P.copy('dve', ki[:], xx[:], [ka])
                kc = P.copy('dve', kf[:], ki[:], [kb])
                kdd = P.tt('dve', xx[:], xx[:], kf[:], ALU.subtract, [kc])
                ke = P.ts('dve', gg[which][:], xx[:], 0.5, ALU.is_gt, waits=[kdd])
                kff = P.tt('dve', kf[:], kf[:], gg[which][:], ALU.add, [ke])
                kg = P.stt(xx[:], kf[:], -C1, ang[:], ALU.mult, ALU.add, [kff])
                kh0 = P.stt(gg[which][:], kf[:], -C2, xx[:], ALU.mult, ALU.add, [kg])
                lim = 3.14159
                lo, hi = (-lim, lim) if which == 0 else (-lim - math.pi / 2, lim - math.pi / 2)
                kh = P.ts('dve', gg[which][:], gg[which][:], lo, ALU.max, hi, ALU.min, waits=[kh0])
                last = [kh]
                khs.append(kh)
            return dict(gg=gg, pk=pk, khs=khs, CH=CH)

        def gen_tables_b(stt_, Ct, St, dst_groups, extra=()):
            CH = stt_['CH']
            gg, pk, khs = stt_['gg'], stt_['pk'], stt_['khs']
            outs = [P.act(pk[0][:], gg[0][:], AF.Sin, waits=[khs[0]]),
                    P.act(pk[1][:], gg[1][:], AF.Sin, bias=hpib[:], waits=[khs[1]] + CST)]
            done = []
            for which, dst in ((0, St), (1, Ct)):
                for g0 in dst_groups:
                    for q in range(4):
                        done.append(P.copy('pool', dst[g0:g0 + 32, q * CH:(q + 1) * CH], pk[which][q * 32:(q + 1) * 32, :], [outs[which]] + list(extra)))
            return done

        st_cd = es.enter_context(ExitStack())
        Cd = sbt(st_cd, "Cd", [128, S], BF16)
        Sd = sbt(st_cd, "Sd", [128, S], BF16)
        st_ma = es.enter_context(ExitStack())
        Cm = sbt(st_ma, "Cm", [128, S], F32)
        Sm = sbt(st_ma, "Sm", [128, S], F32)
        hT_tok = [None] * NT
        with ExitStack() as st0:
            xs = [sbt(st0, f"xs{i}", [128, 8, TT]) for i in range(2)]
            sq = [sbt(st0, f"sq{i}", [128, TT]) for i in range(2)]
            Rb = [sbt(st0, f"Rb{i}", [128, TT]) for i in range(2)]
            xsem = [[P.dsem(f"dxs{i}_{c}") for c in range(8)] for i in range(2)]
            xs_free = [[], []]
            sq_free = [[], []]
            Rb_free = [[], []]
            pr = Ring(banks[0:2])
            gen_tables_b(gen_tables_a(st0, 0, 'm', F32), Cm, Sm, [64])
            for tt in range(NT):
                T = slice(tt * TT, (tt + 1) * TT)
                b = tt % 2
                kxc = [P.dma('sp', xs[b][:, c, :], xT_v[:, c, T], xsem[b][c], xs_free[b]) for c in range(8)]
                kx = kxc[7]
                kb, ps, pw = pr.get()
                kmm = None
                for c in range(8):
                    ks = P.act(sq[c % 2][:], xs[b][:, c, :], AF.Square, waits=[kxc[c]] + sq_free[c % 2])
                    kmm = P.mm(ps[:], ones[:], sq[c % 2][:], start=(c == 0), stop=(c == 7),
                               waits=[ks, k_ones] + (pw if c == 0 else []), sig=True)
                    sq_free[c % 2] = [kmm]
                kr = P.act(Rb[b][:], ps[:], AF.Ln, scale=1.0 / 1024, bias=epsb[:], waits=[kmm, k_eps] + Rb_free[b])
                pr.rel(kb, [kr])
                kr2 = P.act(Rb[b][:], Rb[b][:], AF.Exp, scale=-0.5, waits=[kr])
                kl = None
                for c in range(8):
                    kl = P.stt(hT[:, c, T], xs[b][:, c, :], gpre[:, c:c + 1], Rb[b][:], ALU.mult, ALU.mult,
                               waits=[kr2, kxc[c]] + CST)
                xs_free[b] = [kl]
                Rb_free[b] = [kl]
                hT_tok[tt] = kl
            P.barrier()
        if stop_after == 0:
            P.dead = True

        qst_sem = P.dsem("dqst")
        with ExitStack() as st1:
            wA = sbt(st1, "wA", [128, 8, 672], BF16)
            wkr = sbt(st1, "wkr", [128, 8, 96], BF16)
            wuq = sbt(st1, "wuq", [128, 3, 768], BF16)
            wukv = sbt(st1, "wukv", [128, 2, 1024], BF16)
            dw1 = P.dsem("dw1")
            kz = P.memset('pool', wkr[:], 0.0)
            P.dma('pool', wA[:], w_in_v[:, :, 0:672], dw1)
            P.dma('pool', wkr[:, :, 64:96], w_in_v[:, :, O_KR:O_KR + 32], dw1, [kz])
            P.dma('pool', wuq[:], w_uq.rearrange("(c p) n -> p c n", p=128), dw1)
            k_w1 = P.dma('pool', wukv[:], w_ukv.rearrange("(c p) n -> p c n", p=128), dw1)
            W1 = [k_w1]
            cqn = [sbt(st1, f"cqn{i}", [128, 3, TT], BF16) for i in range(2)]
            ckvn = [sbt(st1, f"ckvn{i}", [128, 2, TT], BF16) for i in range(2)]
            sq = [sbt(st1, f"sq1_{i}", [128, TT]) for i in range(2)]
            Rq = [sbt(st1, f"Rq{i}", [128, TT]) for i in range(2)]
            krb = [sbt(st1, f"krb{i}", [96, TT], BF16) for i in range(2)]
            kro = [sbt(st1, f"kro{i}", [96, TT], BF16) for i in range(2)]
            Qst = sbt(st1, "Qst", [96, 8, TT], BF16)
            Kst = sbt(st1, "Kst", [64, 8, TT], BF16)
            Vst = sbt(st1, "Vst", [128, 8, 4, 128], BF16)
            t1r = Ring([sbt(st1, f"t1_{i}", [128, TT]) for i in range(3)])
            t2r = Ring([sbt(st1, f"t2_{i}", [128, TT]) for i in range(3)])
            k_krb0 = [P.memset('pool', krb[i][:], 0.0) for i in range(2)]
            k_vst1 = P.memset('pool', Vst[:], 1.0)
            krsem = [P.dsem("dkr0"), P.dsem("dkr1")]
            kst_sem = P.dsem("dkst")
            vst_sem = P.dsem("dvst")
            pr = Ring(banks)
            sq_free = [[], []]
            cqn_free = [[], []]
            ckvn_free = [[], []]
            Rq_free = [[], []]
            Rqi = [0]
            krb_free = [[], []]
            kro_free = [[], []]
            Qst_free = [[]]
            Kst_free = [[]]
            Vst_free = [[]]
            rot_mla = rotm[0:96, 0:96]
            LT = {}
            wv = wukv[:, :, :].rearrange("p c (h two d) -> p c h two d", two=2, d=64)

            prL = Ring(banks[0:6])
            prU = Ring(banks[6:8])

            def norm_front(groups, idx, dim):
                kbs, pss, pws = prL.get()
                kmm = None
                for n_, j in enumerate(idx):
                    ks = P.act(sq[n_ % 2][:], groups[j][1][:], AF.Square, waits=[groups[j][2]] + sq_free[n_ % 2])
                    kmm = P.mm(pss[:], ones[:], sq[n_ % 2][:], start=(n_ == 0), stop=(n_ == len(idx) - 1),
                               waits=[ks, k_ones] + (pws if n_ == 0 else []), sig=True)
                    sq_free[n_ % 2] = [kmm]
                ri = Rqi[0] % 2
                Rqi[0] += 1
                kr = P.act(Rq[ri][:], pss[:], AF.Ln, scale=1.0 / dim, bias=epsb[:], waits=[kmm] + Rq_free[ri])
                prL.rel(kbs, [kr])
                kr2 = P.act(Rq[ri][:], Rq[ri][:], AF.Exp, scale=-0.5, waits=[kr])
                return ri, kr2

            def norm_back(groups, idx, dst, dst_free, gcol, ri, kr2):
                kl = None
                for n_, j in enumerate(idx):
                    kl = P.stt(dst[:, n_, :], groups[j][1][:], gcol[:, n_:n_ + 1], Rq[ri][:], ALU.mult, ALU.mult,
                               waits=[kr2, groups[j][2]] + dst_free + CST)
                    prL.rel(groups[j][0], [kl])
                Rq_free[ri] = [kl]
                return kl

            def main_groups(tt, cols):
                T = slice(tt * TT, (tt + 1) * TT)
                hw = [hT_tok[tt]] + W1
                groups = []
                for j in cols:
                    kb, ps, pw = prL.get()
                    km = None
                    for c in range(8):
                        km = P.mm(ps[:], wA[:, c, j * 128:(j + 1) * 128], hT[:, c, T],
                                  start=(c == 0), stop=(c == 7), waits=(hw + pw) if c == 0 else (), sig=(c == 7))
                    groups.append((kb, ps, km))
                return groups

            def Lcq_front(tt):
                groups = main_groups(tt, range(3))
                ri, kr2 = norm_front(groups, [0, 1, 2], 384.0)
                LT[tt] = dict(cq=(groups, ri, kr2))

            def Lcq_back(tt):
                b = tt % 2
                groups, ri, kr2 = LT[tt]['cq']
                LT[tt]['k_cq'] = norm_back(groups, [0, 1, 2], cqn[b], cqn_free[b], gq, ri, kr2)

            def Lkv_front(tt):
                T = slice(tt * TT, (tt + 1) * TT)
                b = tt % 2
                hw = [hT_tok[tt]] + W1
                groups = main_groups(tt, range(3, 5))
                kbk, psk, pwk = prL.get()
                kmk = None
                for c in range(8):
                    kmk = P.mm(psk[0:96, :], wkr[:, c, :], hT[:, c, T], start=(c == 0), stop=(c == 7),
                               waits=(hw + pwk) if c == 0 else (), sig=(c == 7))
                ri, kr2 = norm_front(groups, [0, 1], 256.0)
                ka = P.act(krb[b][64:96, :], psk[64:96, :], AF.Copy, waits=[kmk, k_krb0[b]] + krb_free[b])
                prL.rel(kbk, [ka])
                kb2, pB, pw2 = prL.get()
                kmr = P.mm(pB[0:96, :], rot_mla, krb[b][0:96, :], waits=[ka] + CST + pw2, sig=True)
                LT[tt]['kv'] = (groups, ri, kr2, ka, kb2, pB, kmr)

            def Lkv_back(tt):
                T = slice(tt * TT, (tt + 1) * TT)
                b = tt % 2
                groups, ri, kr2, ka, kb2, pB, kmr = LT[tt]['kv']
                LT[tt]['k_ckv'] = norm_back(groups, [0, 1], ckvn[b], ckvn_free[b], gkv, ri, kr2)
                i1, t1, w1 = t1r.get()
                i2, t2, w2 = t2r.get()
                k1 = P.tt('pool', t1[64:96, :], krb[b][64:96, :], Cm[64:96, T], ALU.mult, [ka] + w1)
                k2 = P.tt('dve', t2[64:96, :], pB[64:96, :], Sm[64:96, T], ALU.mult, [kmr] + w2)
                prL.rel(kb2, [k2])
                krb_free[b] = [kmr, k1]
                k3 = P.tt('dve', kro[b][64:96, :], t1[64:96, :], t2[64:96, :], ALU.add, [k1, k2] + kro_free[b])
                t1r.rel(i1, [k3])
                t2r.rel(i2, [k3])
                kd = None
                for h in range(8):
                    kd = P.dma('sp', KM[h, 64:96, T], kro[b][64:96, :], krsem[b], [k3])
                kro_free[b] = [kd]

            def Uq(tt, hook=None):
                T = slice(tt * TT, (tt + 1) * TT)
                b = tt % 2
                k_cq = LT[tt]['k_cq']
                pend = []
                Q_done = []

                def qA(h):
                    kb, ps, pw = prU.get()
                    km = None
                    for j in range(3):
                        km = P.mm(ps[0:96, :], wuq[:, j, h * 96:(h + 1) * 96], cqn[b][:, j, :], start=(j == 0), stop=(j == 2),
                                  waits=([k_cq] + W1 + pw) if j == 0 else (), sig=(j == 2))
                    ka = P.act(Qst[0:96, h, :], ps[0:96, :], AF.Copy, waits=[km] + Qst_free[0])
                    prU.rel(kb, [ka])
                    pend.append((h, ka))

                def qB():
                    h, ka = pend.pop(0)
                    kb2, pB, pw2 = prU.get()
                    kmr = P.mm(pB[0:96, :], rot_mla, Qst[0:96, h, :], waits=[ka] + CST + pw2, sig=True)
                    i1, t1, w1 = t1r.get()
                    i2, t2, w2 = t2r.get()
                    k1 = P.tt('pool', t1[64:96, :], Qst[64:96, h, :], Cm[64:96, T], ALU.mult, [ka] + w1)
                    k2 = P.tt('dve', t2[64:96, :], pB[64:96, :], Sm[64:96, T], ALU.mult, [kmr] + w2)
                    prU.rel(kb2, [k2])
                    k3 = P.tt('dve', Qst[64:96, h, :], t1[64:96, :], t2[64:96, :], ALU.add, [k1, k2, kmr])
                    t1r.rel(i1, [k3])
                    t2r.rel(i2, [k3])
                    Q_done.append(k3)
                    return kmr

                last_rot = None
                for h in range(8):
                    qA(h)
                    if len(pend) > 1:
                        last_rot = qB()
                    if hook is not None:
                        hook(h)
                while pend:
                    last_rot = qB()
                cqn_free[b] = [last_rot]
                kqs = P.dma('sp', QM.rearrange("h r t -> r h t")[:, :, T], Qst[:], qst_sem, Q_done)
                Qst_free[0] = [kqs]

            def Ukv(tt):
                T = slice(tt * TT, (tt + 1) * TT)
                b = tt % 2
                k_ckv = LT[tt]['k_ckv']
                K_done = []
                for h in range(8):
                    kb, ps, pw = prU.get()
                    km = None
                    for j in range(2):
                        km = P.mm(ps[0:64, :], wukv[:, j, h * 128:h * 128 + 64], ckvn[b][:, j, :], start=(j == 0), stop=(j == 1),
                                  waits=([k_ckv] + W1 + pw) if j == 0 else (), sig=(j == 1))
                    ka = P.copy('act', Kst[0:64, h, :], ps[0:64, :], [km] + Kst_free[0])
                    prU.rel(kb, [ka])
                    K_done.append(ka)
                kks = P.dma('sp', KM.rearrange("h r t -> r h t")[0:64, :, T], Kst[:], kst_sem, K_done)
                Kst_free[0] = [kks]
                V_done = []
                last_v_mm = None
                for s_ in range(4):
                    kb, ps, pw = prU.get()
                    km = None
                    for j in range(2):
                        km = P.mm(ps[:].rearrange("p (h d) -> p h d", d=64), ckvn[b][:, j, s_ * 128:(s_ + 1) * 128], wv[:, j, :, 1, :],
                                  start=(j == 0), stop=(j == 1), waits=([k_ckv] + W1 + pw) if j == 0 else (), sig=(j == 1))
                    eng = 'act' if s_ % 2 == 0 else 'dve'
                    ka = P.copy(eng, Vst[:, :, s_, 0:64], ps[:].rearrange("p (h d) -> p h d", d=64), [km, k_vst1] + Vst_free[0])
                    prU.rel(kb, [ka])
                    V_done.append(ka)
                    last_v_mm = km
                ckvn_free[b] = [last_v_mm]
                kvs = P.dma('sp', VM.rearrange("h p (n e) -> p h n e", e=128)[:, :, tt * 4:(tt + 1) * 4, :], Vst[:], vst_sem, V_done)
                Vst_free[0] = [kvs]

            Lcq_front(0)
            Lcq_back(0)
            Lkv_front(0)
            Lkv_back(0)
            def make_cq_hook(ttn):
                groups = []

                def hook(h):
                    if h in (1, 3, 5):
                        groups.extend(main_groups(ttn, [len(groups)]))

                def finish():
                    ri, kr2 = norm_front(groups, [0, 1, 2], 384.0)
                    LT[ttn] = dict(cq=(groups, ri, kr2))
                return hook, finish

            for tt in range(NT):
                if tt + 1 < NT:
                    hook_, fin_ = make_cq_hook(tt + 1)
                    Uq(tt, hook_)
                    fin_()
                else:
                    Uq(tt)
                if tt + 1 < NT:
                    Lcq_back(tt + 1)
                    Lkv_front(tt + 1)
                Ukv(tt)
                if tt + 1 < NT:
                    Lkv_back(tt + 1)
            P.barrier()

        if stop_after == 1:
            P.dead = True
        st_ma.close()
        ust_sems = [P.dsem("dust0"), P.dsem("dust1")]
        with ExitStack() as st2:
            kc1 = P.memset('pool', Cd[:], 1.0)
            ks0 = P.memset('pool', Sd[:], 0.0)
            tbl_state = gen_tables_a(st2, 1, 'd', BF16)
            Qh = [sbt(st2, f"Qh{i}", [96, S], BF16) for i in range(2)]
            Kh = [sbt(st2, f"Kh{i}", [96, S], BF16) for i in range(2)]
            Vh = [sbt(st2, f"Vh{i}", [128, 32, 128], BF16) for i in range(2)]
            wz = sbt(st2, "wz", [128, 8, 512], BF16)
            dw2 = P.dsem("dw2")
            k_wz = P.dma('pool', wz[:], w_in_v[:, :, O_ZM:O_ZM + 512], dw2)
            dwc = P.dsem("dwcast")
            for o_ in range(16):
                P.dma('pool', WGb[o_], w_in_v[:, :, O_GM + o_ * 128:O_GM + (o_ + 1) * 128], dwc)
            P.dma('pool', WPMb, w_pm, dwc)
            P.dma('pool', WPDb, w_pd, dwc)
            k_wcast = P.dma('pool', WOb, w_out, dwc)
            ptr = Ring([sbt(st2, f"pt{i}", [128, TT], BF16) for i in range(4)])
            szp = sbt(st2, "szp", [128, NT, TT])
            tz = [sbt(st2, f"tz{i}", [128, TT]) for i in range(2)]
            a1 = [sbt(st2, f"a1{i}", [128, TT]) for i in range(2)]
            rd = [sbt(st2, f"rd{i}", [128, TT]) for i in range(2)]
            ust = [sbt(st2, f"ust{i}", [64, TT], BF16) for i in range(2)]
            lsem = [P.dsem("dql0"), P.dsem("dql1")]
            sr = Ring(banks[0:3])
            por = Ring(banks[3:5])
            pzr = Ring(banks[5:7])
            ld_tok = [None, None]
            buf_free = [[], []]
            ep_free = [[], []]
            ust_free = [[], []]
            tz_free = [[], []]
            szp_tok = [None] * NT
            szp_free = [[] for _ in range(NT)]
            epi = 0
            zi = 0

            def load_head(h):
                b = h % 2
                P.dma('sp', Qh[b][:], QM[h], lsem[b], buf_free[b])
                P.dma('sp', Kh[b][:], KM[h], lsem[b], buf_free[b])
                ld_tok[b] = P.dma('sp', Vh[b][:].rearrange("p n e -> p (n e)"), VM[h], lsem[b], buf_free[b])

            load_head(0)
            for h in range(8):
                b = h % 2
                if h + 1 < 8:
                    load_head(h + 1)
                LD = [ld_tok[b]]
                items = []
                for qt in range(NT):
                    nk = 4 * (qt + 1)
                    for kt in range(nk):
                        items.append((qt, kt, nk))
                st = {}
                qstate = {}

                def emit_S(it):
                    nonlocal zi
                    qt, kt, nk = it
                    if kt == 0:
                        if h % 2 == 0:
                            kbz, pz, pwz = pzr.get()
                            kmz = None
                            for c in range(8):
                                kmz = P.mm(pz[:], wz[:, c, (h // 2) * 128:(h // 2 + 1) * 128], hT[:, c, qt * TT:(qt + 1) * TT],
                                           start=(c == 0), stop=(c == 7), waits=([k_wz] + pwz) if c == 0 else (), sig=(c == 7))
                            zz = zi % 2
                            zi += 1
                            kth = P.act(tz[zz][:], pz[:], AF.Tanh, scale=0.5, waits=[kmz] + tz_free[zz])
                            ksz = P.stt(szp[:, qt, :], tz[zz][:], 1.0, pz[:], ALU.add, ALU.mult, [kth] + szp_free[qt])
                            pzr.rel(kbz, [ksz, kth])
                            tz_free[zz] = [ksz]
                            szp_tok[qt] = ksz
                        kbo, po, pwo = por.get()
                        qstate[qt] = dict(kbo=kbo, po=po, pwo=pwo)
                    diag = kt >= 4 * qt
                    j = kt - 4 * qt
                    n0 = 128 * j if diag else 0
                    kb, ps, pw = sr.get()
                    q0 = qt * TT
                    kslc = Kh[b][0:96, kt * 128:(kt + 1) * 128]
                    if not diag:
                        km = P.mm(ps[:, 0:TT], kslc, Qh[b][0:96, q0:q0 + TT], waits=LD + pw, sig=True)
                    else:
                        P.mm(ps[:, n0:n0 + 128], kslc, Qh[b][0:96, q0 + n0:q0 + n0 + 128], start=True, stop=False, waits=LD + pw)
                        km = P.mm(ps[:, n0:n0 + 128], ident[:], maskab[:, 0:128], start=False, stop=True, waits=CST, sig=(j == 3))
                        if j < 3:
                            km = P.mm(ps[:, n0 + 128:TT], kslc, Qh[b][0:96, q0 + n0 + 128:q0 + TT], sig=True)
                    st[it] = dict(kb=kb, ps=ps, km=km, n0=n0)

                def emit_E(it):
                    s_ = st[it]
                    n0 = s_['n0']
                    ki, pt, pw = ptr.get()
                    ke = P.act(pt[:, n0:TT], s_['ps'][:, n0:TT], AF.Exp, scale=SC_MLA, waits=[s_['km']] + pw)
                    sr.rel(s_['kb'], [ke])
                    s_['ki'] = ki; s_['pt'] = pt; s_['ke'] = ke

                def emit_PV(it):
                    nonlocal epi
                    qt, kt, nk = it
                    s_ = st[it]
                    n0 = s_['n0']
                    q_ = qstate[qt]
                    po = q_['po']
                    last = (kt == nk - 1)
                    kpv = P.mm(po[:, n0:TT], Vh[b][:, kt, :], s_['pt'][:, n0:TT], start=(kt == 0), stop=last,
                               waits=[s_['ke']] + (q_['pwo'] if kt == 0 else []), sig=True)
                    ptr.rel(s_['ki'], [kpv])
                    del st[it]
                    if not last:
                        return
                    e = epi % 2
                    epi += 1
                    T = slice(qt * TT, (qt + 1) * TT)
                    hb = (h % 2) * 64
                    kr = P.recip(rd[e][0:64, :], po[64:128, :], [kpv] + ep_free[e])
                    ka1 = P.tt('dve', a1[e][hb:hb + 64, :], po[0:64, :], rd[e][0:64, :], ALU.mult, [kpv, kr] + ep_free[e])
                    por.rel(q_['kbo'], [ka1, kr])
                    ku = P.stt(ust[e][:], a1[e][hb:hb + 64, :], 0.5, szp[hb:hb + 64, qt, :], ALU.mult, ALU.mult, [ka1, szp_tok[qt]] + ust_free[e])
                    ep_free[e] = [ku]
                    if h % 2 == 1:
                        szp_free[qt] = [ku]
                    kus = P.dma('sp', UM[h * 64:(h + 1) * 64, T], ust[e][:], ust_sems[e], [ku])
                    ust_free[e] = [kus]
                    del qstate[qt]

                n_it = len(items)
                for i in range(n_it + 2):
                    if i < n_it:
                        emit_S(items[i])
                    if 0 <= i - 1 < n_it:
                        emit_E(items[i - 1])
                    if 0 <= i - 2 < n_it:
                        emit_PV(items[i - 2])
                buf_free[b] = [('pe', P.cnt['pe'])]
                if h == 0:
                    gen_tables_b(tbl_state, Cd, Sd, [0, 64], [kc1, ks0])
            P.barrier()
        if stop_after == 2:
            P.dead = True

        with ExitStack() as st3:
            wd = [sbt(st3, f"wd{i}", [128, 8, 384], BF16) for i in range(2)]
            wzd = [sbt(st3, f"wzd{i}", [128, 8, 128], BF16) for i in range(2)]
            wdsem = [P.dsem("dwd0"), P.dsem("dwd1")]
            wzsem = [P.dsem("dwz0"), P.dsem("dwz1")]
            Qd = sbt(st3, "Qd", [128, S], BF16)
            Kd = sbt(st3, "Kd", [128, S], BF16)
            Vd = sbt(st3, "Vd", [128, 32, 2, 128], BF16)
            acc = [sbt(st3, f"acc{i}", [128, S]) for i in range(2)]
            abr = Ring([sbt(st3, f"abf{i}", [128, TT], BF16) for i in range(3)])
            t1r = Ring([sbt(st3, f"t1d{i}", [128, TT]) for i in range(2)])
            t2r = Ring([sbt(st3, f"t2d{i}", [128, TT]) for i in range(2)])
            ptr = Ring([sbt(st3, f"ptd{i}", [128, 512], BF16) for i in range(4)])
            rd = [sbt(st3, f"rdd{i}", [128, TT]) for i in range(2)]
            tz = [sbt(st3, f"tzd{i}", [128, TT]) for i in range(2)]
            szd = [sbt(st3, f"szd{i}", [128, TT]) for i in range(2)]
            ustp = [sbt(st3, f"ustd{i}", [128, TT], BF16) for i in range(2)]
            k_vd1 = P.memset('pool', Vd[:], 1.0)
            pr = Ring(banks[0:4])
            pur = [Ring(banks[4:6]), Ring(banks[6:8])]
            mask512 = sbt(st3, "mask512", [128, 512], BF16)
            dmk = P.dsem("dmk")
            P.dma('pool', mask512[:, 0:256], m01_d, dmk)
            k_m512 = P.dma('pool', mask512[:, 256:512], m01_d, dmk)
            wd_free = [[], []]
            wzd_free = [[], []]
            QK_free = []
            Vd_free = []
            acc_free = [[], []]
            ust_free = [[], []]
            tz_free = [[], []]
            szd_free = [[], []]
            rd_free = [[], []]
            epi = 0
            combos = [(hp, g) for hp in range(4) for g in range(3)]
            wtok = {}

            def load_w(ci):
                hp, g = combos[ci]
                sl = ci % 2
                for s3 in range(3):
                    c0 = O_DIL + (s3 * 3 + g) * 512 + hp * 128
                    wtok[ci] = P.dma('pool', wd[sl][:, :, s3 * 128:(s3 + 1) * 128], w_in_v[:, :, c0:c0 + 128], wdsem[sl], wd_free[sl])

            def load_wz(hp):
                sl = hp % 2
                return P.dma('pool', wzd[sl][:], w_in_v[:, :, O_ZD + hp * 128:O_ZD + (hp + 1) * 128], wzsem[sl], wzd_free[sl])

            fin_pending = [None]
            epi_c = [0]

            def make_fin(hp, accw):
                WZ = [wz_tok[hp]]
                zsl = hp % 2

                def step(tt):
                    kmz = None
                    T = slice(tt * TT, (tt + 1) * TT)
                    kbz, pz, pwz = pr.get()
                    for c in range(8):
                        kmz = P.mm(pz[:], wzd[zsl][:, c, :], hT[:, c, T], start=(c == 0), stop=(c == 7),
                                   waits=(WZ + pwz) if c == 0 else (), sig=(c == 7))
                    e = epi_c[0] % 2
                    epi_c[0] += 1
                    kbe, pe_, pwe = pr.get()
                    ke = P.act(pe_[:], pz[:], AF.Exp, scale=-1.0, waits=[kmz] + pwe)
                    kw = None
                    for hh in range(2):
                        kw = P.stt(szd[e][hh * 64:(hh + 1) * 64, :], pe_[hh * 64:(hh + 1) * 64, :], 1.0, acc[hh][64:128, T],
                                   ALU.add, ALU.mult, [ke] + accw + szd_free[e])
                    pr.rel(kbe, [kw])
                    kl = P.act(rd[e][:], szd[e][:], AF.Ln, waits=[kw] + rd_free[e])
                    szd_free[e] = [kl]
                    kx = P.act(rd[e][:], rd[e][:], AF.Exp, scale=-1.0, waits=[kl])
                    i1, t1, w1 = t1r.get()
                    ka1 = None
                    for hh in range(2):
                        ka1 = P.tt('dve', t1[hh * 64:(hh + 1) * 64, :], acc[hh][0:64, T], pz[hh * 64:(hh + 1) * 64, :], ALU.mult,
                                   [kmz] + accw + w1)
                    pr.rel(kbz, [ke, ka1])
                    ku = P.tt('pool', ustp[e][:], t1[:], rd[e][:], ALU.mult, [ka1, kx] + ust_free[e])
                    t1r.rel(i1, [ku])
                    rd_free[e] = [ku]
                    kus = P.dma('sp', UM[512 + hp * 128:512 + (hp + 1) * 128, T], ustp[e][:], ust_sems[e], [ku])
                    ust_free[e] = [kus]
                    acc_free[0] = [ka1, kw]
                    acc_free[1] = [ka1, kw]
                    if tt == NT - 1:
                        wzd_free[zsl] = [kmz]
                return step

            load_w(0)
            wz_tok = {0: load_wz(0)}
            for ci, (hp, g) in enumerate(combos):
                sl = ci % 2
                if ci + 1 < len(combos):
                    load_w(ci + 1)
                WD = [wtok[ci]]
                d = (1, 4, 16)[g]
                L = S // d

                def tokslice(f0, count):
                    r = f0 // L
                    i0 = f0 % L
                    start = r + d * i0
                    return slice(start, start + d * (count - 1) + 1, d)

                pend = []
                lastB = [None]

                def stageA(which, tt):
                    T = slice(tt * TT, (tt + 1) * TT)
                    kb, ps, pw = pr.get()
                    km = None
                    for c in range(8):
                        km = P.mm(ps[:], wd[sl][:, c, which * 128:(which + 1) * 128], hT[:, c, T], start=(c == 0), stop=(c == 7),
                                  waits=(WD + pw) if c == 0 else (), sig=(c == 7))
                    ia, ab, wa = abr.get()
                    ka = P.act(ab[:], ps[:], AF.Copy, waits=[km] + wa)
                    pr.rel(kb, [ka])
                    pend.append((which, tt, ia, ab, ka))

                def stageB():
                    which, tt, ia, ab, ka = pend.pop(0)
                    dst = Qd if which == 0 else Kd
                    T = slice(tt * TT, (tt + 1) * TT)
                    kb2, pB, pw2 = pr.get()
                    kmr = P.mm(pB[:], rotm[:, 96:224], ab[:], waits=[ka] + CST + pw2, sig=True)
                    i1, t1, w1 = t1r.get()
                    i2, t2, w2 = t2r.get()
                    n_i = TT // d
                    nat = "p (i r) -> p i r"
                    t1d = t1[:].rearrange("p (r i) -> p i r", r=d)
                    t2d = t2[:].rearrange("p (r i) -> p i r", r=d)
                    k1 = P.tt('dve', t1d, ab[:].rearrange(nat, r=d), Cd[:, T].rearrange(nat, r=d), ALU.mult, [ka] + w1)
                    k2 = P.tt('dve', t2d, pB[:].rearrange(nat, r=d), Sd[:, T].rearrange(nat, r=d), ALU.mult, [kmr] + w2)
                    pr.rel(kb2, [k2])
                    abr.rel(ia, [kmr, k1])
                    dv = dst[:, :].rearrange("p (r i) -> p r i", r=d)[:, :, tt * n_i:(tt + 1) * n_i]
                    k3 = P.tt('pool', dv, t1[:].rearrange("p (r i) -> p r i", r=d),
                              t2[:].rearrange("p (r i) -> p r i", r=d), ALU.add, [k1, k2] + QK_free)
                    t1r.rel(i1, [k3])
                    t2r.rel(i2, [k3])
                    lastB[0] = k3

                for which in (0, 1):
                    for tt in range(NT):
                        stageA(which, tt)
                        if len(pend) > 1:
                            stageB()
                if stop_after == 2.2:
                    while pend:
                        stageB()
                    P.barrier()
                    P.dead = True
                V_ready = []
                km = None
                for nb4 in range(8):
                    kb, ps, pw = pr.get()
                    for q4 in range(4):
                        n = nb4 * 4 + q4
                        tsl = tokslice(n * 128, 128)
                        for c in range(8):
                            km = P.mm(ps[:, q4 * 128:(q4 + 1) * 128], hT[:, c, tsl], wd[sl][:, c, 256:384], start=(c == 0), stop=(c == 7),
                                      waits=(WD + pw) if (c == 0 and q4 == 0) else (), sig=(c == 7 and q4 == 3))
                    eng = 'act' if nb4 % 2 == 0 else 'dve'
                    ka = P.copy(eng, Vd[:, nb4 * 4:(nb4 + 1) * 4, :, 0:64], ps[:].rearrange("p (n h d) -> p n h d", n=4, h=2),
                                [km, k_vd1] + Vd_free)
                    pr.rel(kb, [ka])
                    V_ready.append(ka)
                    if nb4 == 0:
                        while pend:
                            stageB()
                    if fin_pending[0] is not None:
                        fin_pending[0](nb4)
                        if nb4 == 7:
                            fin_pending[0] = None
                wd_free[sl] = [km]
                if g == 0 and hp + 1 < 4:
                    wz_tok[hp + 1] = load_wz(hp + 1)
                QK_ready = [('pool', P.cnt['pool']), ('dve', P.cnt['dve'])]
                if stop_after == 2.3:
                    P.barrier()
                    P.dead = True
                nbp = L // 128
                pstate = [{}, {}]

                def get_pu(hh, n):
                    bk = n // 4
                    if bk not in pstate[hh]:
                        kbu, pu, pwu = pur[hh].get()
                        pstate[hh][bk] = dict(kb=kbu, pu=pu, pw=pwu, first=True)
                    return pstate[hh][bk]

                sts = {}

                def dS(n):
                    bb = n % nbp
                    N = 256 if bb + 1 < nbp else 128
                    tl = []
                    for hh in range(2):
                        p0 = hh * 64
                        kb, ps, pw = pr.get()
                        km_ = P.mm(ps[:, 0:N], Kd[p0:p0 + 64, n * 128:(n + 1) * 128], Qd[p0:p0 + 64, n * 128:n * 128 + N],
                                   waits=QK_ready + pw, sig=True)
                        tl.append((kb, ps, km_))
                    sts[n] = dict(tl=tl, N=N)

                def dE(n):
                    s_ = sts[n]
                    N = s_['N']
                    ki, pt, pw = ptr.get()
                    kes = []
                    for hh in range(2):
                        kb, ps, km_ = s_['tl'][hh]
                        ke = P.act(pt[:, hh * 256:hh * 256 + N], ps[:, 0:N], AF.Exp, scale=SC_DIL, waits=[km_] + pw)
                        pr.rel(kb, [ke])
                        kes.append(ke)
                    if N == 256:
                        kmk = P.tt('dve', pt[:, 0:512], pt[:, 0:512], mask512[:, 0:512], ALU.mult, kes + [k_m512])
                    else:
                        ptv = pt[:, :].rearrange("p (h c) -> p h c", h=2)[:, :, 0:128]
                        kmk = P.tt('dve', ptv, ptv, mask512[:, :].rearrange("p (h c) -> p h c", h=2)[:, :, 0:128], ALU.mult, kes + [k_m512])
                    s_['ki'] = ki; s_['pt'] = pt; s_['ke'] = kmk

                def dPV(n):
                    s_ = sts[n]
                    N = s_['N']
                    bb = n % nbp
                    kpv = None
                    for hh in range(2):
                        u = get_pu(hh, n)
                        c0 = (n % 4) * 128
                        w = [s_['ke']] + V_ready + (u['pw'] if u['first'] else [])
                        u['first'] = False
                        kpv = P.mm(u['pu'][:, c0:c0 + 128], Vd[:, n, hh, :], s_['pt'][:, hh * 256:hh * 256 + 128], start=(bb == 0), stop=True,
                                   waits=w, sig=True)
                        if N == 256:
                            u2 = get_pu(hh, n + 1)
                            c1 = ((n + 1) % 4) * 128
                            w2 = (u2['pw'] if u2['first'] else [])
                            u2['first'] = False
                            kpv = P.mm(u2['pu'][:, c1:c1 + 128], Vd[:, n, hh, :], s_['pt'][:, hh * 256 + 128:hh * 256 + 256], start=True, stop=False,
                                       waits=w2, sig=True)
                    ptr.rel(s_['ki'], [kpv])
                    del sts[n]
                    if n % 4 == 3:
                        bk = n // 4
                        f0 = bk * 512
                        r = f0 // L
                        i0 = f0 % L
                        for hh in range(2):
                            u = pstate[hh][bk]
                            accv = acc[hh][:, :].rearrange("p (i r) -> p r i", r=d)
                            if L >= 512:
                                dstv = accv[:, r, i0:i0 + 512]
                                srcv = u['pu'][:, :]
                            else:
                                nr = 512 // L
                                dstv = accv[:, r:r + nr, :]
                                srcv = u['pu'][:, :].rearrange("p (r i) -> p r i", r=nr)
                            if g == 0:
                                eng = 'act' if hh == 0 else 'dve'
                                ka_ = P.copy(eng, dstv, srcv, [kpv] + acc_free[hh])
                            else:
                                ka_ = P.tt('dve', dstv, dstv, srcv, ALU.add, [kpv] + acc_free[hh])
                            pur[hh].rel(u['kb'], [ka_])

                for i in range(32 + 2):
                    if i < 32:
                        dS(i)
                    if 0 <= i - 1 < 32:
                        dE(i - 1)
                    if 0 <= i - 2 < 32:
                        dPV(i - 2)
                if g == 0:
                    acc_free = [[], []]
                QK_free = [('pe', P.cnt['pe'])]
                Vd_free = [('pe', P.cnt['pe'])]
                if stop_after == 2.4 or (stop_after == 2.5 and ci == 1) or (stop_after == 2.6 and ci == 2):
                    P.barrier()
                    P.dead = True
                if g == 2:
                    fin_pending[0] = make_fin(hp, [('dve', P.cnt['dve']), ('act', P.cnt['act'])])
            if fin_pending[0] is not None:
                for tt_ in range(NT):
                    fin_pending[0](tt_)
                fin_pending[0] = None
            P.barrier()

        st_cd.close()
        if stop_after == 3:
            P.dead = True
        with ExitStack() as st4:
            wg = sbt(st4, "wg", [128, 16, 8, 128], BF16)
            wpm = sbt(st4, "wpm", [128, 4, 1024], BF16)
            wpd = sbt(st4, "wpd", [128, 4, 1024], BF16)
            wout = sbt(st4, "wout", [128, 8, 1024], BF16)
            gpost = sbt(st4, "gpost", [128, 1024])
            dw3 = [P.dsem(f"dw3_{i}") for i in range(4)]
            k_wg_oc = []
            for oc_ in range(8):
                sm_ = P.dsem(f"dwg{oc_}")
                P.dma('sp', wg[:, oc_, :, :], WGb[oc_], sm_, [k_wcast])
                k_wg_oc.append(P.dma('sp', wg[:, 8 + oc_, :, :], WGb[8 + oc_], sm_, [k_wcast]))
                if oc_ == 0:
                    k_wpm = P.dma('sp', wpm[:], WPMb.rearrange("(c p) n -> p c n", p=128), dw3[1], [k_wcast])
                    k_wpd = P.dma('sp', wpd[:], WPDb.rearrange("(c p) n -> p c n", p=128), dw3[1], [k_wcast])
                    k_wpm = k_wpd
            k_wout = P.dma('sp', wout[:], WOb.rearrange("(c p) n -> p c n", p=128), dw3[2], [k_wcast])
            k_gp = P.dma('sp', gpost[:], gpost_d.broadcast_to([128, 1024]), dw3[3])
            um = [sbt(st4, f"um{i}", [128, 4, TT], BF16) for i in range(2)]
            ud = [sbt(st4, f"ud{i}", [128, 4, TT], BF16) for i in range(2)]
            usem = [P.dsem("dul0"), P.dsem("dul1")]
            sg = [sbt(st4, f"sg{i}", [128, TT]) for i in range(2)]
            m1 = sbt(st4, "m1", [128, TT])
            m2 = sbt(st4, "m2", [128, TT])
            mg = [sbt(st4, f"mg{i}", [128, 8, TT], BF16) for i in range(2)]
            xt = [sbt(st4, f"xt{i}", [128, 1024]) for i in range(3)]
            o1 = [sbt(st4, f"o1_{i}", [128, 1024]) for i in range(2)]
            junk = sbt(st4, "junk", [128, 512])
            ssq = [sbt(st4, f"ssq{i}", [128, 4]) for i in range(2)]
            xsem = [P.dsem(f"dxt{i}") for i in range(3)]
            ysem = [P.dsem(f"dy{i}") for i in range(3)]
            pr = Ring(banks[0:4])
            por = Ring(banks[4:8])
            u_free = [[], []]
            u_tok = [None, None]
            sg_free = [[], []]
            mm_free = [[], []]
            mg_free = [[], []]
            mg_done = [None, None]
            xt_free = [[], [], []]
            o1_free = [[], []]
            ssq_free = [[], []]
            junk_free = [[]]
            out_toks = []
            xi = [0]
            last_umm = [None, None]
            last_omm = [None, None]

            def load_u(tt):
                ub = tt % 2
                T = slice(tt * TT, (tt + 1) * TT)
                P.dma('sp', um[ub][:], UM[0:512, T].rearrange("(c p) t -> p c t", p=128), usem[ub], u_free[ub])
                u_tok[ub] = P.dma('sp', ud[ub][:], UM[512:1024, T].rearrange("(c p) t -> p c t", p=128), usem[ub], u_free[ub])

            def G_oc(tt, oc):
                ub = tt % 2
                T = slice(tt * TT, (tt + 1) * TT)
                grp = []
                for (wt, off, src, nk, kw) in ((wg, 0, None, 8, k_wg_oc[oc]), (wg, 8, None, 8, k_wg_oc[oc]), (wpm, 0, um[ub], 4, k_wpm), (wpd, 0, ud[ub], 4, k_wpd)):
                    kb, ps, pw = pr.get()
                    km = None
                    for c in range(nk):
                        rhs = hT[:, c, T] if src is None else src[:, c, :]
                        lw = wt[:, off + oc, c, :] if src is None else wt[:, c, oc * 128:(oc + 1) * 128]
                        km = P.mm(ps[:], lw, rhs, start=(c == 0), stop=(c == nk - 1),
                                  waits=([kw, u_tok[ub]] + pw) if c == 0 else (), sig=(c == nk - 1))
                    grp.append((kb, ps, km))
                last_umm[ub] = grp[3][2]
                ks1 = P.act(sg[0][:], grp[0][1][:], AF.Sigmoid, waits=[grp[0][2]] + sg_free[0])
                pr.rel(grp[0][0], [ks1])
                ks2 = P.act(sg[1][:], grp[1][1][:], AF.Sigmoid, waits=[grp[1][2]] + sg_free[1])
                pr.rel(grp[1][0], [ks2])
                k1 = P.tt('dve', m1[:], sg[0][:], grp[2][1][:], ALU.mult, [ks1, grp[2][2]] + mm_free[0])
                pr.rel(grp[2][0], [k1])
                k2 = P.tt('dve', m2[:], sg[1][:], grp[3][1][:], ALU.mult, [ks2, grp[3][2]] + mm_free[1])
                pr.rel(grp[3][0], [k2])
                sg_free[0] = [k1]
                sg_free[1] = [k2]
                k3 = P.tt('pool', mg[ub][:, oc, :], m1[:], m2[:], ALU.add, [k1, k2] + (mg_free[ub] if oc == 0 else []))
                mm_free[0] = [k3]
                mm_free[1] = [k3]
                mg_done[ub] = k3
                if oc == 7:
                    u_free[ub] = [last_umm[ub]]

            def O_s(tt, s):
                ub = tt % 2
                r0 = tt * TT + s * 128
                xb = xi[0] % 3
                ob = xi[0] % 2
                xi[0] += 1
                kx = P.dma('sp', xt[xb][:], x[r0:r0 + 128, :], xsem[xb], xt_free[xb])
                halves = []
                for hf in range(2):
                    kb, po, pw = por.get()
                    km = None
                    for oc in range(8):
                        km = P.mm(po[:], mg[ub][:, oc, s * 128:(s + 1) * 128], wout[:, oc, hf * 512:(hf + 1) * 512], start=(oc == 0), stop=(oc == 7),
                                  waits=([k_wout, mg_done[ub]] + pw) if oc == 0 else (), sig=(oc == 7))
                    kq = P.act(junk[:], po[:], AF.Square, accum=ssq[ob][:, hf:hf + 1], waits=[km] + junk_free[0] + ssq_free[ob])
                    junk_free[0] = [kq]
                    halves.append((kb, po, km, kq))
                    last_omm[ub] = km
                ka = P.tt('dve', ssq[ob][:, 2:3], ssq[ob][:, 0:1], ssq[ob][:, 1:2], ALU.add, [halves[0][3], halves[1][3]])
                kr = P.act(ssq[ob][:, 3:4], ssq[ob][:, 2:3], AF.Sqrt, scale=1.0 / 1024, bias=epsb[:], waits=[ka])
                kr2 = P.recip(ssq[ob][:, 3:4], ssq[ob][:, 3:4], [kr])
                ko = None
                for hf in range(2):
                    H = slice(hf * 512, (hf + 1) * 512)
                    ko = P.stt(o1[ob][:, H], halves[hf][1][:], ssq[ob][:, 3:4], gpost[:, H], ALU.mult, ALU.mult,
                               waits=[kr2, k_gp] + o1_free[ob])
                    por.rel(halves[hf][0], [ko])
                ssq_free[ob] = [ko]
                kf = P.tt('pool', xt[xb][:], o1[ob][:], xt[xb][:], ALU.add, [ko, kx])
                o1_free[ob] = [kf]
                kys = P.dma('sp', y[r0:r0 + 128, :], xt[xb][:], ysem[xb], [kf])
                xt_free[xb] = [kys]
                out_toks.append(kys)
                if s == 3:
                    mg_free[ub] = [last_omm[ub]]

            load_u(0)
            for oc in range(8):
                G_oc(0, oc)
            for tt in range(NT):
                if tt + 1 < NT:
                    load_u(tt + 1)
                for s in range(4):
                    O_s(tt, s)
                    if tt + 1 < NT:
                        G_oc(tt + 1, 2 * s)
                        G_oc(tt + 1, 2 * s + 1)
            P.wait('sp', out_toks)
            P.barrier()
        P.emit()
    return nc


_NC_CACHE = {}


def _consts():
    p = np.arange(128)[:, None]
    f = np.arange(128)[None, :]
    ident = np.eye(128, dtype=np.float32)
    maskab = np.zeros((128, 256), np.float32)
    maskab[:, 0:128] = np.where(p > f, NEG, 0.0)
    maskab[:, 128:256] = np.where(f > p, NEG, 0.0)
    rot = np.zeros((128, 224), np.float32)
    for i in range(16):
        rot[80 + i, 64 + i] = -1.0
        rot[64 + i, 80 + i] = 1.0
    for hb in (0, 64):
        for i in range(8):
            rot[hb + 8 + i, 96 + hb + i] = -1.0
            rot[hb + i, 96 + hb + 8 + i] = 1.0
    theta = np.float32(500000.0)
    inv_mla = (theta ** (-np.arange(0, 32, 2, dtype=np.float32) / np.float32(32))).astype(np.float32)
    inv_dil = (theta ** (-np.arange(0, 16, 2, dtype=np.float32) / np.float32(16))).astype(np.float32)
    freqs = np.zeros((128, 2), np.float32)
    for q in range(4):
        for i in range(32):
            freqs[q * 32 + i, 0] = inv_mla[i % 16]
            freqs[q * 32 + i, 1] = inv_dil[i % 8] if i < 16 else 0.0
    mask01 = (maskab == 0.0).astype(np.float32)
    return dict(ident=ident, maskab=maskab, rotm=rot, freqs=freqs, mask01=mask01)


def make_in_maps(x, positions, pre_norm_g, w_in, q_norm_g, w_uq, kv_norm_g, w_ukv,
                 w_proj_mla, w_proj_dil, w_out, post_norm_g):
    cst = _consts()
    f32 = np.float32
    shared = dict(
        w_in=np.ascontiguousarray(w_in[0], f32), w_uq=np.ascontiguousarray(w_uq[0], f32),
        w_ukv=np.ascontiguousarray(w_ukv[0], f32), w_pm=np.ascontiguousarray(w_proj_mla[0], f32),
        w_pd=np.ascontiguousarray(w_proj_dil[0], f32), w_out=np.ascontiguousarray(w_out[0], f32),
        gpre=np.ascontiguousarray(np.asarray(pre_norm_g[0], f32).reshape(8, 128).T),
        gq=np.ascontiguousarray(np.asarray(q_norm_g[0], f32).reshape(3, 128).T),
        gkv=np.ascontiguousarray(np.asarray(kv_norm_g[0], f32).reshape(2, 128).T),
        gpost=np.ascontiguousarray(np.asarray(post_norm_g[0], f32).reshape(1, 1024)),
        **cst)
    maps = []
    for b in range(8):
        xb = np.asarray(x[b], f32)
        m = dict(shared)
        m["x"] = np.ascontiguousarray(xb)
        m["xT"] = np.ascontiguousarray(xb.T)
        m["pos"] = np.ascontiguousarray(np.asarray(positions[b], np.int32).reshape(1, S))
        maps.append(m)
    return maps


def kernel(x, positions, pre_norm_g, w_in, q_norm_g, w_uq, kv_norm_g, w_ukv,
           w_proj_mla, w_proj_dil, w_out, post_norm_g):
    if "nc" not in _NC_CACHE:
        _NC_CACHE["nc"] = build(False)
    nc = _NC_CACHE["nc"]
    maps = make_in_maps(x, positions, pre_norm_g, w_in, q_norm_g, w_uq, kv_norm_g, w_ukv,
                        w_proj_mla, w_proj_dil, w_out, post_norm_g)
    res = run_bass_kernel_spmd(nc, maps, core_ids=list(range(8)))
    out = np.stack([np.asarray(r["y"], np.float32) for r in res.results], axis=0)
    return out
```

```python
import math
import numpy as np
import concourse.bass as bass
import concourse.mybir as mybir
from concourse.bass_utils import run_bass_kernel_spmd
from contextlib import ExitStack

F32 = mybir.dt.float32
BF16 = mybir.dt.bfloat16
I32 = mybir.dt.int32
AF = mybir.ActivationFunctionType
ALU = mybir.AluOpType

S = 4096
TT = 512
NT = S // TT
EPS = 1e-6
NEG = -30000.0
C1 = 6.28125
C2 = 2.0 * math.pi - 6.28125
O_CQ, O_CKV, O_KR, O_DIL, O_ZM, O_ZD, O_GM, O_GD = 0, 384, 640, 672, 5280, 5792, 6304, 7328
SC_MLA = 96.0 ** -0.5
SC_DIL = 64.0 ** -0.5


class Prog:
    ENG = ('pe', 'act', 'dve', 'pool', 'sp')

    def __init__(self, nc, es):
        self.nc = nc
        self.es = es
        self.streams = {e: [] for e in self.ENG}
        self.sem = {}
        self.cnt = {}
        self.dsems = []
        self.dead = False
        for e in self.ENG[:4]:
            self._newsem(e)

    def _newsem(self, name):
        self.sem[name] = self.es.enter_context(self.nc.semaphore(name))
        self.cnt[name] = 0
        return name

    def dsem(self, name):
        self.dsems.append(name)
        return self._newsem(name)

    def op(self, eng, fn, waits=(), sig=True):
        if self.dead:
            return (eng, self.cnt[eng])
        tok = None
        if sig:
            self.cnt[eng] += 1
            tok = (eng, self.cnt[eng])
        self.streams[eng].append((fn, [w for w in waits if w is not None], (eng, 1) if sig else None))
        return tok

    def dma(self, q, out, in_, sem, waits=()):
        if self.dead:
            return (sem, self.cnt[sem])
        self.cnt[sem] += 16
        tok = (sem, self.cnt[sem])
        self.streams[q].append((lambda e: e.dma_start(out=out, in_=in_), [w for w in waits if w is not None], (sem, 16)))
        return tok

    def wait(self, eng, waits):
        if self.dead:
            return
        self.streams[eng].append((None, [w for w in waits if w is not None], None))

    def barrier(self):
        toks = [(s, self.cnt[s]) for s in list(self.ENG[:4]) + self.dsems if self.cnt[s] > 0]
        for e in self.ENG:
            self.wait(e, toks)

    def mm(self, out, lhsT, rhs, start=True, stop=True, waits=(), sig=False):
        return self.op('pe', lambda e: e.matmul(out, lhsT=lhsT, rhs=rhs, start=start, stop=stop), waits, sig)

    def act(self, out, in_, func, scale=None, bias=None, accum=None, waits=(), sig=True):
        kw = {}
        if scale is not None:
            kw['scale'] = scale
        if bias is not None:
            kw['bias'] = bias
        if accum is not None:
            kw['accum_out'] = accum
        return self.op('act', lambda e: e.activation(out=out, in_=in_, func=func, **kw), waits, sig)

    def tt(self, eng, out, in0, in1, op, waits=(), sig=True):
        return self.op(eng, lambda e: e.tensor_tensor(out=out, in0=in0, in1=in1, op=op), waits, sig)

    def ts(self, eng, out, in0, s1, op0, s2=None, op1=None, waits=(), sig=True):
        if op1 is None:
            return self.op(eng, lambda e: e.tensor_scalar(out=out, in0=in0, scalar1=s1, scalar2=None, op0=op0), waits, sig)
        return self.op(eng, lambda e: e.tensor_scalar(out=out, in0=in0, scalar1=s1, scalar2=s2, op0=op0, op1=op1), waits, sig)

    def stt(self, out, in0, scalar, in1, op0, op1, waits=(), sig=True):
        return self.op('dve', lambda e: e.scalar_tensor_tensor(out=out, in0=in0, scalar=scalar, in1=in1, op0=op0, op1=op1), waits, sig)

    def copy(self, eng, out, in_, waits=(), sig=True):
        if eng == 'act':
            return self.act(out, in_, AF.Copy, waits=waits, sig=sig)
        return self.op(eng, lambda e: e.tensor_copy(out=out, in_=in_), waits, sig)

    def recip(self, out, in_, waits=(), sig=True):
        return self.op('dve', lambda e: e.reciprocal(out=out, in_=in_), waits, sig)

    def memset(self, eng, ap, val, waits=(), sig=True):
        return self.op(eng, lambda e: e.memset(ap, val), waits, sig)

    def check(self):
        pos = {e: 0 for e in self.ENG}
        val = {s: 0 for s in self.sem}
        progress = True
        while progress:
            progress = False
            for e in self.ENG:
                st = self.streams[e]
                while pos[e] < len(st):
                    fn, waits, inc = st[pos[e]]
                    if all(val[s] >= v for (s, v) in waits):
                        if inc is not None:
                            val[inc[0]] += inc[1]
                        pos[e] += 1
                        progress = True
                    else:
                        break
        bad = {e: (pos[e], len(self.streams[e])) for e in self.ENG if pos[e] < len(self.streams[e])}
        if bad:
            msg = []
            for e, (p, n) in bad.items():
                fn, waits, inc = self.streams[e][p]
                msg.append("%s stuck at %d/%d waits=%s have=%s" % (e, p, n, [w for w in waits if val[w[0]] < w[1]], [(w[0], val[w[0]]) for w in waits if val[w[0]] < w[1]]))
            raise RuntimeError("DEADLOCK: " + "; ".join(msg))

    def emit(self):
        self.check()
        nc = self.nc
        engobj = {'pe': 'tensor', 'act': 'scalar', 'dve': 'vector', 'pool': 'gpsimd', 'sp': 'sync'}
        with nc.Block() as block:
            for en in self.ENG:
                stream = self.streams[en]

                def body(engine, stream=stream):
                    waited = {}
                    for fn, waits, inc in stream:
                        for (s, v) in waits:
                            if waited.get(s, 0) >= v:
                                continue
                            engine.wait_ge(self.sem[s], v)
                            waited[s] = v
                        if fn is not None:
                            ins = fn(engine)
                            if inc is not None:
                                ins.then_inc(self.sem[inc[0]], inc[1])
                getattr(block, engobj[en])(body)


class Ring:
    def __init__(self, items):
        self.items = list(items)
        self.free = [[] for _ in self.items]
        self.busy = [False for _ in self.items]
        self.i = 0

    def get(self):
        k = self.i % len(self.items)
        self.i += 1
        assert not self.busy[k], "ring slot re-acquired before its release was emitted"
        self.busy[k] = True
        w = self.free[k]
        self.free[k] = []
        return k, self.items[k], w

    def rel(self, k, toks):
        self.free[k] = [t for t in toks if t is not None]
        self.busy[k] = False


class _Stop(Exception):
    pass


def build(debug=False, stop_after=99):
    nc = bass.Bass("TRN2", target_bir_lowering=False)

    def din(name, shape, dt=F32):
        return nc.dram_tensor(name, shape, dt, kind="ExternalInput").ap()

    xT = din("xT", [1024, S]); x = din("x", [S, 1024]); pos = din("pos", [1, S], I32)
    w_in = din("w_in", [1024, 8352]); w_uq = din("w_uq", [384, 768]); w_ukv = din("w_ukv", [256, 1024])
    w_pm = din("w_pm", [512, 1024]); w_pd = din("w_pd", [512, 1024]); w_out = din("w_out", [1024, 1024])
    ident_d = din("ident", [128, 128]); mask_d = din("maskab", [128, 256]); rot_d = din("rotm", [128, 224])
    freq_d = din("freqs", [128, 2]); gpre_d = din("gpre", [128, 8]); gq_d = din("gq", [128, 3])
    gkv_d = din("gkv", [128, 2]); gpost_d = din("gpost", [1, 1024])
    m01_d = din("mask01", [128, 256])
    y = nc.dram_tensor("y", [S, 1024], F32, kind="ExternalOutput").ap()
    sk = "ExternalOutput" if debug else "Internal"
    QM = nc.dram_tensor("QM", [8, 96, S], BF16, kind=sk).ap()
    KM = nc.dram_tensor("KM", [8, 96, S], BF16, kind=sk).ap()
    VM = nc.dram_tensor("VM", [8, 128, 32 * 128], BF16, kind=sk).ap()
    UM = nc.dram_tensor("UM", [1024, S], BF16, kind=sk).ap()
    WGb = nc.dram_tensor("WGb", [16, 128, 8, 128], BF16, kind="Internal").ap()
    WPMb = nc.dram_tensor("WPMb", [512, 1024], BF16, kind="Internal").ap()
    WPDb = nc.dram_tensor("WPDb", [512, 1024], BF16, kind="Internal").ap()
    WOb = nc.dram_tensor("WOb", [1024, 1024], BF16, kind="Internal").ap()

    w_in_v = w_in.rearrange("(c p) n -> p c n", p=128)
    xT_v = xT.rearrange("(c p) t -> p c t", p=128)

    with ExitStack() as es:
        P = Prog(nc, es)

        uid = [0]

        def sbt(stack, name, shape, dt=F32):
            uid[0] += 1
            return stack.enter_context(nc.sbuf_tensor('s%d_%s' % (uid[0], name), shape, dt))

        banks = [es.enter_context(nc.psum_tensor(f"ps{i}", [128, 512], F32)) for i in range(8)]

        hT = sbt(es, "hT", [128, 8, S], BF16)
        ident = sbt(es, "ident", [128, 128], BF16)
        maskab = sbt(es, "maskab", [128, 256], BF16)
        rotm = sbt(es, "rotm", [128, 224], BF16)
        ones = sbt(es, "ones", [128, 128], F32)
        freqs = sbt(es, "freqs", [128, 2], F32)
        gpre = sbt(es, "gpre", [128, 8], F32)
        gq = sbt(es, "gq", [128, 3], F32)
        gkv = sbt(es, "gkv", [128, 2], F32)
        epsb = sbt(es, "epsb", [128, 1], F32)
        hpib = sbt(es, "hpib", [128, 1], F32)

        dc = P.dsem("dcst")
        dcp = P.dsem("dcstp")
        P.dma('pool', ident[:], ident_d, dcp)
        P.dma('pool', maskab[:], mask_d, dcp)
        k_cstp = P.dma('pool', rotm[:], rot_d, dcp)
        P.dma('sp', freqs[:], freq_d, dc)
        P.dma('sp', gpre[:], gpre_d, dc)
        P.dma('sp', gq[:], gq_d, dc)
        k_cst = P.dma('sp', gkv[:], gkv_d, dc)
        k_ones = P.memset('dve', ones[:], 1.0)
        k_eps = P.memset('dve', epsb[:], EPS)
        k_hpi = P.memset('dve', hpib[:], math.pi / 2)
        CST = [k_cst, k_cstp, k_ones, k_eps, k_hpi]

        def gen_tables_a(stack, fcol, tag, pk_dt):
            CH = 1024
            posi = sbt(stack, "tb_posi", [128, CH], I32)
            ang = sbt(stack, "tb_ang", [128, CH])
            xx = sbt(stack, "tb_x", [128, CH])
            ki = sbt(stack, "tb_ki", [128, CH], I32)
            kf = sbt(stack, "tb_kf", [128, CH])
            gg = [sbt(stack, "tb_g0", [128, CH]), sbt(stack, "tb_g1", [128, CH])]
            pk = [sbt(stack, "tb_pk0", [128, CH], pk_dt), sbt(stack, "tb_pk1", [128, CH], pk_dt)]
            dsm = P.dsem("dtb" + tag)
            kd = None
            for q in range(4):
                kd = P.dma('sp', posi[q * 32:(q + 1) * 32, :], pos[:, q * CH:(q + 1) * CH].broadcast_to([32, CH]), dsm)
            k0 = P.copy('dve', xx[:], posi[:], [kd])
            k1 = P.ts('dve', ang[:], xx[:], freqs[:, fcol:fcol + 1], ALU.mult, waits=[k0] + CST)
            last = [k1]
            khs = []
            for which in (0, 1):
                ka = P.ts('dve', xx[:], ang[:], 1.0 / (2 * math.pi), ALU.mult, 0.25 * which, ALU.add, waits=last)
                kb = P.copy('dve', ki[:], xx[:], [ka])
                kc = P.copy('dve', kf[:], ki[:], [kb])
                kdd = P.tt('dve', xx[:], xx[:], kf[:], ALU.subtract, [kc])
                ke = P.ts('dve', gg[which][:], xx[:], 0.5, ALU.is_gt, waits=[kdd])
                kff = P.tt('dve', kf[:], kf[:], gg[which][:], ALU.add, [ke])
                kg = P.stt(xx[:], kf[:], -C1, ang[:], ALU.mult, ALU.add, [kff])
                kh0 = P.stt(gg[which][:], kf[:], -C2, xx[:], ALU.mult, ALU.add, [kg])
                lim = 3.14159
                lo, hi = (-lim, lim) if which == 0 else (-lim - math.pi / 2, lim - math.pi / 2)
                kh = P.ts('dve', gg[which][:], gg[which][:], lo, ALU.max, hi, ALU.min, waits=[kh0])
                last = [kh]
                khs.append(kh)
            return dict(gg=gg, pk=pk, khs=khs, CH=CH)

        def gen_tables_b(stt_, Ct, St, dst_groups, extra=()):
            CH = stt_['CH']
            gg, pk, khs = stt_['gg'], stt_['pk'], stt_['khs']
            outs = [P.act(pk[0][:], gg[0][:], AF.Sin, waits=[khs[0]]),
                    P.act(pk[1][:], gg[1][:], AF.Sin, bias=hpib[:], waits=[khs[1]] + CST)]
            done = []
            for which, dst in ((0, St), (1, Ct)):
                for g0 in dst_groups:
                    for q in range(4):
                        done.append(P.copy('pool', dst[g0:g0 + 32, q * CH:(q + 1) * CH], pk[which][q * 32:(q + 1) * 32, :], [outs[which]] + list(extra)))
            return done

        st_cd = es.enter_context(ExitStack())
        Cd = sbt(st_cd, "Cd", [128, S], BF16)
        Sd = sbt(st_cd, "Sd", [128, S], BF16)
        st_ma = es.enter_context(ExitStack())
        Cm = sbt(st_ma, "Cm", [128, S], F32)
        Sm = sbt(st_ma, "Sm", [128, S], F32)
        hT_tok = [None] * NT
        with ExitStack() as st0:
            xs = [sbt(st0, f"xs{i}", [128, 8, TT]) for i in range(2)]
            sq = [sbt(st0, f"sq{i}", [128, TT]) for i in range(2)]
            Rb = [sbt(st0, f"Rb{i}", [128, TT]) for i in range(2)]
            xsem = [[P.dsem(f"dxs{i}_{c}") for c in range(8)] for i in range(2)]
            xs_free = [[], []]
            sq_free = [[], []]
            Rb_free = [[], []]
            pr = Ring(banks[0:2])
            gen_tables_b(gen_tables_a(st0, 0, 'm', F32), Cm, Sm, [64])
            for tt in range(NT):
                T = slice(tt * TT, (tt + 1) * TT)
                b = tt % 2
                kxc = [P.dma('sp', xs[b][:, c, :], xT_v[:, c, T], xsem[b][c], xs_free[b]) for c in range(8)]
                kx = kxc[7]
                kb, ps, pw = pr.get()
                kmm = None
                for c in range(8):
                    ks = P.act(sq[c % 2][:], xs[b][:, c, :], AF.Square, waits=[kxc[c]] + sq_free[c % 2])
                    kmm = P.mm(ps[:], ones[:], sq[c % 2][:], start=(c == 0), stop=(c == 7),
                               waits=[ks, k_ones] + (pw if c == 0 else []), sig=True)
                    sq_free[c % 2] = [kmm]
                kr = P.act(Rb[b][:], ps[:], AF.Ln, scale=1.0 / 1024, bias=epsb[:], waits=[kmm, k_eps] + Rb_free[b])
                pr.rel(kb, [kr])
                kr2 = P.act(Rb[b][:], Rb[b][:], AF.Exp, scale=-0.5, waits=[kr])
                kl = None
                for c in range(8):
                    kl = P.stt(hT[:, c, T], xs[b][:, c, :], gpre[:, c:c + 1], Rb[b][:], ALU.mult, ALU.mult,
                               waits=[kr2, kxc[c]] + CST)
                xs_free[b] = [kl]
                Rb_free[b] = [kl]
                hT_tok[tt] = kl
            P.barrier()
        if stop_after == 0:
            P.dead = True

        qst_sem = P.dsem("dqst")
        with ExitStack() as st1:
            wA = sbt(st1, "wA", [128, 8, 672], BF16)
            wkr = sbt(st1, "wkr", [128, 8, 96], BF16)
            wuq = sbt(st1, "wuq", [128, 3, 768], BF16)
            wukv = sbt(st1, "wukv", [128, 2, 1024], BF16)
            dw1 = P.dsem("dw1")
            kz = P.memset('pool', wkr[:], 0.0)
            P.dma('pool', wA[:], w_in_v[:, :, 0:672], dw1)
            P.dma('pool', wkr[:, :, 64:96], w_in_v[:, :, O_KR:O_KR + 32], dw1, [kz])
            P.dma('pool', wuq[:], w_uq.rearrange("(c p) n -> p c n", p=128), dw1)
            k_w1 = P.dma('pool', wukv[:], w_ukv.rearrange("(c p) n -> p c n", p=128), dw1)
            W1 = [k_w1]
            cqn = [sbt(st1, f"cqn{i}", [128, 3, TT], BF16) for i in range(2)]
            ckvn = [sbt(st1, f"ckvn{i}", [128, 2, TT], BF16) for i in range(2)]
            sq = [sbt(st1, f"sq1_{i}", [128, TT]) for i in range(2)]
            Rq = [sbt(st1, f"Rq{i}", [128, TT]) for i in range(2)]
            krb = [sbt(st1, f"krb{i}", [96, TT], BF16) for i in range(2)]
            kro = [sbt(st1, f"kro{i}", [96, TT], BF16) for i in range(2)]
            Qst = sbt(st1, "Qst", [96, 8, TT], BF16)
            Kst = sbt(st1, "Kst", [64, 8, TT], BF16)
            Vst = sbt(st1, "Vst", [128, 8, 4, 128], BF16)
            t1r = Ring([sbt(st1, f"t1_{i}", [128, TT]) for i in range(3)])
            t2r = Ring([sbt(st1, f"t2_{i}", [128, TT]) for i in range(3)])
            k_krb0 = [P.memset('pool', krb[i][:], 0.0) for i in range(2)]
            k_vst1 = P.memset('pool', Vst[:], 1.0)
            krsem = [P.dsem("dkr0"), P.dsem("dkr1")]
            kst_sem = P.dsem("dkst")
            vst_sem = P.dsem("dvst")
            pr = Ring(banks)
            sq_free = [[], []]
            cqn_free = [[], []]
            ckvn_free = [[], []]
            Rq_free = [[], []]
            Rqi = [0]
            krb_free = [[], []]
            kro_free = [[], []]
            Qst_free = [[]]
            Kst_free = [[]]
            Vst_free = [[]]
            rot_mla = rotm[0:96, 0:96]
            LT = {}
            wv = wukv[:, :, :].rearrange("p c (h two d) -> p c h two d", two=2, d=64)

            prL = Ring(banks[0:6])
            prU = Ring(banks[6:8])

            def norm_front(groups, idx, dim):
                kbs, pss, pws = prL.get()
                kmm = None
                for n_, j in enumerate(idx):
                    ks = P.act(sq[n_ % 2][:], groups[j][1][:], AF.Square, waits=[groups[j][2]] + sq_free[n_ % 2])
                    kmm = P.mm(pss[:], ones[:], sq[n_ % 2][:], start=(n_ == 0), stop=(n_ == len(idx) - 1),
                               waits=[ks, k_ones] + (pws if n_ == 0 else []), sig=True)
                    sq_free[n_ % 2] = [kmm]
                ri = Rqi[0] % 2
                Rqi[0] += 1
                kr = P.act(Rq[ri][:], pss[:], AF.Ln, scale=1.0 / dim, bias=epsb[:], waits=[kmm] + Rq_free[ri])
                prL.rel(kbs, [kr])
                kr2 = P.act(Rq[ri][:], Rq[ri][:], AF.Exp, scale=-0.5, waits=[kr])
                return ri, kr2

            def norm_back(groups, idx, dst, dst_free, gcol, ri, kr2):
                kl = None
                for n_, j in enumerate(idx):
                    kl = P.stt(dst[:, n_, :], groups[j][1][:], gcol[:, n_:n_ + 1], Rq[ri][:], ALU.mult, ALU.mult,
                               waits=[kr2, groups[j][2]] + dst_free + CST)
                    prL.rel(groups[j][0], [kl])
                Rq_free[ri] = [kl]
                return kl

            def main_groups(tt, cols):
                T = slice(tt * TT, (tt + 1) * TT)
                hw = [hT_tok[tt]] + W1
                groups = []
                for j in cols:
                    kb, ps, pw = prL.get()
                    km = None
                    for c in range(8):
                        km = P.mm(ps[:], wA[:, c, j * 128:(j + 1) * 128], hT[:, c, T],
                                  start=(c == 0), stop=(c == 7), waits=(hw + pw) if c == 0 else (), sig=(c == 7))
                    groups.append((kb, ps, km))
                return groups

            def Lcq_front(tt):
                groups = main_groups(tt, range(3))
                ri, kr2 = norm_front(groups, [0, 1, 2], 384.0)
                LT[tt] = dict(cq=(groups, ri, kr2))

            def Lcq_back(tt):
                b = tt % 2
                groups, ri, kr2 = LT[tt]['cq']
                LT[tt]['k_cq'] = norm_back(groups, [0, 1, 2], cqn[b], cqn_free[b], gq, ri, kr2)

            def Lkv_front(tt):
                T = slice(tt * TT, (tt + 1) * TT)
                b = tt % 2
                hw = [hT_tok[tt]] + W1
                groups = main_groups(tt, range(3, 5))
                kbk, psk, pwk = prL.get()
                kmk = None
                for c in range(8):
                    kmk = P.mm(psk[0:96, :], wkr[:, c, :], hT[:, c, T], start=(c == 0), stop=(c == 7),
                               waits=(hw + pwk) if c == 0 else (), sig=(c == 7))
                ri, kr2 = norm_front(groups, [0, 1], 256.0)
                ka = P.act(krb[b][64:96, :], psk[64:96, :], AF.Copy, waits=[kmk, k_krb0[b]] + krb_free[b])
                prL.rel(kbk, [ka])
                kb2, pB, pw2 = prL.get()
                kmr = P.mm(pB[0:96, :], rot_mla, krb[b][0:96, :], waits=[ka] + CST + pw2, sig=True)
                LT[tt]['kv'] = (groups, ri, kr2, ka, kb2, pB, kmr)

            def Lkv_back(tt):
                T = slice(tt * TT, (tt + 1) * TT)
                b = tt % 2
                groups, ri, kr2, ka, kb2, pB, kmr = LT[tt]['kv']
                LT[tt]['k_ckv'] = norm_back(groups, [0, 1], ckvn[b], ckvn_free[b], gkv, ri, kr2)
                i1, t1, w1 = t1r.get()
                i2, t2, w2 = t2r.get()
                k1 = P.tt('pool', t1[64:96, :], krb[b][64:96, :], Cm[64:96, T], ALU.mult, [ka] + w1)
                k2 = P.tt('dve', t2[64:96, :], pB[64:96, :], Sm[64:96, T], ALU.mult, [kmr] + w2)
                prL.rel(kb2, [k2])
                krb_free[b] = [kmr, k1]
                k3 = P.tt('dve', kro[b][64:96, :], t1[64:96, :], t2[64:96, :], ALU.add, [k1, k2] + kro_free[b])
                t1r.rel(i1, [k3])
                t2r.rel(i2, [k3])
                kd = None
                for h in range(8):
                    kd = P.dma('sp', KM[h, 64:96, T], kro[b][64:96, :], krsem[b], [k3])
                kro_free[b] = [kd]

            def Uq(tt):
                T = slice(tt * TT, (tt + 1) * TT)
                b = tt % 2
                k_cq = LT[tt]['k_cq']
                pend = []
                Q_done = []

                def qA(h):
                    kb, ps, pw = prU.get()
                    km = None
                    for j in range(3):
                        km = P.mm(ps[0:96, :], wuq[:, j, h * 96:(h + 1) * 96], cqn[b][:, j, :], start=(j == 0), stop=(j == 2),
                                  waits=([k_cq] + W1 + pw) if j == 0 else (), sig=(j == 2))
                    ka = P.act(Qst[0:96, h, :], ps[0:96, :], AF.Copy, waits=[km] + Qst_free[0])
                    prU.rel(kb, [ka])
                    pend.append((h, ka))

                def qB():
                    h, ka = pend.pop(0)
                    kb2, pB, pw2 = prU.get()
                    kmr = P.mm(pB[0:96, :], rot_mla, Qst[0:96, h, :], waits=[ka] + CST + pw2, sig=True)
                    i1, t1, w1 = t1r.get()
                    i2, t2, w2 = t2r.get()
                    k1 = P.tt('pool', t1[64:96, :], Qst[64:96, h, :], Cm[64:96, T], ALU.mult, [ka] + w1)
                    k2 = P.tt('dve', t2[64:96, :], pB[64:96, :], Sm[64:96, T], ALU.mult, [kmr] + w2)
                    prU.rel(kb2, [k2])
                    k3 = P.tt('dve', Qst[64:96, h, :], t1[64:96, :], t2[64:96, :], ALU.add, [k1, k2, kmr])
                    t1r.rel(i1, [k3])
                    t2r.rel(i2, [k3])
                    Q_done.append(k3)
                    return kmr

                last_rot = None
                for h in range(8):
                    qA(h)
                    if len(pend) > 1:
                        last_rot = qB()
                while pend:
                    last_rot = qB()
                cqn_free[b] = [last_rot]
                kqs = P.dma('sp', QM.rearrange("h r t -> r h t")[:, :, T], Qst[:], qst_sem, Q_done)
                Qst_free[0] = [kqs]

            def Ukv(tt):
                T = slice(tt * TT, (tt + 1) * TT)
                b = tt % 2
                k_ckv = LT[tt]['k_ckv']
                K_done = []
                for h in range(8):
                    kb, ps, pw = prU.get()
                    km = None
                    for j in range(2):
                        km = P.mm(ps[0:64, :], wukv[:, j, h * 128:h * 128 + 64], ckvn[b][:, j, :], start=(j == 0), stop=(j == 1),
                                  waits=([k_ckv] + W1 + pw) if j == 0 else (), sig=(j == 1))
                    ka = P.copy('act', Kst[0:64, h, :], ps[0:64, :], [km] + Kst_free[0])
                    prU.rel(kb, [ka])
                    K_done.append(ka)
                kks = P.dma('sp', KM.rearrange("h r t -> r h t")[0:64, :, T], Kst[:], kst_sem, K_done)
                Kst_free[0] = [kks]
                V_done = []
                last_v_mm = None
                for s_ in range(4):
                    kb, ps, pw = prU.get()
                    km = None
                    for j in range(2):
                        km = P.mm(ps[:].rearrange("p (h d) -> p h d", d=64), ckvn[b][:, j, s_ * 128:(s_ + 1) * 128], wv[:, j, :, 1, :],
                                  start=(j == 0), stop=(j == 1), waits=([k_ckv] + W1 + pw) if j == 0 else (), sig=(j == 1))
                    eng = 'act' if s_ % 2 == 0 else 'dve'
                    ka = P.copy(eng, Vst[:, :, s_, 0:64], ps[:].rearrange("p (h d) -> p h d", d=64), [km, k_vst1] + Vst_free[0])
                    prU.rel(kb, [ka])
                    V_done.append(ka)
                    last_v_mm = km
                ckvn_free[b] = [last_v_mm]
                kvs = P.dma('sp', VM.rearrange("h p (n e) -> p h n e", e=128)[:, :, tt * 4:(tt + 1) * 4, :], Vst[:], vst_sem, V_done)
                Vst_free[0] = [kvs]

            Lcq_front(0)
            Lcq_back(0)
            Lkv_front(0)
            Lkv_back(0)
            for tt in range(NT):
                if tt + 1 < NT:
                    Lcq_front(tt + 1)
                Uq(tt)
                if tt + 1 < NT:
                    Lcq_back(tt + 1)
                    Lkv_front(tt + 1)
                Ukv(tt)
                if tt + 1 < NT:
                    Lkv_back(tt + 1)
            P.barrier()

        if stop_after == 1:
            P.dead = True
        st_ma.close()
        ust_sems = [P.dsem("dust0"), P.dsem("dust1")]
        with ExitStack() as st2:
            kc1 = P.memset('pool', Cd[:], 1.0)
            ks0 = P.memset('pool', Sd[:], 0.0)
            tbl_state = gen_tables_a(st2, 1, 'd', BF16)
            Qh = [sbt(st2, f"Qh{i}", [96, S], BF16) for i in range(2)]
            Kh = [sbt(st2, f"Kh{i}", [96, S], BF16) for i in range(2)]
            Vh = [sbt(st2, f"Vh{i}", [128, 32, 128], BF16) for i in range(2)]
            wz = sbt(st2, "wz", [128, 8, 512], BF16)
            dw2 = P.dsem("dw2")
            k_wz = P.dma('pool', wz[:], w_in_v[:, :, O_ZM:O_ZM + 512], dw2)
            dwc = P.dsem("dwcast")
            for o_ in range(16):
                P.dma('pool', WGb[o_], w_in_v[:, :, O_GM + o_ * 128:O_GM + (o_ + 1) * 128], dwc)
            P.dma('pool', WPMb, w_pm, dwc)
            P.dma('pool', WPDb, w_pd, dwc)
            k_wcast = P.dma('pool', WOb, w_out, dwc)
            ptr = Ring([sbt(st2, f"pt{i}", [128, TT], BF16) for i in range(4)])
            szp = sbt(st2, "szp", [128, NT, TT])
            tz = [sbt(st2, f"tz{i}", [128, TT]) for i in range(2)]
            a1 = [sbt(st2, f"a1{i}", [128, TT]) for i in range(2)]
            rd = [sbt(st2, f"rd{i}", [128, TT]) for i in range(2)]
            ust = [sbt(st2, f"ust{i}", [64, TT], BF16) for i in range(2)]
            lsem = [P.dsem("dql0"), P.dsem("dql1")]
            sr = Ring(banks[0:3])
            por = Ring(banks[3:5])
            pzr = Ring(banks[5:7])
            ld_tok = [None, None]
            buf_free = [[], []]
            ep_free = [[], []]
            ust_free = [[], []]
            tz_free = [[], []]
            szp_tok = [None] * NT
            szp_free = [[] for _ in range(NT)]
            epi = 0
            zi = 0

            def load_head(h):
                b = h % 2
                P.dma('sp', Qh[b][:], QM[h], lsem[b], buf_free[b])
                P.dma('sp', Kh[b][:], KM[h], lsem[b], buf_free[b])
                ld_tok[b] = P.dma('sp', Vh[b][:].rearrange("p n e -> p (n e)"), VM[h], lsem[b], buf_free[b])

            load_head(0)
            for h in range(8):
                b = h % 2
                if h + 1 < 8:
                    load_head(h + 1)
                LD = [ld_tok[b]]
                items = []
                for qt in range(NT):
                    nk = 4 * (qt + 1)
                    for kt in range(nk):
                        items.append((qt, kt, nk))
                st = {}
                qstate = {}

                def emit_S(it):
                    nonlocal zi
                    qt, kt, nk = it
                    if kt == 0:
                        if h % 2 == 0:
                            kbz, pz, pwz = pzr.get()
                            kmz = None
                            for c in range(8):
                                kmz = P.mm(pz[:], wz[:, c, (h // 2) * 128:(h // 2 + 1) * 128], hT[:, c, qt * TT:(qt + 1) * TT],
                                           start=(c == 0), stop=(c == 7), waits=([k_wz] + pwz) if c == 0 else (), sig=(c == 7))
                            zz = zi % 2
                            zi += 1
                            kth = P.act(tz[zz][:], pz[:], AF.Tanh, scale=0.5, waits=[kmz] + tz_free[zz])
                            ksz = P.stt(szp[:, qt, :], tz[zz][:], 1.0, pz[:], ALU.add, ALU.mult, [kth] + szp_free[qt])
                            pzr.rel(kbz, [ksz, kth])
                            tz_free[zz] = [ksz]
                            szp_tok[qt] = ksz
                        kbo, po, pwo = por.get()
                        qstate[qt] = dict(kbo=kbo, po=po, pwo=pwo)
                    diag = kt >= 4 * qt
                    j = kt - 4 * qt
                    n0 = 128 * j if diag else 0
                    kb, ps, pw = sr.get()
                    q0 = qt * TT
                    kslc = Kh[b][0:96, kt * 128:(kt + 1) * 128]
                    if not diag:
                        km = P.mm(ps[:, 0:TT], kslc, Qh[b][0:96, q0:q0 + TT], waits=LD + pw, sig=True)
                    else:
                        P.mm(ps[:, n0:n0 + 128], kslc, Qh[b][0:96, q0 + n0:q0 + n0 + 128], start=True, stop=False, waits=LD + pw)
                        km = P.mm(ps[:, n0:n0 + 128], ident[:], maskab[:, 0:128], start=False, stop=True, waits=CST, sig=(j == 3))
                        if j < 3:
                            km = P.mm(ps[:, n0 + 128:TT], kslc, Qh[b][0:96, q0 + n0 + 128:q0 + TT], sig=True)
                    st[it] = dict(kb=kb, ps=ps, km=km, n0=n0)

                def emit_E(it):
                    s_ = st[it]
                    n0 = s_['n0']
                    ki, pt, pw = ptr.get()
                    ke = P.act(pt[:, n0:TT], s_['ps'][:, n0:TT], AF.Exp, scale=SC_MLA, waits=[s_['km']] + pw)
                    sr.rel(s_['kb'], [ke])
                    s_['ki'] = ki; s_['pt'] = pt; s_['ke'] = ke

                def emit_PV(it):
                    nonlocal epi
                    qt, kt, nk = it
                    s_ = st[it]
                    n0 = s_['n0']
                    q_ = qstate[qt]
                    po = q_['po']
                    last = (kt == nk - 1)
                    kpv = P.mm(po[:, n0:TT], Vh[b][:, kt, :], s_['pt'][:, n0:TT], start=(kt == 0), stop=last,
                               waits=[s_['ke']] + (q_['pwo'] if kt == 0 else []), sig=True)
                    ptr.rel(s_['ki'], [kpv])
                    del st[it]
                    if not last:
                        return
                    e = epi % 2
                    epi += 1
                    T = slice(qt * TT, (qt + 1) * TT)
                    hb = (h % 2) * 64
                    kr = P.recip(rd[e][0:64, :], po[64:128, :], [kpv] + ep_free[e])
                    ka1 = P.tt('dve', a1[e][hb:hb + 64, :], po[0:64, :], rd[e][0:64, :], ALU.mult, [kpv, kr] + ep_free[e])
                    por.rel(q_['kbo'], [ka1, kr])
                    ku = P.stt(ust[e][:], a1[e][hb:hb + 64, :], 0.5, szp[hb:hb + 64, qt, :], ALU.mult, ALU.mult, [ka1, szp_tok[qt]] + ust_free[e])
                    ep_free[e] = [ku]
                    if h % 2 == 1:
                        szp_free[qt] = [ku]
                    kus = P.dma('sp', UM[h * 64:(h + 1) * 64, T], ust[e][:], ust_sems[e], [ku])
                    ust_free[e] = [kus]
                    del qstate[qt]

                n_it = len(items)
                for i in range(n_it + 2):
                    if i < n_it:
                        emit_S(items[i])
                    if 0 <= i - 1 < n_it:
                        emit_E(items[i - 1])
                    if 0 <= i - 2 < n_it:
                        emit_PV(items[i - 2])
                buf_free[b] = [('pe', P.cnt['pe'])]
                if h == 0:
                    gen_tables_b(tbl_state, Cd, Sd, [0, 64], [kc1, ks0])
            P.barrier()
        if stop_after == 2:
            P.dead = True

        with ExitStack() as st3:
            wd = [sbt(st3, f"wd{i}", [128, 8, 384], BF16) for i in range(2)]
            wzd = [sbt(st3, f"wzd{i}", [128, 8, 128], BF16) for i in range(2)]
            wdsem = [P.dsem("dwd0"), P.dsem("dwd1")]
            wzsem = [P.dsem("dwz0"), P.dsem("dwz1")]
            Qd = sbt(st3, "Qd", [128, S], BF16)
            Kd = sbt(st3, "Kd", [128, S], BF16)
            Vd = sbt(st3, "Vd", [128, 32, 2, 128], BF16)
            acc = [sbt(st3, f"acc{i}", [128, S]) for i in range(2)]
            abr = Ring([sbt(st3, f"abf{i}", [128, TT], BF16) for i in range(3)])
            t1r = Ring([sbt(st3, f"t1d{i}", [128, TT]) for i in range(2)])
            t2r = Ring([sbt(st3, f"t2d{i}", [128, TT]) for i in range(2)])
            ptr = Ring([sbt(st3, f"ptd{i}", [128, 512], BF16) for i in range(4)])
            rd = [sbt(st3, f"rdd{i}", [128, TT]) for i in range(2)]
            tz = [sbt(st3, f"tzd{i}", [128, TT]) for i in range(2)]
            szd = [sbt(st3, f"szd{i}", [128, TT]) for i in range(2)]
            ustp = [sbt(st3, f"ustd{i}", [128, TT], BF16) for i in range(2)]
            k_vd1 = P.memset('pool', Vd[:], 1.0)
            pr = Ring(banks[0:4])
            pur = [Ring(banks[4:6]), Ring(banks[6:8])]
            mask512 = sbt(st3, "mask512", [128, 512], BF16)
            dmk = P.dsem("dmk")
            P.dma('pool', mask512[:, 0:256], m01_d, dmk)
            k_m512 = P.dma('pool', mask512[:, 256:512], m01_d, dmk)
            wd_free = [[], []]
            wzd_free = [[], []]
            QK_free = []
            Vd_free = []
            acc_free = [[], []]
            ust_free = [[], []]
            tz_free = [[], []]
            szd_free = [[], []]
            rd_free = [[], []]
            epi = 0
            combos = [(hp, g) for hp in range(4) for g in range(3)]
            wtok = {}

            def load_w(ci):
                hp, g = combos[ci]
                sl = ci % 2
                for s3 in range(3):
                    c0 = O_DIL + (s3 * 3 + g) * 512 + hp * 128
                    wtok[ci] = P.dma('pool', wd[sl][:, :, s3 * 128:(s3 + 1) * 128], w_in_v[:, :, c0:c0 + 128], wdsem[sl], wd_free[sl])

            def load_wz(hp):
                sl = hp % 2
                return P.dma('pool', wzd[sl][:], w_in_v[:, :, O_ZD + hp * 128:O_ZD + (hp + 1) * 128], wzsem[sl], wzd_free[sl])

            fin_pending = [None]
            epi_c = [0]

            def make_fin(hp, accw):
                WZ = [wz_tok[hp]]
                zsl = hp % 2

                def step(tt):
                    kmz = None
                    T = slice(tt * TT, (tt + 1) * TT)
                    kbz, pz, pwz = pr.get()
                    for c in range(8):
                        kmz = P.mm(pz[:], wzd[zsl][:, c, :], hT[:, c, T], start=(c == 0), stop=(c == 7),
                                   waits=(WZ + pwz) if c == 0 else (), sig=(c == 7))
                    e = epi_c[0] % 2
                    epi_c[0] += 1
                    kbe, pe_, pwe = pr.get()
                    ke = P.act(pe_[:], pz[:], AF.Exp, scale=-1.0, waits=[kmz] + pwe)
                    kw = None
                    for hh in range(2):
                        kw = P.stt(szd[e][hh * 64:(hh + 1) * 64, :], pe_[hh * 64:(hh + 1) * 64, :], 1.0, acc[hh][64:128, T],
                                   ALU.add, ALU.mult, [ke] + accw + szd_free[e])
                    pr.rel(kbe, [kw])
                    kl = P.act(rd[e][:], szd[e][:], AF.Ln, waits=[kw] + rd_free[e])
                    szd_free[e] = [kl]
                    kx = P.act(rd[e][:], rd[e][:], AF.Exp, scale=-1.0, waits=[kl])
                    i1, t1, w1 = t1r.get()
                    ka1 = None
                    for hh in range(2):
                        ka1 = P.tt('dve', t1[hh * 64:(hh + 1) * 64, :], acc[hh][0:64, T], pz[hh * 64:(hh + 1) * 64, :], ALU.mult,
                                   [kmz] + accw + w1)
                    pr.rel(kbz, [ke, ka1])
                    ku = P.tt('pool', ustp[e][:], t1[:], rd[e][:], ALU.mult, [ka1, kx] + ust_free[e])
                    t1r.rel(i1, [ku])
                    rd_free[e] = [ku]
                    kus = P.dma('sp', UM[512 + hp * 128:512 + (hp + 1) * 128, T], ustp[e][:], ust_sems[e], [ku])
                    ust_free[e] = [kus]
                    acc_free[0] = [ka1, kw]
                    acc_free[1] = [ka1, kw]
                    if tt == NT - 1:
                        wzd_free[zsl] = [kmz]
                return step

            load_w(0)
            wz_tok = {0: load_wz(0)}
            for ci, (hp, g) in enumerate(combos):
                sl = ci % 2
                if ci + 1 < len(combos):
                    load_w(ci + 1)
                WD = [wtok[ci]]
                d = (1, 4, 16)[g]
                L = S // d

                def tokslice(f0, count):
                    r = f0 // L
                    i0 = f0 % L
                    start = r + d * i0
                    return slice(start, start + d * (count - 1) + 1, d)

                pend = []
                lastB = [None]

                def stageA(which, tt):
                    T = slice(tt * TT, (tt + 1) * TT)
                    kb, ps, pw = pr.get()
                    km = None
                    for c in range(8):
                        km = P.mm(ps[:], wd[sl][:, c, which * 128:(which + 1) * 128], hT[:, c, T], start=(c == 0), stop=(c == 7),
                                  waits=(WD + pw) if c == 0 else (), sig=(c == 7))
                    ia, ab, wa = abr.get()
                    ka = P.act(ab[:], ps[:], AF.Copy, waits=[km] + wa)
                    pr.rel(kb, [ka])
                    pend.append((which, tt, ia, ab, ka))

                def stageB():
                    which, tt, ia, ab, ka = pend.pop(0)
                    dst = Qd if which == 0 else Kd
                    T = slice(tt * TT, (tt + 1) * TT)
                    kb2, pB, pw2 = pr.get()
                    kmr = P.mm(pB[:], rotm[:, 96:224], ab[:], waits=[ka] + CST + pw2, sig=True)
                    i1, t1, w1 = t1r.get()
                    i2, t2, w2 = t2r.get()
                    n_i = TT // d
                    nat = "p (i r) -> p i r"
                    t1d = t1[:].rearrange("p (r i) -> p i r", r=d)
                    t2d = t2[:].rearrange("p (r i) -> p i r", r=d)
                    k1 = P.tt('dve', t1d, ab[:].rearrange(nat, r=d), Cd[:, T].rearrange(nat, r=d), ALU.mult, [ka] + w1)
                    k2 = P.tt('dve', t2d, pB[:].rearrange(nat, r=d), Sd[:, T].rearrange(nat, r=d), ALU.mult, [kmr] + w2)
                    pr.rel(kb2, [k2])
                    abr.rel(ia, [kmr, k1])
                    dv = dst[:, :].rearrange("p (r i) -> p r i", r=d)[:, :, tt * n_i:(tt + 1) * n_i]
                    k3 = P.tt('pool', dv, t1[:].rearrange("p (r i) -> p r i", r=d),
                              t2[:].rearrange("p (r i) -> p r i", r=d), ALU.add, [k1, k2] + QK_free)
                    t1r.rel(i1, [k3])
                    t2r.rel(i2, [k3])
                    lastB[0] = k3

                for which in (0, 1):
                    for tt in range(NT):
                        stageA(which, tt)
                        if len(pend) > 1:
                            stageB()
                if stop_after == 2.2:
                    while pend:
                        stageB()
                    P.barrier()
                    P.dead = True
                V_ready = []
                km = None
                for nb4 in range(8):
                    kb, ps, pw = pr.get()
                    for q4 in range(4):
                        n = nb4 * 4 + q4
                        tsl = tokslice(n * 128, 128)
                        for c in range(8):
                            km = P.mm(ps[:, q4 * 128:(q4 + 1) * 128], hT[:, c, tsl], wd[sl][:, c, 256:384], start=(c == 0), stop=(c == 7),
                                      waits=(WD + pw) if (c == 0 and q4 == 0) else (), sig=(c == 7 and q4 == 3))
                    eng = 'act' if nb4 % 2 == 0 else 'dve'
                    ka = P.copy(eng, Vd[:, nb4 * 4:(nb4 + 1) * 4, :, 0:64], ps[:].rearrange("p (n h d) -> p n h d", n=4, h=2),
                                [km, k_vd1] + Vd_free)
                    pr.rel(kb, [ka])
                    V_ready.append(ka)
                    if nb4 == 0:
                        while pend:
                            stageB()
                    if fin_pending[0] is not None:
                        fin_pending[0](nb4)
                        if nb4 == 7:
                            fin_pending[0] = None
                wd_free[sl] = [km]
                if g == 0 and hp + 1 < 4:
                    wz_tok[hp + 1] = load_wz(hp + 1)
                QK_ready = [('pool', P.cnt['pool']), ('dve', P.cnt['dve'])]
                if stop_after == 2.3:
                    P.barrier()
                    P.dead = True
                nbp = L // 128
                pstate = [{}, {}]

                def get_pu(hh, n):
                    bk = n // 4
                    if bk not in pstate[hh]:
                        kbu, pu, pwu = pur[hh].get()
                        pstate[hh][bk] = dict(kb=kbu, pu=pu, pw=pwu, first=True)
                    return pstate[hh][bk]

                sts = {}

                def dS(n):
                    bb = n % nbp
                    N = 256 if bb + 1 < nbp else 128
                    tl = []
                    for hh in range(2):
                        p0 = hh * 64
                        kb, ps, pw = pr.get()
                        km_ = P.mm(ps[:, 0:N], Kd[p0:p0 + 64, n * 128:(n + 1) * 128], Qd[p0:p0 + 64, n * 128:n * 128 + N],
                                   waits=QK_ready + pw, sig=True)
                        tl.append((kb, ps, km_))
                    sts[n] = dict(tl=tl, N=N)

                def dE(n):
                    s_ = sts[n]
                    N = s_['N']
                    ki, pt, pw = ptr.get()
                    kes = []
                    for hh in range(2):
                        kb, ps, km_ = s_['tl'][hh]
                        ke = P.act(pt[:, hh * 256:hh * 256 + N], ps[:, 0:N], AF.Exp, scale=SC_DIL, waits=[km_] + pw)
                        pr.rel(kb, [ke])
                        kes.append(ke)
                    if N == 256:
                        kmk = P.tt('dve', pt[:, 0:512], pt[:, 0:512], mask512[:, 0:512], ALU.mult, kes + [k_m512])
                    else:
                        ptv = pt[:, :].rearrange("p (h c) -> p h c", h=2)[:, :, 0:128]
                        kmk = P.tt('dve', ptv, ptv, mask512[:, :].rearrange("p (h c) -> p h c", h=2)[:, :, 0:128], ALU.mult, kes + [k_m512])
                    s_['ki'] = ki; s_['pt'] = pt; s_['ke'] = kmk

                def dPV(n):
                    s_ = sts[n]
                    N = s_['N']
                    bb = n % nbp
                    kpv = None
                    for hh in range(2):
                        u = get_pu(hh, n)
                        c0 = (n % 4) * 128
                        w = [s_['ke']] + V_ready + (u['pw'] if u['first'] else [])
                        u['first'] = False
                        kpv = P.mm(u['pu'][:, c0:c0 + 128], Vd[:, n, hh, :], s_['pt'][:, hh * 256:hh * 256 + 128], start=(bb == 0), stop=True,
                                   waits=w, sig=True)
                        if N == 256:
                            u2 = get_pu(hh, n + 1)
                            c1 = ((n + 1) % 4) * 128
                            w2 = (u2['pw'] if u2['first'] else [])
                            u2['first'] = False
                            kpv = P.mm(u2['pu'][:, c1:c1 + 128], Vd[:, n, hh, :], s_['pt'][:, hh * 256 + 128:hh * 256 + 256], start=True, stop=False,
                                       waits=w2, sig=True)
                    ptr.rel(s_['ki'], [kpv])
                    del sts[n]
                    if n % 4 == 3:
                        bk = n // 4
                        f0 = bk * 512
                        r = f0 // L
                        i0 = f0 % L
                        for hh in range(2):
                            u = pstate[hh][bk]
                            accv = acc[hh][:, :].rearrange("p (i r) -> p r i", r=d)
                            if L >= 512:
                                dstv = accv[:, r, i0:i0 + 512]
                                srcv = u['pu'][:, :]
                            else:
                                nr = 512 // L
                                dstv = accv[:, r:r + nr, :]
                                srcv = u['pu'][:, :].rearrange("p (r i) -> p r i", r=nr)
                            if g == 0:
                                eng = 'act' if hh == 0 else 'dve'
                                ka_ = P.copy(eng, dstv, srcv, [kpv] + acc_free[hh])
                            else:
                                ka_ = P.tt('dve', dstv, dstv, srcv, ALU.add, [kpv] + acc_free[hh])
                            pur[hh].rel(u['kb'], [ka_])

                for i in range(32 + 3):
                    if i < 32:
                        dS(i)
                    if 0 <= i - 1 < 32:
                        dE(i - 1)
                    if 0 <= i - 3 < 32:
                        dPV(i - 3)
                if g == 0:
                    acc_free = [[], []]
                QK_free = [('pe', P.cnt['pe'])]
                Vd_free = [('pe', P.cnt['pe'])]
                if stop_after == 2.4 or (stop_after == 2.5 and ci == 1) or (stop_after == 2.6 and ci == 2):
                    P.barrier()
                    P.dead = True
                if g == 2:
                    fin_pending[0] = make_fin(hp, [('dve', P.cnt['dve']), ('act', P.cnt['act'])])
            if fin_pending[0] is not None:
                for tt_ in range(NT):
                    fin_pending[0](tt_)
                fin_pending[0] = None
            P.barrier()

        st_cd.close()
        if stop_after == 3:
            P.dead = True
        with ExitStack() as st4:
            wg = sbt(st4, "wg", [128, 16, 8, 128], BF16)
            wpm = sbt(st4, "wpm", [128, 4, 1024], BF16)
            wpd = sbt(st4, "wpd", [128, 4, 1024], BF16)
            wout = sbt(st4, "wout", [128, 8, 1024], BF16)
            gpost = sbt(st4, "gpost", [128, 1024])
            dw3 = [P.dsem(f"dw3_{i}") for i in range(4)]
            k_wg_oc = []
            for oc_ in range(8):
                sm_ = P.dsem(f"dwg{oc_}")
                P.dma('sp', wg[:, oc_, :, :], WGb[oc_], sm_, [k_wcast])
                k_wg_oc.append(P.dma('sp', wg[:, 8 + oc_, :, :], WGb[8 + oc_], sm_, [k_wcast]))
                if oc_ == 0:
                    k_wpm = P.dma('sp', wpm[:], WPMb.rearrange("(c p) n -> p c n", p=128), dw3[1], [k_wcast])
                    k_wpd = P.dma('sp', wpd[:], WPDb.rearrange("(c p) n -> p c n", p=128), dw3[1], [k_wcast])
                    k_wpm = k_wpd
            k_wout = P.dma('sp', wout[:], WOb.rearrange("(c p) n -> p c n", p=128), dw3[2], [k_wcast])
            k_gp = P.dma('sp', gpost[:], gpost_d.broadcast_to([128, 1024]), dw3[3])
            um = [sbt(st4, f"um{i}", [128, 4, TT], BF16) for i in range(2)]
            ud = [sbt(st4, f"ud{i}", [128, 4, TT], BF16) for i in range(2)]
            usem = [P.dsem("dul0"), P.dsem("dul1")]
            sg = [sbt(st4, f"sg{i}", [128, TT]) for i in range(2)]
            m1 = sbt(st4, "m1", [128, TT])
            m2 = sbt(st4, "m2", [128, TT])
            mg = [sbt(st4, f"mg{i}", [128, 8, TT], BF16) for i in range(2)]
            xt = [sbt(st4, f"xt{i}", [128, 1024]) for i in range(3)]
            o1 = [sbt(st4, f"o1_{i}", [128, 1024]) for i in range(2)]
            junk = sbt(st4, "junk", [128, 512])
            ssq = [sbt(st4, f"ssq{i}", [128, 4]) for i in range(2)]
            xsem = [P.dsem(f"dxt{i}") for i in range(3)]
            ysem = [P.dsem(f"dy{i}") for i in range(3)]
            pr = Ring(banks[0:4])
            por = Ring(banks[4:8])
            u_free = [[], []]
            u_tok = [None, None]
            sg_free = [[], []]
            mm_free = [[], []]
            mg_free = [[], []]
            mg_done = [None, None]
            xt_free = [[], [], []]
            o1_free = [[], []]
            ssq_free = [[], []]
            junk_free = [[]]
            out_toks = []
            xi = [0]
            last_umm = [None, None]
            last_omm = [None, None]

            def load_u(tt):
                ub = tt % 2
                T = slice(tt * TT, (tt + 1) * TT)
                P.dma('sp', um[ub][:], UM[0:512, T].rearrange("(c p) t -> p c t", p=128), usem[ub], u_free[ub])
                u_tok[ub] = P.dma('sp', ud[ub][:], UM[512:1024, T].rearrange("(c p) t -> p c t", p=128), usem[ub], u_free[ub])

            def G_oc(tt, oc):
                ub = tt % 2
                T = slice(tt * TT, (tt + 1) * TT)
                grp = []
                for (wt, off, src, nk, kw) in ((wg, 0, None, 8, k_wg_oc[oc]), (wg, 8, None, 8, k_wg_oc[oc]), (wpm, 0, um[ub], 4, k_wpm), (wpd, 0, ud[ub], 4, k_wpd)):
                    kb, ps, pw = pr.get()
                    km = None
                    for c in range(nk):
                        rhs = hT[:, c, T] if src is None else src[:, c, :]
                        lw = wt[:, off + oc, c, :] if src is None else wt[:, c, oc * 128:(oc + 1) * 128]
                        km = P.mm(ps[:], lw, rhs, start=(c == 0), stop=(c == nk - 1),
                                  waits=([kw, u_tok[ub]] + pw) if c == 0 else (), sig=(c == nk - 1))
                    grp.append((kb, ps, km))
                last_umm[ub] = grp[3][2]
                ks1 = P.act(sg[0][:], grp[0][1][:], AF.Sigmoid, waits=[grp[0][2]] + sg_free[0])
                pr.rel(grp[0][0], [ks1])
                ks2 = P.act(sg[1][:], grp[1][1][:], AF.Sigmoid, waits=[grp[1][2]] + sg_free[1])
                pr.rel(grp[1][0], [ks2])
                k1 = P.tt('dve', m1[:], sg[0][:], grp[2][1][:], ALU.mult, [ks1, grp[2][2]] + mm_free[0])
                pr.rel(grp[2][0], [k1])
                k2 = P.tt('dve', m2[:], sg[1][:], grp[3][1][:], ALU.mult, [ks2, grp[3][2]] + mm_free[1])
                pr.rel(grp[3][0], [k2])
                sg_free[0] = [k1]
                sg_free[1] = [k2]
                k3 = P.tt('pool', mg[ub][:, oc, :], m1[:], m2[:], ALU.add, [k1, k2] + (mg_free[ub] if oc == 0 else []))
                mm_free[0] = [k3]
                mm_free[1] = [k3]
                mg_done[ub] = k3
                if oc == 7:
                    u_free[ub] = [last_umm[ub]]

            def O_s(tt, s):
                ub = tt % 2
                r0 = tt * TT + s * 128
                xb = xi[0] % 3
                ob = xi[0] % 2
                xi[0] += 1
                kx = P.dma('sp', xt[xb][:], x[r0:r0 + 128, :], xsem[xb], xt_free[xb])
                halves = []
                for hf in range(2):
                    kb, po, pw = por.get()
                    km = None
                    for oc in range(8):
                        km = P.mm(po[:], mg[ub][:, oc, s * 128:(s + 1) * 128], wout[:, oc, hf * 512:(hf + 1) * 512], start=(oc == 0), stop=(oc == 7),
                                  waits=([k_wout, mg_done[ub]] + pw) if oc == 0 else (), sig=(oc == 7))
                    kq = P.act(junk[:], po[:], AF.Square, accum=ssq[ob][:, hf:hf + 1], waits=[km] + junk_free[0] + ssq_free[ob])
                    junk_free[0] = [kq]
                    halves.append((kb, po, km, kq))
                    last_omm[ub] = km
                ka = P.tt('dve', ssq[ob][:, 2:3], ssq[ob][:, 0:1], ssq[ob][:, 1:2], ALU.add, [halves[0][3], halves[1][3]])
                kr = P.act(ssq[ob][:, 3:4], ssq[ob][:, 2:3], AF.Sqrt, scale=1.0 / 1024, bias=epsb[:], waits=[ka])
                kr2 = P.recip(ssq[ob][:, 3:4], ssq[ob][:, 3:4], [kr])
                ko = None
                for hf in range(2):
                    H = slice(hf * 512, (hf + 1) * 512)
                    ko = P.stt(o1[ob][:, H], halves[hf][1][:], ssq[ob][:, 3:4], gpost[:, H], ALU.mult, ALU.mult,
                               waits=[kr2, k_gp] + o1_free[ob])
                    por.rel(halves[hf][0], [ko])
                ssq_free[ob] = [ko]
                kf = P.tt('pool', xt[xb][:], o1[ob][:], xt[xb][:], ALU.add, [ko, kx])
                o1_free[ob] = [kf]
                kys = P.dma('sp', y[r0:r0 + 128, :], xt[xb][:], ysem[xb], [kf])
                xt_free[xb] = [kys]
                out_toks.append(kys)
                if s == 3:
                    mg_free[ub] = [last_omm[ub]]

            load_u(0)
            for oc in range(8):
                G_oc(0, oc)
            for tt in range(NT):
                if tt + 1 < NT:
                    load_u(tt + 1)
                for s in range(4):
                    O_s(tt, s)
                    if tt + 1 < NT:
                        G_oc(tt + 1, 2 * s)
                        G_oc(tt + 1, 2 * s + 1)
            P.wait('sp', out_toks)
            P.barrier()
        P.emit()
    return nc


_NC_CACHE = {}


def _consts():
    p = np.arange(128)[:, None]
    f = np.arange(128)[None, :]
    ident = np.eye(128, dtype=np.float32)
    maskab = np.zeros((128, 256), np.float32)
    maskab[:, 0:128] = np.where(p > f, NEG, 0.0)
    maskab[:, 128:256] = np.where(f > p, NEG, 0.0)
    rot = np.zeros((128, 224), np.float32)
    for i in range(16):
        rot[80 + i, 64 + i] = -1.0
        rot[64 + i, 80 + i] = 1.0
    for hb in (0, 64):
        for i in range(8):
            rot[hb + 8 + i, 96 + hb + i] = -1.0
            rot[hb + i, 96 + hb + 8 + i] = 1.0
    theta = np.float32(500000.0)
    inv_mla = (theta ** (-np.arange(0, 32, 2, dtype=np.float32) / np.float32(32))).astype(np.float32)
    inv_dil = (theta ** (-np.arange(0, 16, 2, dtype=np.float32) / np.float32(16))).astype(np.float32)
    freqs = np.zeros((128, 2), np.float32)
    for q in range(4):
        for i in range(32):
            freqs[q * 32 + i, 0] = inv_mla[i % 16]
            freqs[q * 32 + i, 1] = inv_dil[i % 8] if i < 16 else 0.0
    mask01 = (maskab == 0.0).astype(np.float32)
    return dict(ident=ident, maskab=maskab, rotm=rot, freqs=freqs, mask01=mask01)


def make_in_maps(x, positions, pre_norm_g, w_in, q_norm_g, w_uq, kv_norm_g, w_ukv,
                 w_proj_mla, w_proj_dil, w_out, post_norm_g):
    cst = _consts()
    f32 = np.float32
    shared = dict(
        w_in=np.ascontiguousarray(w_in[0], f32), w_uq=np.ascontiguousarray(w_uq[0], f32),
        w_ukv=np.ascontiguousarray(w_ukv[0], f32), w_pm=np.ascontiguousarray(w_proj_mla[0], f32),
        w_pd=np.ascontiguousarray(w_proj_dil[0], f32), w_out=np.ascontiguousarray(w_out[0], f32),
        gpre=np.ascontiguousarray(np.asarray(pre_norm_g[0], f32).reshape(8, 128).T),
        gq=np.ascontiguousarray(np.asarray(q_norm_g[0], f32).reshape(3, 128).T),
        gkv=np.ascontiguousarray(np.asarray(kv_norm_g[0], f32).reshape(2, 128).T),
        gpost=np.ascontiguousarray(np.asarray(post_norm_g[0], f32).reshape(1, 1024)),
        **cst)
    maps = []
    for b in range(8):
        xb = np.asarray(x[b], f32)
        m = dict(shared)
        m["x"] = np.ascontiguousarray(xb)
        m["xT"] = np.ascontiguousarray(xb.T)
        m["pos"] = np.ascontiguousarray(np.asarray(positions[b], np.int32).reshape(1, S))
        maps.append(m)
    return maps


def kernel(x, positions, pre_norm_g, w_in, q_norm_g, w_uq, kv_norm_g, w_ukv,
           w_proj_mla, w_proj_dil, w_out, post_norm_g):
    if "nc" not in _NC_CACHE:
        _NC_CACHE["nc"] = build(False)
    nc = _NC_CACHE["nc"]
    maps = make_in_maps(x, positions, pre_norm_g, w_in, q_norm_g, w_uq, kv_norm_g, w_ukv,
                        w_proj_mla, w_proj_dil, w_out, post_norm_g)
    res = run_bass_kernel_spmd(nc, maps, core_ids=list(range(8)))
    out = np.stack([np.asarray(r["y"], np.float32) for r in res.results], axis=0)
    return out
```

```python
import math
import numpy as np
import concourse.bass as bass
import concourse.mybir as mybir
from concourse.bass_utils import run_bass_kernel_spmd
from contextlib import ExitStack

F32 = mybir.dt.float32
BF16 = mybir.dt.bfloat16
I32 = mybir.dt.int32
AF = mybir.ActivationFunctionType
ALU = mybir.AluOpType

S = 4096
TT = 512
NT = S // TT
EPS = 1e-6
NEG = -30000.0
C1 = 6.28125
C2 = 2.0 * math.pi - 6.28125
O_CQ, O_CKV, O_KR, O_DIL, O_ZM, O_ZD, O_GM, O_GD = 0, 384, 640, 672, 5280, 5792, 6304, 7328
SC_MLA = 96.0 ** -0.5
SC_DIL = 64.0 ** -0.5


class Prog:
    ENG = ('pe', 'act', 'dve', 'pool', 'sp')

    def __init__(self, nc, es):
        self.nc = nc
        self.es = es
        self.streams = {e: [] for e in self.ENG}
        self.sem = {}
        self.cnt = {}
        self.dsems = []
        self.dead = False
        for e in self.ENG[:4]:
            self._newsem(e)

    def _newsem(self, name):
        self.sem[name] = self.es.enter_context(self.nc.semaphore(name))
        self.cnt[name] = 0
        return name

    def dsem(self, name):
        self.dsems.append(name)
        return self._newsem(name)

    def op(self, eng, fn, waits=(), sig=True):
        if self.dead:
            return (eng, self.cnt[eng])
        tok = None
        if sig:
            self.cnt[eng] += 1
            tok = (eng, self.cnt[eng])
        self.streams[eng].append((fn, [w for w in waits if w is not None], (eng, 1) if sig else None))
        return tok

    def dma(self, q, out, in_, sem, waits=()):
        if self.dead:
            return (sem, self.cnt[sem])
        self.cnt[sem] += 16
        tok = (sem, self.cnt[sem])
        self.streams[q].append((lambda e: e.dma_start(out=out, in_=in_), [w for w in waits if w is not None], (sem, 16)))
        return tok

    def wait(self, eng, waits):
        if self.dead:
            return
        self.streams[eng].append((None, [w for w in waits if w is not None], None))

    def barrier(self):
        toks = [(s, self.cnt[s]) for s in list(self.ENG[:4]) + self.dsems if self.cnt[s] > 0]
        for e in self.ENG:
            self.wait(e, toks)

    def mm(self, out, lhsT, rhs, start=True, stop=True, waits=(), sig=False):
        return self.op('pe', lambda e: e.matmul(out, lhsT=lhsT, rhs=rhs, start=start, stop=stop), waits, sig)

    def act(self, out, in_, func, scale=None, bias=None, accum=None, waits=(), sig=True):
        kw = {}
        if scale is not None:
            kw['scale'] = scale
        if bias is not None:
            kw['bias'] = bias
        if accum is not None:
            kw['accum_out'] = accum
        return self.op('act', lambda e: e.activation(out=out, in_=in_, func=func, **kw), waits, sig)

    def tt(self, eng, out, in0, in1, op, waits=(), sig=True):
        return self.op(eng, lambda e: e.tensor_tensor(out=out, in0=in0, in1=in1, op=op), waits, sig)

    def ts(self, eng, out, in0, s1, op0, s2=None, op1=None, waits=(), sig=True):
        if op1 is None:
            return self.op(eng, lambda e: e.tensor_scalar(out=out, in0=in0, scalar1=s1, scalar2=None, op0=op0), waits, sig)
        return self.op(eng, lambda e: e.tensor_scalar(out=out, in0=in0, scalar1=s1, scalar2=s2, op0=op0, op1=op1), waits, sig)

    def stt(self, out, in0, scalar, in1, op0, op1, waits=(), sig=True):
        return self.op('dve', lambda e: e.scalar_tensor_tensor(out=out, in0=in0, scalar=scalar, in1=in1, op0=op0, op1=op1), waits, sig)

    def copy(self, eng, out, in_, waits=(), sig=True):
        if eng == 'act':
            return self.act(out, in_, AF.Copy, waits=waits, sig=sig)
        return self.op(eng, lambda e: e.tensor_copy(out=out, in_=in_), waits, sig)

    def recip(self, out, in_, waits=(), sig=True):
        return self.op('dve', lambda e: e.reciprocal(out=out, in_=in_), waits, sig)

    def memset(self, eng, ap, val, waits=(), sig=True):
        return self.op(eng, lambda e: e.memset(ap, val), waits, sig)

    def check(self):
        pos = {e: 0 for e in self.ENG}
        val = {s: 0 for s in self.sem}
        progress = True
        while progress:
            progress = False
            for e in self.ENG:
                st = self.streams[e]
                while pos[e] < len(st):
                    fn, waits, inc = st[pos[e]]
                    if all(val[s] >= v for (s, v) in waits):
                        if inc is not None:
                            val[inc[0]] += inc[1]
                        pos[e] += 1
                        progress = True
                    else:
                        break
        bad = {e: (pos[e], len(self.streams[e])) for e in self.ENG if pos[e] < len(self.streams[e])}
        if bad:
            msg = []
            for e, (p, n) in bad.items():
                fn, waits, inc = self.streams[e][p]
                msg.append("%s stuck at %d/%d waits=%s have=%s" % (e, p, n, [w for w in waits if val[w[0]] < w[1]], [(w[0], val[w[0]]) for w in waits if val[w[0]] < w[1]]))
            raise RuntimeError("DEADLOCK: " + "; ".join(msg))

    def emit(self):
        self.check()
        nc = self.nc
        engobj = {'pe': 'tensor', 'act': 'scalar', 'dve': 'vector', 'pool': 'gpsimd', 'sp': 'sync'}
        with nc.Block() as block:
            for en in self.ENG:
                stream = self.streams[en]

                def body(engine, stream=stream):
                    waited = {}
                    for fn, waits, inc in stream:
                        for (s, v) in waits:
                            if waited.get(s, 0) >= v:
                                continue
                            engine.wait_ge(self.sem[s], v)
                            waited[s] = v
                        if fn is not None:
                            ins = fn(engine)
                            if inc is not None:
                                ins.then_inc(self.sem[inc[0]], inc[1])
                getattr(block, engobj[en])(body)


class Ring:
    def __init__(self, items):
        self.items = list(items)
        self.free = [[] for _ in self.items]
        self.busy = [False for _ in self.items]
        self.i = 0

    def get(self):
        k = self.i % len(self.items)
        self.i += 1
        assert not self.busy[k], "ring slot re-acquired before its release was emitted"
        self.busy[k] = True
        w = self.free[k]
        self.free[k] = []
        return k, self.items[k], w

    def rel(self, k, toks):
        self.free[k] = [t for t in toks if t is not None]
        self.busy[k] = False


class _Stop(Exception):
    pass


def build(debug=False, stop_after=99):
    nc = bass.Bass("TRN2", target_bir_lowering=False)

    def din(name, shape, dt=F32):
        return nc.dram_tensor(name, shape, dt, kind="ExternalInput").ap()

    xT = din("xT", [1024, S]); x = din("x", [S, 1024]); pos = din("pos", [1, S], I32)
    w_in = din("w_in", [1024, 8352]); w_uq = din("w_uq", [384, 768]); w_ukv = din("w_ukv", [256, 1024])
    w_pm = din("w_pm", [512, 1024]); w_pd = din("w_pd", [512, 1024]); w_out = din("w_out", [1024, 1024])
    ident_d = din("ident", [128, 128]); mask_d = din("maskab", [128, 256]); rot_d = din("rotm", [128, 224])
    freq_d = din("freqs", [128, 2]); gpre_d = din("gpre", [128, 8]); gq_d = din("gq", [128, 3])
    gkv_d = din("gkv", [128, 2]); gpost_d = din("gpost", [1, 1024])
    m01_d = din("mask01", [128, 256])
    y = nc.dram_tensor("y", [S, 1024], F32, kind="ExternalOutput").ap()
    sk = "ExternalOutput" if debug else "Internal"
    QM = nc.dram_tensor("QM", [8, 96, S], BF16, kind=sk).ap()
    KM = nc.dram_tensor("KM", [8, 96, S], BF16, kind=sk).ap()
    VM = nc.dram_tensor("VM", [8, 128, 32 * 128], BF16, kind=sk).ap()
    UM = nc.dram_tensor("UM", [1024, S], BF16, kind=sk).ap()
    WGb = nc.dram_tensor("WGb", [16, 128, 8, 128], BF16, kind="Internal").ap()
    WPMb = nc.dram_tensor("WPMb", [512, 1024], BF16, kind="Internal").ap()
    WPDb = nc.dram_tensor("WPDb", [512, 1024], BF16, kind="Internal").ap()
    WOb = nc.dram_tensor("WOb", [1024, 1024], BF16, kind="Internal").ap()

    w_in_v = w_in.rearrange("(c p) n -> p c n", p=128)
    xT_v = xT.rearrange("(c p) t -> p c t", p=128)

    with ExitStack() as es:
        P = Prog(nc, es)

        uid = [0]

        def sbt(stack, name, shape, dt=F32):
            uid[0] += 1
            return stack.enter_context(nc.sbuf_tensor('s%d_%s' % (uid[0], name), shape, dt))

        banks = [es.enter_context(nc.psum_tensor(f"ps{i}", [128, 512], F32)) for i in range(8)]

        hT = sbt(es, "hT", [128, 8, S], BF16)
        ident = sbt(es, "ident", [128, 128], BF16)
        maskab = sbt(es, "maskab", [128, 256], BF16)
        rotm = sbt(es, "rotm", [128, 224], BF16)
        ones = sbt(es, "ones", [128, 128], F32)
        freqs = sbt(es, "freqs", [128, 2], F32)
        gpre = sbt(es, "gpre", [128, 8], F32)
        gq = sbt(es, "gq", [128, 3], F32)
        gkv = sbt(es, "gkv", [128, 2], F32)
        epsb = sbt(es, "epsb", [128, 1], F32)
        hpib = sbt(es, "hpib", [128, 1], F32)

        dc = P.dsem("dcst")
        dcp = P.dsem("dcstp")
        P.dma('pool', ident[:], ident_d, dcp)
        P.dma('pool', maskab[:], mask_d, dcp)
        k_cstp = P.dma('pool', rotm[:], rot_d, dcp)
        P.dma('sp', freqs[:], freq_d, dc)
        P.dma('sp', gpre[:], gpre_d, dc)
        P.dma('sp', gq[:], gq_d, dc)
        k_cst = P.dma('sp', gkv[:], gkv_d, dc)
        k_ones = P.memset('dve', ones[:], 1.0)
        k_eps = P.memset('dve', epsb[:], EPS)
        k_hpi = P.memset('dve', hpib[:], math.pi / 2)
        CST = [k_cst, k_cstp, k_ones, k_eps, k_hpi]

        def gen_tables_a(stack, fcol, tag, pk_dt):
            CH = 1024
            posi = sbt(stack, "tb_posi", [128, CH], I32)
            ang = sbt(stack, "tb_ang", [128, CH])
            xx = sbt(stack, "tb_x", [128, CH])
            ki = sbt(stack, "tb_ki", [128, CH], I32)
            kf = sbt(stack, "tb_kf", [128, CH])
            gg = [sbt(stack, "tb_g0", [128, CH]), sbt(stack, "tb_g1", [128, CH])]
            pk = [sbt(stack, "tb_pk0", [128, CH], pk_dt), sbt(stack, "tb_pk1", [128, CH], pk_dt)]
            dsm = P.dsem("dtb" + tag)
            kd = None
            for q in range(4):
                kd = P.dma('sp', posi[q * 32:(q + 1) * 32, :], pos[:, q * CH:(q + 1) * CH].broadcast_to([32, CH]), dsm)
            k0 = P.copy('dve', xx[:], posi[:], [kd])
            k1 = P.ts('dve', ang[:], xx[:], freqs[:, fcol:fcol + 1], ALU.mult, waits=[k0] + CST)
            last = [k1]
            khs = []
            for which in (0, 1):
                ka = P.ts('dve', xx[:], ang[:], 1.0 / (2 * math.pi), ALU.mult, 0.25 * which, ALU.add, waits=last)
                kb = P.copy('dve', ki[:], xx[:], [ka])
                kc = P.copy('dve', kf[:], ki[:], [kb])
                kdd = P.tt('dve', xx[:], xx[:], kf[:], ALU.subtract, [kc])
                ke = P.ts('dve', gg[which][:], xx[:], 0.5, ALU.is_gt, waits=[kdd])
                kff = P.tt('dve', kf[:], kf[:], gg[which][:], ALU.add, [ke])
                kg = P.stt(xx[:], kf[:], -C1, ang[:], ALU.mult, ALU.add, [kff])
                kh0 = P.stt(gg[which][:], kf[:], -C2, xx[:], ALU.mult, ALU.add, [kg])
                lim = 3.14159
                lo, hi = (-lim, lim) if which == 0 else (-lim - math.pi / 2, lim - math.pi / 2)
                kh = P.ts('dve', gg[which][:], gg[which][:], lo, ALU.max, hi, ALU.min, waits=[kh0])
                last = [kh]
                khs.append(kh)
            return dict(gg=gg, pk=pk, khs=khs, CH=CH)

        def gen_tables_b(stt_, Ct, St, dst_groups, extra=()):
            CH = stt_['CH']
            gg, pk, khs = stt_['gg'], stt_['pk'], stt_['khs']
            outs = [P.act(pk[0][:], gg[0][:], AF.Sin, waits=[khs[0]]),
                    P.act(pk[1][:], gg[1][:], AF.Sin, bias=hpib[:], waits=[khs[1]] + CST)]
            done = []
            for which, dst in ((0, St), (1, Ct)):
                for g0 in dst_groups:
                    for q in range(4):
                        done.append(P.copy('pool', dst[g0:g0 + 32, q * CH:(q + 1) * CH], pk[which][q * 32:(q + 1) * 32, :], [outs[which]] + list(extra)))
            return done

        st_cd = es.enter_context(ExitStack())
        Cd = sbt(st_cd, "Cd", [128, S], BF16)
        Sd = sbt(st_cd, "Sd", [128, S], BF16)
        st_ma = es.enter_context(ExitStack())
        Cm = sbt(st_ma, "Cm", [128, S], F32)
        Sm = sbt(st_ma, "Sm", [128, S], F32)
        hT_tok = [None] * NT
        with ExitStack() as st0:
            xs = [sbt(st0, f"xs{i}", [128, 8, TT]) for i in range(2)]
            sq = [sbt(st0, f"sq{i}", [128, TT]) for i in range(2)]
            Rb = [sbt(st0, f"Rb{i}", [128, TT]) for i in range(2)]
            xsem = [[P.dsem(f"dxs{i}_{c}") for c in range(8)] for i in range(2)]
            xs_free = [[], []]
            sq_free = [[], []]
            Rb_free = [[], []]
            pr = Ring(banks[0:2])
            gen_tables_b(gen_tables_a(st0, 0, 'm', F32), Cm, Sm, [64])
            for tt in range(NT):
                T = slice(tt * TT, (tt + 1) * TT)
                b = tt % 2
                kxc = [P.dma('sp', xs[b][:, c, :], xT_v[:, c, T], xsem[b][c], xs_free[b]) for c in range(8)]
                kx = kxc[7]
                kb, ps, pw = pr.get()
                kmm = None
                for c in range(8):
                    ks = P.act(sq[c % 2][:], xs[b][:, c, :], AF.Square, waits=[kxc[c]] + sq_free[c % 2])
                    kmm = P.mm(ps[:], ones[:], sq[c % 2][:], start=(c == 0), stop=(c == 7),
                               waits=[ks, k_ones] + (pw if c == 0 else []), sig=True)
                    sq_free[c % 2] = [kmm]
                kr = P.act(Rb[b][:], ps[:], AF.Ln, scale=1.0 / 1024, bias=epsb[:], waits=[kmm, k_eps] + Rb_free[b])
                pr.rel(kb, [kr])
                kr2 = P.act(Rb[b][:], Rb[b][:], AF.Exp, scale=-0.5, waits=[kr])
                kl = None
                for c in range(8):
                    kl = P.stt(hT[:, c, T], xs[b][:, c, :], gpre[:, c:c + 1], Rb[b][:], ALU.mult, ALU.mult,
                               waits=[kr2, kxc[c]] + CST)
                xs_free[b] = [kl]
                Rb_free[b] = [kl]
                hT_tok[tt] = kl
            P.barrier()
        if stop_after == 0:
            P.dead = True

        qst_sem = P.dsem("dqst")
        with ExitStack() as st1:
            wA = sbt(st1, "wA", [128, 8, 672], BF16)
            wkr = sbt(st1, "wkr", [128, 8, 96], BF16)
            wuq = sbt(st1, "wuq", [128, 3, 768], BF16)
            wukv = sbt(st1, "wukv", [128, 2, 1024], BF16)
            dw1 = P.dsem("dw1")
            kz = P.memset('pool', wkr[:], 0.0)
            P.dma('pool', wA[:], w_in_v[:, :, 0:672], dw1)
            P.dma('pool', wkr[:, :, 64:96], w_in_v[:, :, O_KR:O_KR + 32], dw1, [kz])
            P.dma('pool', wuq[:], w_uq.rearrange("(c p) n -> p c n", p=128), dw1)
            k_w1 = P.dma('pool', wukv[:], w_ukv.rearrange("(c p) n -> p c n", p=128), dw1)
            W1 = [k_w1]
            cqn = [sbt(st1, f"cqn{i}", [128, 3, TT], BF16) for i in range(2)]
            ckvn = [sbt(st1, f"ckvn{i}", [128, 2, TT], BF16) for i in range(2)]
            sq = [sbt(st1, f"sq1_{i}", [128, TT]) for i in range(2)]
            Rq = [sbt(st1, f"Rq{i}", [128, TT]) for i in range(2)]
            krb = [sbt(st1, f"krb{i}", [96, TT], BF16) for i in range(2)]
            kro = [sbt(st1, f"kro{i}", [96, TT], BF16) for i in range(2)]
            Qst = sbt(st1, "Qst", [96, 8, TT], BF16)
            Kst = sbt(st1, "Kst", [64, 8, TT], BF16)
            Vst = sbt(st1, "Vst", [128, 8, 4, 128], BF16)
            t1r = Ring([sbt(st1, f"t1_{i}", [128, TT]) for i in range(3)])
            t2r = Ring([sbt(st1, f"t2_{i}", [128, TT]) for i in range(3)])
            k_krb0 = [P.memset('pool', krb[i][:], 0.0) for i in range(2)]
            k_vst1 = P.memset('pool', Vst[:], 1.0)
            krsem = [P.dsem("dkr0"), P.dsem("dkr1")]
            kst_sem = P.dsem("dkst")
            vst_sem = P.dsem("dvst")
            pr = Ring(banks)
            sq_free = [[], []]
            cqn_free = [[], []]
            ckvn_free = [[], []]
            Rq_free = [[], []]
            Rqi = [0]
            krb_free = [[], []]
            kro_free = [[], []]
            Qst_free = [[]]
            Kst_free = [[]]
            Vst_free = [[]]
            rot_mla = rotm[0:96, 0:96]
            LT = {}
            wv = wukv[:, :, :].rearrange("p c (h two d) -> p c h two d", two=2, d=64)

            prL = Ring(banks[0:6])
            prU = Ring(banks[6:8])

            def norm_front(groups, idx, dim):
                kbs, pss, pws = prL.get()
                kmm = None
                for n_, j in enumerate(idx):
                    ks = P.act(sq[n_ % 2][:], groups[j][1][:], AF.Square, waits=[groups[j][2]] + sq_free[n_ % 2])
                    kmm = P.mm(pss[:], ones[:], sq[n_ % 2][:], start=(n_ == 0), stop=(n_ == len(idx) - 1),
                               waits=[ks, k_ones] + (pws if n_ == 0 else []), sig=True)
                    sq_free[n_ % 2] = [kmm]
                ri = Rqi[0] % 2
                Rqi[0] += 1
                kr = P.act(Rq[ri][:], pss[:], AF.Ln, scale=1.0 / dim, bias=epsb[:], waits=[kmm] + Rq_free[ri])
                prL.rel(kbs, [kr])
                kr2 = P.act(Rq[ri][:], Rq[ri][:], AF.Exp, scale=-0.5, waits=[kr])
                return ri, kr2

            def norm_back(groups, idx, dst, dst_free, gcol, ri, kr2):
                kl = None
                for n_, j in enumerate(idx):
                    kl = P.stt(dst[:, n_, :], groups[j][1][:], gcol[:, n_:n_ + 1], Rq[ri][:], ALU.mult, ALU.mult,
                               waits=[kr2, groups[j][2]] + dst_free + CST)
                    prL.rel(groups[j][0], [kl])
                Rq_free[ri] = [kl]
                return kl

            def main_groups(tt, cols):
                T = slice(tt * TT, (tt + 1) * TT)
                hw = [hT_tok[tt]] + W1
                groups = []
                for j in cols:
                    kb, ps, pw = prL.get()
                    km = None
                    for c in range(8):
                        km = P.mm(ps[:], wA[:, c, j * 128:(j + 1) * 128], hT[:, c, T],
                                  start=(c == 0), stop=(c == 7), waits=(hw + pw) if c == 0 else (), sig=(c == 7))
                    groups.append((kb, ps, km))
                return groups

            def Lcq_front(tt):
                groups = main_groups(tt, range(3))
                ri, kr2 = norm_front(groups, [0, 1, 2], 384.0)
                LT[tt] = dict(cq=(groups, ri, kr2))

            def Lcq_back(tt):
                b = tt % 2
                groups, ri, kr2 = LT[tt]['cq']
                LT[tt]['k_cq'] = norm_back(groups, [0, 1, 2], cqn[b], cqn_free[b], gq, ri, kr2)

            def Lkv_front(tt):
                T = slice(tt * TT, (tt + 1) * TT)
                b = tt % 2
                hw = [hT_tok[tt]] + W1
                groups = main_groups(tt, range(3, 5))
                kbk, psk, pwk = prL.get()
                kmk = None
                for c in range(8):
                    kmk = P.mm(psk[0:96, :], wkr[:, c, :], hT[:, c, T], start=(c == 0), stop=(c == 7),
                               waits=(hw + pwk) if c == 0 else (), sig=(c == 7))
                ri, kr2 = norm_front(groups, [0, 1], 256.0)
                ka = P.act(krb[b][64:96, :], psk[64:96, :], AF.Copy, waits=[kmk, k_krb0[b]] + krb_free[b])
                prL.rel(kbk, [ka])
                kb2, pB, pw2 = prL.get()
                kmr = P.mm(pB[0:96, :], rot_mla, krb[b][0:96, :], waits=[ka] + CST + pw2, sig=True)
                LT[tt]['kv'] = (groups, ri, kr2, ka, kb2, pB, kmr)

            def Lkv_back(tt):
                T = slice(tt * TT, (tt + 1) * TT)
                b = tt % 2
                groups, ri, kr2, ka, kb2, pB, kmr = LT[tt]['kv']
                LT[tt]['k_ckv'] = norm_back(groups, [0, 1], ckvn[b], ckvn_free[b], gkv, ri, kr2)
                i1, t1, w1 = t1r.get()
                i2, t2, w2 = t2r.get()
                k1 = P.tt('pool', t1[64:96, :], krb[b][64:96, :], Cm[64:96, T], ALU.mult, [ka] + w1)
                k2 = P.tt('dve', t2[64:96, :], pB[64:96, :], Sm[64:96, T], ALU.mult, [kmr] + w2)
                prL.rel(kb2, [k2])
                krb_free[b] = [kmr, k1]
                k3 = P.tt('dve', kro[b][64:96, :], t1[64:96, :], t2[64:96, :], ALU.add, [k1, k2] + kro_free[b])
                t1r.rel(i1, [k3])
                t2r.rel(i2, [k3])
                kd = None
                for h in range(8):
                    kd = P.dma('sp', KM[h, 64:96, T], kro[b][64:96, :], krsem[b], [k3])
                kro_free[b] = [kd]

            def Uq(tt):
                T = slice(tt * TT, (tt + 1) * TT)
                b = tt % 2
                k_cq = LT[tt]['k_cq']
                pend = []
                Q_done = []

                def qA(h):
                    kb, ps, pw = prU.get()
                    km = None
                    for j in range(3):
                        km = P.mm(ps[0:96, :], wuq[:, j, h * 96:(h + 1) * 96], cqn[b][:, j, :], start=(j == 0), stop=(j == 2),
                                  waits=([k_cq] + W1 + pw) if j == 0 else (), sig=(j == 2))
                    ka = P.act(Qst[0:96, h, :], ps[0:96, :], AF.Copy, waits=[km] + Qst_free[0])
                    prU.rel(kb, [ka])
                    pend.append((h, ka))

                def qB():
                    h, ka = pend.pop(0)
                    kb2, pB, pw2 = prU.get()
                    kmr = P.mm(pB[0:96, :], rot_mla, Qst[0:96, h, :], waits=[ka] + CST + pw2, sig=True)
                    i1, t1, w1 = t1r.get()
                    i2, t2, w2 = t2r.get()
                    k1 = P.tt('pool', t1[64:96, :], Qst[64:96, h, :], Cm[64:96, T], ALU.mult, [ka] + w1)
                    k2 = P.tt('dve', t2[64:96, :], pB[64:96, :], Sm[64:96, T], ALU.mult, [kmr] + w2)
                    prU.rel(kb2, [k2])
                    k3 = P.tt('dve', Qst[64:96, h, :], t1[64:96, :], t2[64:96, :], ALU.add, [k1, k2, kmr])
                    t1r.rel(i1, [k3])
                    t2r.rel(i2, [k3])
                    Q_done.append(k3)
                    return kmr

                last_rot = None
                for h in range(8):
                    qA(h)
                    if len(pend) > 1:
                        last_rot = qB()
                while pend:
                    last_rot = qB()
                cqn_free[b] = [last_rot]
                kqs = P.dma('sp', QM.rearrange("h r t -> r h t")[:, :, T], Qst[:], qst_sem, Q_done)
                Qst_free[0] = [kqs]

            def Ukv(tt):
                T = slice(tt * TT, (tt + 1) * TT)
                b = tt % 2
                k_ckv = LT[tt]['k_ckv']
                K_done = []
                for h in range(8):
                    kb, ps, pw = prU.get()
                    km = None
                    for j in range(2):
                        km = P.mm(ps[0:64, :], wukv[:, j, h * 128:h * 128 + 64], ckvn[b][:, j, :], start=(j == 0), stop=(j == 1),
                                  waits=([k_ckv] + W1 + pw) if j == 0 else (), sig=(j == 1))
                    ka = P.copy('act', Kst[0:64, h, :], ps[0:64, :], [km] + Kst_free[0])
                    prU.rel(kb, [ka])
                    K_done.append(ka)
                kks = P.dma('sp', KM.rearrange("h r t -> r h t")[0:64, :, T], Kst[:], kst_sem, K_done)
                Kst_free[0] = [kks]
                V_done = []
                last_v_mm = None
                for s_ in range(4):
                    kb, ps, pw = prU.get()
                    km = None
                    for j in range(2):
                        km = P.mm(ps[:].rearrange("p (h d) -> p h d", d=64), ckvn[b][:, j, s_ * 128:(s_ + 1) * 128], wv[:, j, :, 1, :],
                                  start=(j == 0), stop=(j == 1), waits=([k_ckv] + W1 + pw) if j == 0 else (), sig=(j == 1))
                    eng = 'act' if s_ % 2 == 0 else 'dve'
                    ka = P.copy(eng, Vst[:, :, s_, 0:64], ps[:].rearrange("p (h d) -> p h d", d=64), [km, k_vst1] + Vst_free[0])
                    prU.rel(kb, [ka])
                    V_done.append(ka)
                    last_v_mm = km
                ckvn_free[b] = [last_v_mm]
                kvs = P.dma('sp', VM.rearrange("h p (n e) -> p h n e", e=128)[:, :, tt * 4:(tt + 1) * 4, :], Vst[:], vst_sem, V_done)
                Vst_free[0] = [kvs]

            Lcq_front(0)
            Lcq_back(0)
            Lkv_front(0)
            Lkv_back(0)
            for tt in range(NT):
                if tt + 1 < NT:
                    Lcq_front(tt + 1)
                Uq(tt)
                if tt + 1 < NT:
                    Lcq_back(tt + 1)
                    Lkv_front(tt + 1)
                Ukv(tt)
                if tt + 1 < NT:
                    Lkv_back(tt + 1)
            P.barrier()

        if stop_after == 1:
            P.dead = True
        st_ma.close()
        ust_sems = [P.dsem("dust0"), P.dsem("dust1")]
        with ExitStack() as st2:
            kc1 = P.memset('pool', Cd[:], 1.0)
            ks0 = P.memset('pool', Sd[:], 0.0)
            tbl_state = gen_tables_a(st2, 1, 'd', BF16)
            Qh = [sbt(st2, f"Qh{i}", [96, S], BF16) for i in range(2)]
            Kh = [sbt(st2, f"Kh{i}", [96, S], BF16) for i in range(2)]
            Vh = [sbt(st2, f"Vh{i}", [128, 32, 128], BF16) for i in range(2)]
            wz = sbt(st2, "wz", [128, 8, 512], BF16)
            dw2 = P.dsem("dw2")
            k_wz = P.dma('pool', wz[:], w_in_v[:, :, O_ZM:O_ZM + 512], dw2)
            dwc = P.dsem("dwcast")
            for o_ in range(16):
                P.dma('pool', WGb[o_], w_in_v[:, :, O_GM + o_ * 128:O_GM + (o_ + 1) * 128], dwc)
            P.dma('pool', WPMb, w_pm, dwc)
            P.dma('pool', WPDb, w_pd, dwc)
            k_wcast = P.dma('pool', WOb, w_out, dwc)
            ptr = Ring([sbt(st2, f"pt{i}", [128, TT], BF16) for i in range(4)])
            szp = sbt(st2, "szp", [128, NT, TT])
            tz = [sbt(st2, f"tz{i}", [128, TT]) for i in range(2)]
            a1 = [sbt(st2, f"a1{i}", [128, TT]) for i in range(2)]
            rd = [sbt(st2, f"rd{i}", [128, TT]) for i in range(2)]
            ust = [sbt(st2, f"ust{i}", [64, TT], BF16) for i in range(2)]
            lsem = [P.dsem("dql0"), P.dsem("dql1")]
            sr = Ring(banks[0:3])
            por = Ring(banks[3:5])
            pzr = Ring(banks[5:7])
            ld_tok = [None, None]
            buf_free = [[], []]
            ep_free = [[], []]
            ust_free = [[], []]
            tz_free = [[], []]
            szp_tok = [None] * NT
            szp_free = [[] for _ in range(NT)]
            epi = 0
            zi = 0

            def load_head(h):
                b = h % 2
                P.dma('sp', Qh[b][:], QM[h], lsem[b], buf_free[b])
                P.dma('sp', Kh[b][:], KM[h], lsem[b], buf_free[b])
                ld_tok[b] = P.dma('sp', Vh[b][:].rearrange("p n e -> p (n e)"), VM[h], lsem[b], buf_free[b])

            load_head(0)
            for h in range(8):
                b = h % 2
                if h + 1 < 8:
                    load_head(h + 1)
                LD = [ld_tok[b]]
                items = []
                for qt in range(NT):
                    nk = 4 * (qt + 1)
                    for kt in range(nk):
                        items.append((qt, kt, nk))
                st = {}
                qstate = {}

                def emit_S(it):
                    nonlocal zi
                    qt, kt, nk = it
                    if kt == 0:
                        if h % 2 == 0:
                            kbz, pz, pwz = pzr.get()
                            kmz = None
                            for c in range(8):
                                kmz = P.mm(pz[:], wz[:, c, (h // 2) * 128:(h // 2 + 1) * 128], hT[:, c, qt * TT:(qt + 1) * TT],
                                           start=(c == 0), stop=(c == 7), waits=([k_wz] + pwz) if c == 0 else (), sig=(c == 7))
                            zz = zi % 2
                            zi += 1
                            kth = P.act(tz[zz][:], pz[:], AF.Tanh, scale=0.5, waits=[kmz] + tz_free[zz])
                            ksz = P.stt(szp[:, qt, :], tz[zz][:], 1.0, pz[:], ALU.add, ALU.mult, [kth] + szp_free[qt])
                            pzr.rel(kbz, [ksz, kth])
                            tz_free[zz] = [ksz]
                            szp_tok[qt] = ksz
                        kbo, po, pwo = por.get()
                        qstate[qt] = dict(kbo=kbo, po=po, pwo=pwo)
                    diag = kt >= 4 * qt
                    j = kt - 4 * qt
                    n0 = 128 * j if diag else 0
                    kb, ps, pw = sr.get()
                    q0 = qt * TT
                    kslc = Kh[b][0:96, kt * 128:(kt + 1) * 128]
                    if not diag:
                        km = P.mm(ps[:, 0:TT], kslc, Qh[b][0:96, q0:q0 + TT], waits=LD + pw, sig=True)
                    else:
                        P.mm(ps[:, n0:n0 + 128], kslc, Qh[b][0:96, q0 + n0:q0 + n0 + 128], start=True, stop=False, waits=LD + pw)
                        km = P.mm(ps[:, n0:n0 + 128], ident[:], maskab[:, 0:128], start=False, stop=True, waits=CST, sig=(j == 3))
                        if j < 3:
                            km = P.mm(ps[:, n0 + 128:TT], kslc, Qh[b][0:96, q0 + n0 + 128:q0 + TT], sig=True)
                    st[it] = dict(kb=kb, ps=ps, km=km, n0=n0)

                def emit_E(it):
                    s_ = st[it]
                    n0 = s_['n0']
                    ki, pt, pw = ptr.get()
                    ke = P.act(pt[:, n0:TT], s_['ps'][:, n0:TT], AF.Exp, scale=SC_MLA, waits=[s_['km']] + pw)
                    sr.rel(s_['kb'], [ke])
                    s_['ki'] = ki; s_['pt'] = pt; s_['ke'] = ke

                def emit_PV(it):
                    nonlocal epi
                    qt, kt, nk = it
                    s_ = st[it]
                    n0 = s_['n0']
                    q_ = qstate[qt]
                    po = q_['po']
                    last = (kt == nk - 1)
                    kpv = P.mm(po[:, n0:TT], Vh[b][:, kt, :], s_['pt'][:, n0:TT], start=(kt == 0), stop=last,
                               waits=[s_['ke']] + (q_['pwo'] if kt == 0 else []), sig=True)
                    ptr.rel(s_['ki'], [kpv])
                    del st[it]
                    if not last:
                        return
                    e = epi % 2
                    epi += 1
                    T = slice(qt * TT, (qt + 1) * TT)
                    hb = (h % 2) * 64
                    kr = P.recip(rd[e][0:64, :], po[64:128, :], [kpv] + ep_free[e])
                    ka1 = P.tt('dve', a1[e][hb:hb + 64, :], po[0:64, :], rd[e][0:64, :], ALU.mult, [kpv, kr] + ep_free[e])
                    por.rel(q_['kbo'], [ka1, kr])
                    ku = P.stt(ust[e][:], a1[e][hb:hb + 64, :], 0.5, szp[hb:hb + 64, qt, :], ALU.mult, ALU.mult, [ka1, szp_tok[qt]] + ust_free[e])
                    ep_free[e] = [ku]
                    if h % 2 == 1:
                        szp_free[qt] = [ku]
                    kus = P.dma('sp', UM[h * 64:(h + 1) * 64, T], ust[e][:], ust_sems[e], [ku])
                    ust_free[e] = [kus]
                    del qstate[qt]

                n_it = len(items)
                for i in range(n_it + 3):
                    if i < n_it:
                        emit_S(items[i])
                    if 0 <= i - 1 < n_it:
                        emit_E(items[i - 1])
                    if 0 <= i - 3 < n_it:
                        emit_PV(items[i - 3])
                buf_free[b] = [('pe', P.cnt['pe'])]
                if h == 0:
                    gen_tables_b(tbl_state, Cd, Sd, [0, 64], [kc1, ks0])
            P.barrier()
        if stop_after == 2:
            P.dead = True

        with ExitStack() as st3:
            wd = [sbt(st3, f"wd{i}", [128, 8, 384], BF16) for i in range(2)]
            wzd = [sbt(st3, f"wzd{i}", [128, 8, 128], BF16) for i in range(2)]
            wdsem = [P.dsem("dwd0"), P.dsem("dwd1")]
            wzsem = [P.dsem("dwz0"), P.dsem("dwz1")]
            Qd = sbt(st3, "Qd", [128, S], BF16)
            Kd = sbt(st3, "Kd", [128, S], BF16)
            Vd = sbt(st3, "Vd", [128, 32, 2, 128], BF16)
            acc = [sbt(st3, f"acc{i}", [128, S]) for i in range(2)]
            abr = Ring([sbt(st3, f"abf{i}", [128, TT], BF16) for i in range(3)])
            t1r = Ring([sbt(st3, f"t1d{i}", [128, TT]) for i in range(2)])
            t2r = Ring([sbt(st3, f"t2d{i}", [128, TT]) for i in range(2)])
            ptr = Ring([sbt(st3, f"ptd{i}", [128, 512], BF16) for i in range(4)])
            rd = [sbt(st3, f"rdd{i}", [128, TT]) for i in range(2)]
            tz = [sbt(st3, f"tzd{i}", [128, TT]) for i in range(2)]
            szd = [sbt(st3, f"szd{i}", [128, TT]) for i in range(2)]
            ustp = [sbt(st3, f"ustd{i}", [128, TT], BF16) for i in range(2)]
            k_vd1 = P.memset('pool', Vd[:], 1.0)
            pr = Ring(banks[0:4])
            pur = [Ring(banks[4:6]), Ring(banks[6:8])]
            mask512 = sbt(st3, "mask512", [128, 512], BF16)
            dmk = P.dsem("dmk")
            P.dma('pool', mask512[:, 0:256], m01_d, dmk)
            k_m512 = P.dma('pool', mask512[:, 256:512], m01_d, dmk)
            wd_free = [[], []]
            wzd_free = [[], []]
            QK_free = []
            Vd_free = []
            acc_free = [[], []]
            ust_free = [[], []]
            tz_free = [[], []]
            szd_free = [[], []]
            rd_free = [[], []]
            epi = 0
            combos = [(hp, g) for hp in range(4) for g in range(3)]
            wtok = {}

            def load_w(ci):
                hp, g = combos[ci]
                sl = ci % 2
                for s3 in range(3):
                    c0 = O_DIL + (s3 * 3 + g) * 512 + hp * 128
                    wtok[ci] = P.dma('pool', wd[sl][:, :, s3 * 128:(s3 + 1) * 128], w_in_v[:, :, c0:c0 + 128], wdsem[sl], wd_free[sl])

            def load_wz(hp):
                sl = hp % 2
                return P.dma('pool', wzd[sl][:], w_in_v[:, :, O_ZD + hp * 128:O_ZD + (hp + 1) * 128], wzsem[sl], wzd_free[sl])

            fin_pending = [None]
            epi_c = [0]

            def make_fin(hp, accw):
                WZ = [wz_tok[hp]]
                zsl = hp % 2

                def step(tt):
                    kmz = None
                    T = slice(tt * TT, (tt + 1) * TT)
                    kbz, pz, pwz = pr.get()
                    for c in range(8):
                        kmz = P.mm(pz[:], wzd[zsl][:, c, :], hT[:, c, T], start=(c == 0), stop=(c == 7),
                                   waits=(WZ + pwz) if c == 0 else (), sig=(c == 7))
                    e = epi_c[0] % 2
                    epi_c[0] += 1
                    kbe, pe_, pwe = pr.get()
                    ke = P.act(pe_[:], pz[:], AF.Exp, scale=-1.0, waits=[kmz] + pwe)
                    kw = None
                    for hh in range(2):
                        kw = P.stt(szd[e][hh * 64:(hh + 1) * 64, :], pe_[hh * 64:(hh + 1) * 64, :], 1.0, acc[hh][64:128, T],
                                   ALU.add, ALU.mult, [ke] + accw + szd_free[e])
                    pr.rel(kbe, [kw])
                    kl = P.act(rd[e][:], szd[e][:], AF.Ln, waits=[kw] + rd_free[e])
                    szd_free[e] = [kl]
                    kx = P.act(rd[e][:], rd[e][:], AF.Exp, scale=-1.0, waits=[kl])
                    i1, t1, w1 = t1r.get()
                    ka1 = None
                    for hh in range(2):
                        ka1 = P.tt('dve', t1[hh * 64:(hh + 1) * 64, :], acc[hh][0:64, T], pz[hh * 64:(hh + 1) * 64, :], ALU.mult,
                                   [kmz] + accw + w1)
                    pr.rel(kbz, [ke, ka1])
                    ku = P.tt('pool', ustp[e][:], t1[:], rd[e][:], ALU.mult, [ka1, kx] + ust_free[e])
                    t1r.rel(i1, [ku])
                    rd_free[e] = [ku]
                    kus = P.dma('sp', UM[512 + hp * 128:512 + (hp + 1) * 128, T], ustp[e][:], ust_sems[e], [ku])
                    ust_free[e] = [kus]
                    acc_free[0] = [ka1, kw]
                    acc_free[1] = [ka1, kw]
                    if tt == NT - 1:
                        wzd_free[zsl] = [kmz]
                return step

            load_w(0)
            wz_tok = {0: load_wz(0)}
            for ci, (hp, g) in enumerate(combos):
                sl = ci % 2
                if ci + 1 < len(combos):
                    load_w(ci + 1)
                WD = [wtok[ci]]
                d = (1, 4, 16)[g]
                L = S // d

                def tokslice(f0, count):
                    r = f0 // L
                    i0 = f0 % L
                    start = r + d * i0
                    return slice(start, start + d * (count - 1) + 1, d)

                pend = []
                lastB = [None]

                def stageA(which, tt):
                    T = slice(tt * TT, (tt + 1) * TT)
                    kb, ps, pw = pr.get()
                    km = None
                    for c in range(8):
                        km = P.mm(ps[:], wd[sl][:, c, which * 128:(which + 1) * 128], hT[:, c, T], start=(c == 0), stop=(c == 7),
                                  waits=(WD + pw) if c == 0 else (), sig=(c == 7))
                    ia, ab, wa = abr.get()
                    ka = P.act(ab[:], ps[:], AF.Copy, waits=[km] + wa)
                    pr.rel(kb, [ka])
                    pend.append((which, tt, ia, ab, ka))

                def stageB():
                    which, tt, ia, ab, ka = pend.pop(0)
                    dst = Qd if which == 0 else Kd
                    T = slice(tt * TT, (tt + 1) * TT)
                    kb2, pB, pw2 = pr.get()
                    kmr = P.mm(pB[:], rotm[:, 96:224], ab[:], waits=[ka] + CST + pw2, sig=True)
                    i1, t1, w1 = t1r.get()
                    i2, t2, w2 = t2r.get()
                    n_i = TT // d
                    nat = "p (i r) -> p i r"
                    t1d = t1[:].rearrange("p (r i) -> p i r", r=d)
                    t2d = t2[:].rearrange("p (r i) -> p i r", r=d)
                    k1 = P.tt('dve', t1d, ab[:].rearrange(nat, r=d), Cd[:, T].rearrange(nat, r=d), ALU.mult, [ka] + w1)
                    k2 = P.tt('dve', t2d, pB[:].rearrange(nat, r=d), Sd[:, T].rearrange(nat, r=d), ALU.mult, [kmr] + w2)
                    pr.rel(kb2, [k2])
                    abr.rel(ia, [kmr, k1])
                    dv = dst[:, :].rearrange("p (r i) -> p r i", r=d)[:, :, tt * n_i:(tt + 1) * n_i]
                    k3 = P.tt('pool', dv, t1[:].rearrange("p (r i) -> p r i", r=d),
                              t2[:].rearrange("p (r i) -> p r i", r=d), ALU.add, [k1, k2] + QK_free)
                    t1r.rel(i1, [k3])
                    t2r.rel(i2, [k3])
                    lastB[0] = k3

                for which in (0, 1):
                    for tt in range(NT):
                        stageA(which, tt)
                        if len(pend) > 1:
                            stageB()
                if stop_after == 2.2:
                    while pend:
                        stageB()
                    P.barrier()
                    P.dead = True
                V_ready = []
                km = None
                for nb4 in range(8):
                    kb, ps, pw = pr.get()
                    for q4 in range(4):
                        n = nb4 * 4 + q4
                        tsl = tokslice(n * 128, 128)
                        for c in range(8):
                            km = P.mm(ps[:, q4 * 128:(q4 + 1) * 128], hT[:, c, tsl], wd[sl][:, c, 256:384], start=(c == 0), stop=(c == 7),
                                      waits=(WD + pw) if (c == 0 and q4 == 0) else (), sig=(c == 7 and q4 == 3))
                    eng = 'act' if nb4 % 2 == 0 else 'dve'
                    ka = P.copy(eng, Vd[:, nb4 * 4:(nb4 + 1) * 4, :, 0:64], ps[:].rearrange("p (n h d) -> p n h d", n=4, h=2),
                                [km, k_vd1] + Vd_free)
                    pr.rel(kb, [ka])
                    V_ready.append(ka)
                    if nb4 == 0:
                        while pend:
                            stageB()
                    if fin_pending[0] is not None:
                        fin_pending[0](nb4)
                        if nb4 == 7:
                            fin_pending[0] = None
                wd_free[sl] = [km]
                if g == 0 and hp + 1 < 4:
                    wz_tok[hp + 1] = load_wz(hp + 1)
                QK_ready = [('pool', P.cnt['pool']), ('dve', P.cnt['dve'])]
                if stop_after == 2.3:
                    P.barrier()
                    P.dead = True
                nbp = L // 128
                pstate = [{}, {}]

                def get_pu(hh, n):
                    bk = n // 4
                    if bk not in pstate[hh]:
                        kbu, pu, pwu = pur[hh].get()
                        pstate[hh][bk] = dict(kb=kbu, pu=pu, pw=pwu, first=True)
                    return pstate[hh][bk]

                sts = {}

                def dS(n):
                    bb = n % nbp
                    N = 256 if bb + 1 < nbp else 128
                    tl = []
                    for hh in range(2):
                        p0 = hh * 64
                        kb, ps, pw = pr.get()
                        km_ = P.mm(ps[:, 0:N], Kd[p0:p0 + 64, n * 128:(n + 1) * 128], Qd[p0:p0 + 64, n * 128:n * 128 + N],
                                   waits=QK_ready + pw, sig=True)
                        tl.append((kb, ps, km_))
                    sts[n] = dict(tl=tl, N=N)

                def dE(n):
                    s_ = sts[n]
                    N = s_['N']
                    ki, pt, pw = ptr.get()
                    kes = []
                    for hh in range(2):
                        kb, ps, km_ = s_['tl'][hh]
                        ke = P.act(pt[:, hh * 256:hh * 256 + N], ps[:, 0:N], AF.Exp, scale=SC_DIL, waits=[km_] + pw)
                        pr.rel(kb, [ke])
                        kes.append(ke)
                    if N == 256:
                        kmk = P.tt('dve', pt[:, 0:512], pt[:, 0:512], mask512[:, 0:512], ALU.mult, kes + [k_m512])
                    else:
                        ptv = pt[:, :].rearrange("p (h c) -> p h c", h=2)[:, :, 0:128]
                        kmk = P.tt('dve', ptv, ptv, mask512[:, :].rearrange("p (h c) -> p h c", h=2)[:, :, 0:128], ALU.mult, kes + [k_m512])
                    s_['ki'] = ki; s_['pt'] = pt; s_['ke'] = kmk

                def dPV(n):
                    s_ = sts[n]
                    N = s_['N']
                    bb = n % nbp
                    kpv = None
                    for hh in range(2):
                        u = get_pu(hh, n)
                        c0 = (n % 4) * 128
                        w = [s_['ke']] + V_ready + (u['pw'] if u['first'] else [])
                        u['first'] = False
                        kpv = P.mm(u['pu'][:, c0:c0 + 128], Vd[:, n, hh, :], s_['pt'][:, hh * 256:hh * 256 + 128], start=(bb == 0), stop=True,
                                   waits=w, sig=True)
                        if N == 256:
                            u2 = get_pu(hh, n + 1)
                            c1 = ((n + 1) % 4) * 128
                            w2 = (u2['pw'] if u2['first'] else [])
                            u2['first'] = False
                            kpv = P.mm(u2['pu'][:, c1:c1 + 128], Vd[:, n, hh, :], s_['pt'][:, hh * 256 + 128:hh * 256 + 256], start=True, stop=False,
                                       waits=w2, sig=True)
                    ptr.rel(s_['ki'], [kpv])
                    del sts[n]
                    if n % 4 == 3:
                        bk = n // 4
                        f0 = bk * 512
                        r = f0 // L
                        i0 = f0 % L
                        for hh in range(2):
                            u = pstate[hh][bk]
                            accv = acc[hh][:, :].rearrange("p (i r) -> p r i", r=d)
                            if L >= 512:
                                dstv = accv[:, r, i0:i0 + 512]
                                srcv = u['pu'][:, :]
                            else:
                                nr = 512 // L
                                dstv = accv[:, r:r + nr, :]
                                srcv = u['pu'][:, :].rearrange("p (r i) -> p r i", r=nr)
                            if g == 0:
                                eng = 'act' if hh == 0 else 'dve'
                                ka_ = P.copy(eng, dstv, srcv, [kpv] + acc_free[hh])
                            else:
                                ka_ = P.tt('dve', dstv, dstv, srcv, ALU.add, [kpv] + acc_free[hh])
                            pur[hh].rel(u['kb'], [ka_])

                for i in range(32 + 3):
                    if i < 32:
                        dS(i)
                    if 0 <= i - 1 < 32:
                        dE(i - 1)
                    if 0 <= i - 3 < 32:
                        dPV(i - 3)
                if g == 0:
                    acc_free = [[], []]
                QK_free = [('pe', P.cnt['pe'])]
                Vd_free = [('pe', P.cnt['pe'])]
                if stop_after == 2.4 or (stop_after == 2.5 and ci == 1) or (stop_after == 2.6 and ci == 2):
                    P.barrier()
                    P.dead = True
                if g == 2:
                    fin_pending[0] = make_fin(hp, [('dve', P.cnt['dve']), ('act', P.cnt['act'])])
            if fin_pending[0] is not None:
                for tt_ in range(NT):
                    fin_pending[0](tt_)
                fin_pending[0] = None
            P.barrier()

        st_cd.close()
        if stop_after == 3:
            P.dead = True
        with ExitStack() as st4:
            wg = sbt(st4, "wg", [128, 16, 8, 128], BF16)
            wpm = sbt(st4, "wpm", [128, 4, 1024], BF16)
            wpd = sbt(st4, "wpd", [128, 4, 1024], BF16)
            wout = sbt(st4, "wout", [128, 8, 1024], BF16)
            gpost = sbt(st4, "gpost", [128, 1024])
            dw3 = [P.dsem(f"dw3_{i}") for i in range(4)]
            k_wg_oc = []
            for oc_ in range(8):
                sm_ = P.dsem(f"dwg{oc_}")
                P.dma('sp', wg[:, oc_, :, :], WGb[oc_], sm_, [k_wcast])
                k_wg_oc.append(P.dma('sp', wg[:, 8 + oc_, :, :], WGb[8 + oc_], sm_, [k_wcast]))
                if oc_ == 0:
                    k_wpm = P.dma('sp', wpm[:], WPMb.rearrange("(c p) n -> p c n", p=128), dw3[1], [k_wcast])
                    k_wpd = P.dma('sp', wpd[:], WPDb.rearrange("(c p) n -> p c n", p=128), dw3[1], [k_wcast])
                    k_wpm = k_wpd
            k_wout = P.dma('sp', wout[:], WOb.rearrange("(c p) n -> p c n", p=128), dw3[2], [k_wcast])
            k_gp = P.dma('sp', gpost[:], gpost_d.broadcast_to([128, 1024]), dw3[3])
            um = [sbt(st4, f"um{i}", [128, 4, TT], BF16) for i in range(2)]
            ud = [sbt(st4, f"ud{i}", [128, 4, TT], BF16) for i in range(2)]
            usem = [P.dsem("dul0"), P.dsem("dul1")]
            sg = [sbt(st4, f"sg{i}", [128, TT]) for i in range(2)]
            m1 = sbt(st4, "m1", [128, TT])
            m2 = sbt(st4, "m2", [128, TT])
            mg = [sbt(st4, f"mg{i}", [128, 8, TT], BF16) for i in range(2)]
            xt = [sbt(st4, f"xt{i}", [128, 1024]) for i in range(3)]
            o1 = [sbt(st4, f"o1_{i}", [128, 1024]) for i in range(2)]
            junk = sbt(st4, "junk", [128, 512])
            ssq = [sbt(st4, f"ssq{i}", [128, 4]) for i in range(2)]
            xsem = [P.dsem(f"dxt{i}") for i in range(3)]
            ysem = [P.dsem(f"dy{i}") for i in range(3)]
            pr = Ring(banks[0:4])
            por = Ring(banks[4:8])
            u_free = [[], []]
            u_tok = [None, None]
            sg_free = [[], []]
            mm_free = [[], []]
            mg_free = [[], []]
            mg_done = [None, None]
            xt_free = [[], [], []]
            o1_free = [[], []]
            ssq_free = [[], []]
            junk_free = [[]]
            out_toks = []
            xi = [0]
            last_umm = [None, None]
            last_omm = [None, None]

            def load_u(tt):
                ub = tt % 2
                T = slice(tt * TT, (tt + 1) * TT)
                P.dma('sp', um[ub][:], UM[0:512, T].rearrange("(c p) t -> p c t", p=128), usem[ub], u_free[ub])
                u_tok[ub] = P.dma('sp', ud[ub][:], UM[512:1024, T].rearrange("(c p) t -> p c t", p=128), usem[ub], u_free[ub])

            def G_oc(tt, oc):
                ub = tt % 2
                T = slice(tt * TT, (tt + 1) * TT)
                grp = []
                for (wt, off, src, nk, kw) in ((wg, 0, None, 8, k_wg_oc[oc]), (wg, 8, None, 8, k_wg_oc[oc]), (wpm, 0, um[ub], 4, k_wpm), (wpd, 0, ud[ub], 4, k_wpd)):
                    kb, ps, pw = pr.get()
                    km = None
                    for c in range(nk):
                        rhs = hT[:, c, T] if src is None else src[:, c, :]
                        lw = wt[:, off + oc, c, :] if src is None else wt[:, c, oc * 128:(oc + 1) * 128]
                        km = P.mm(ps[:], lw, rhs, start=(c == 0), stop=(c == nk - 1),
                                  waits=([kw, u_tok[ub]] + pw) if c == 0 else (), sig=(c == nk - 1))
                    grp.append((kb, ps, km))
                last_umm[ub] = grp[3][2]
                ks1 = P.act(sg[0][:], grp[0][1][:], AF.Sigmoid, waits=[grp[0][2]] + sg_free[0])
                pr.rel(grp[0][0], [ks1])
                ks2 = P.act(sg[1][:], grp[1][1][:], AF.Sigmoid, waits=[grp[1][2]] + sg_free[1])
                pr.rel(grp[1][0], [ks2])
                k1 = P.tt('dve', m1[:], sg[0][:], grp[2][1][:], ALU.mult, [ks1, grp[2][2]] + mm_free[0])
                pr.rel(grp[2][0], [k1])
                k2 = P.tt('dve', m2[:], sg[1][:], grp[3][1][:], ALU.mult, [ks2, grp[3][2]] + mm_free[1])
                pr.rel(grp[3][0], [k2])
                sg_free[0] = [k1]
                sg_free[1] = [k2]
                k3 = P.tt('pool', mg[ub][:, oc, :], m1[:], m2[:], ALU.add, [k1, k2] + (mg_free[ub] if oc == 0 else []))
                mm_free[0] = [k3]
                mm_free[1] = [k3]
                mg_done[ub] = k3
                if oc == 7:
                    u_free[ub] = [last_umm[ub]]

            def O_s(tt, s):
                ub = tt % 2
                r0 = tt * TT + s * 128
                xb = xi[0] % 3
                ob = xi[0] % 2
                xi[0] += 1
                kx = P.dma('sp', xt[xb][:], x[r0:r0 + 128, :], xsem[xb], xt_free[xb])
                halves = []
                for hf in range(2):
                    kb, po, pw = por.get()
                    km = None
                    for oc in range(8):
                        km = P.mm(po[:], mg[ub][:, oc, s * 128:(s + 1) * 128], wout[:, oc, hf * 512:(hf + 1) * 512], start=(oc == 0), stop=(oc == 7),
                                  waits=([k_wout, mg_done[ub]] + pw) if oc == 0 else (), sig=(oc == 7))
                    kq = P.act(junk[:], po[:], AF.Square, accum=ssq[ob][:, hf:hf + 1], waits=[km] + junk_free[0] + ssq_free[ob])
                    junk_free[0] = [kq]
                    halves.append((kb, po, km, kq))
                    last_omm[ub] = km
                ka = P.tt('dve', ssq[ob][:, 2:3], ssq[ob][:, 0:1], ssq[ob][:, 1:2], ALU.add, [halves[0][3], halves[1][3]])
                kr = P.act(ssq[ob][:, 3:4], ssq[ob][:, 2:3], AF.Sqrt, scale=1.0 / 1024, bias=epsb[:], waits=[ka])
                kr2 = P.recip(ssq[ob][:, 3:4], ssq[ob][:, 3:4], [kr])
                ko = None
                for hf in range(2):
                    H = slice(hf * 512, (hf + 1) * 512)
                    ko = P.stt(o1[ob][:, H], halves[hf][1][:], ssq[ob][:, 3:4], gpost[:, H], ALU.mult, ALU.mult,
                               waits=[kr2, k_gp] + o1_free[ob])
                    por.rel(halves[hf][0], [ko])
                ssq_free[ob] = [ko]
                kf = P.tt('pool', xt[xb][:], o1[ob][:], xt[xb][:], ALU.add, [ko, kx])
                o1_free[ob] = [kf]
                kys = P.dma('sp', y[r0:r0 + 128, :], xt[xb][:], ysem[xb], [kf])
                xt_free[xb] = [kys]
                out_toks.append(kys)
                if s == 3:
                    mg_free[ub] = [last_omm[ub]]

            load_u(0)
            for oc in range(8):
                G_oc(0, oc)
            for tt in range(NT):
                if tt + 1 < NT:
                    load_u(tt + 1)
                for s in range(4):
                    O_s(tt, s)
                    if tt + 1 < NT:
                        G_oc(tt + 1, 2 * s)
                        G_oc(tt + 1, 2 * s + 1)
            P.wait('sp', out_toks)
            P.barrier()
        P.emit()
    return nc


_NC_CACHE = {}


def _consts():
    p = np.arange(128)[:, None]
    f = np.arange(128)[None, :]
    ident = np.eye(128, dtype=np.float32)
    maskab = np.zeros((128, 256), np.float32)
    maskab[:, 0:128] = np.where(p > f, NEG, 0.0)
    maskab[:, 128:256] = np.where(f > p, NEG, 0.0)
    rot = np.zeros((128, 224), np.float32)
    for i in range(16):
        rot[80 + i, 64 + i] = -1.0
        rot[64 + i, 80 + i] = 1.0
    for hb in (0, 64):
        for i in range(8):
            rot[hb + 8 + i, 96 + hb + i] = -1.0
            rot[hb + i, 96 + hb + 8 + i] = 1.0
    theta = np.float32(500000.0)
    inv_mla = (theta ** (-np.arange(0, 32, 2, dtype=np.float32) / np.float32(32))).astype(np.float32)
    inv_dil = (theta ** (-np.arange(0, 16, 2, dtype=np.float32) / np.float32(16))).astype(np.float32)
    freqs = np.zeros((128, 2), np.float32)
    for q in range(4):
        for i in range(32):
            freqs[q * 32 + i, 0] = inv_mla[i % 16]
            freqs[q * 32 + i, 1] = inv_dil[i % 8] if i < 16 else 0.0
    mask01 = (maskab == 0.0).astype(np.float32)
    return dict(ident=ident, maskab=maskab, rotm=rot, freqs=freqs, mask01=mask01)


def make_in_maps(x, positions, pre_norm_g, w_in, q_norm_g, w_uq, kv_norm_g, w_ukv,
                 w_proj_mla, w_proj_dil, w_out, post_norm_g):
    cst = _consts()
    f32 = np.float32
    shared = dict(
        w_in=np.ascontiguousarray(w_in[0], f32), w_uq=np.ascontiguousarray(w_uq[0], f32),
        w_ukv=np.ascontiguousarray(w_ukv[0], f32), w_pm=np.ascontiguousarray(w_proj_mla[0], f32),
        w_pd=np.ascontiguousarray(w_proj_dil[0], f32), w_out=np.ascontiguousarray(w_out[0], f32),
        gpre=np.ascontiguousarray(np.asarray(pre_norm_g[0], f32).reshape(8, 128).T),
        gq=np.ascontiguousarray(np.asarray(q_norm_g[0], f32).reshape(3, 128).T),
        gkv=np.ascontiguousarray(np.asarray(kv_norm_g[0], f32).reshape(2, 128).T),
        gpost=np.ascontiguousarray(np.asarray(post_norm_g[0], f32).reshape(1, 1024)),
        **cst)
    maps = []
    for b in range(8):
        xb = np.asarray(x[b], f32)
        m = dict(shared)
        m["x"] = np.ascontiguousarray(xb)
        m["xT"] = np.ascontiguousarray(xb.T)
        m["pos"] = np.ascontiguousarray(np.asarray(positions[b], np.int32).reshape(1, S))
        maps.append(m)
    return maps


def kernel(x, positions, pre_norm_g, w_in, q_norm_g, w_uq, kv_norm_g, w_ukv,
           w_proj_mla, w_proj_dil, w_out, post_norm_g):
    if "nc" not in _NC_CACHE:
        _NC_CACHE["nc"] = build(False)
    nc = _NC_CACHE["nc"]
    maps = make_in_maps(x, positions, pre_norm_g, w_in, q_norm_g, w_uq, kv_norm_g, w_ukv,
                        w_proj_mla, w_proj_dil, w_out, post_norm_g)
    res = run_bass_kernel_spmd(nc, maps, core_ids=list(range(8)))
    out = np.stack([np.asarray(r["y"], np.float32) for r in res.results], axis=0)
    return out
```

```python
import math
import numpy as np
import concourse.bass as bass
import concourse.mybir as mybir
from concourse.bass_utils import run_bass_kernel_spmd
from contextlib import ExitStack

F32 = mybir.dt.float32
BF16 = mybir.dt.bfloat16
I32 = mybir.dt.int32
AF = mybir.ActivationFunctionType
ALU = mybir.AluOpType

S = 4096
TT = 512
NT = S // TT
EPS = 1e-6
NEG = -30000.0
C1 = 6.28125
C2 = 2.0 * math.pi - 6.28125
O_CQ, O_CKV, O_KR, O_DIL, O_ZM, O_ZD, O_GM, O_GD = 0, 384, 640, 672, 5280, 5792, 6304, 7328
SC_MLA = 96.0 ** -0.5
SC_DIL = 64.0 ** -0.5


class Prog:
    ENG = ('pe', 'act', 'dve', 'pool', 'sp')

    def __init__(self, nc, es):
        self.nc = nc
        self.es = es
        self.streams = {e: [] for e in self.ENG}
        self.sem = {}
        self.cnt = {}
        self.dsems = []
        self.dead = False
        for e in self.ENG[:4]:
            self._newsem(e)

    def _newsem(self, name):
        self.sem[name] = self.es.enter_context(self.nc.semaphore(name))
        self.cnt[name] = 0
        return name

    def dsem(self, name):
        self.dsems.append(name)
        return self._newsem(name)

    def op(self, eng, fn, waits=(), sig=True):
        if self.dead:
            return (eng, self.cnt[eng])
        tok = None
        if sig:
            self.cnt[eng] += 1
            tok = (eng, self.cnt[eng])
        self.streams[eng].append((fn, [w for w in waits if w is not None], (eng, 1) if sig else None))
        return tok

    def dma(self, q, out, in_, sem, waits=()):
        if self.dead:
            return (sem, self.cnt[sem])
        self.cnt[sem] += 16
        tok = (sem, self.cnt[sem])
        self.streams[q].append((lambda e: e.dma_start(out=out, in_=in_), [w for w in waits if w is not None], (sem, 16)))
        return tok

    def wait(self, eng, waits):
        if self.dead:
            return
        self.streams[eng].append((None, [w for w in waits if w is not None], None))

    def barrier(self):
        toks = [(s, self.cnt[s]) for s in list(self.ENG[:4]) + self.dsems if self.cnt[s] > 0]
        for e in self.ENG:
            self.wait(e, toks)

    def mm(self, out, lhsT, rhs, start=True, stop=True, waits=(), sig=False):
        return self.op('pe', lambda e: e.matmul(out, lhsT=lhsT, rhs=rhs, start=start, stop=stop), waits, sig)

    def act(self, out, in_, func, scale=None, bias=None, accum=None, waits=(), sig=True):
        kw = {}
        if scale is not None:
            kw['scale'] = scale
        if bias is not None:
            kw['bias'] = bias
        if accum is not None:
            kw['accum_out'] = accum
        return self.op('act', lambda e: e.activation(out=out, in_=in_, func=func, **kw), waits, sig)

    def tt(self, eng, out, in0, in1, op, waits=(), sig=True):
        return self.op(eng, lambda e: e.tensor_tensor(out=out, in0=in0, in1=in1, op=op), waits, sig)

    def ts(self, eng, out, in0, s1, op0, s2=None, op1=None, waits=(), sig=True):
        if op1 is None:
            return self.op(eng, lambda e: e.tensor_scalar(out=out, in0=in0, scalar1=s1, scalar2=None, op0=op0), waits, sig)
        return self.op(eng, lambda e: e.tensor_scalar(out=out, in0=in0, scalar1=s1, scalar2=s2, op0=op0, op1=op1), waits, sig)

    def stt(self, out, in0, scalar, in1, op0, op1, waits=(), sig=True):
        return self.op('dve', lambda e: e.scalar_tensor_tensor(out=out, in0=in0, scalar=scalar, in1=in1, op0=op0, op1=op1), waits, sig)

    def copy(self, eng, out, in_, waits=(), sig=True):
        if eng == 'act':
            return self.act(out, in_, AF.Copy, waits=waits, sig=sig)
        return self.op(eng, lambda e: e.tensor_copy(out=out, in_=in_), waits, sig)

    def recip(self, out, in_, waits=(), sig=True):
        return self.op('dve', lambda e: e.reciprocal(out=out, in_=in_), waits, sig)

    def memset(self, eng, ap, val, waits=(), sig=True):
        return self.op(eng, lambda e: e.memset(ap, val), waits, sig)

    def check(self):
        pos = {e: 0 for e in self.ENG}
        val = {s: 0 for s in self.sem}
        progress = True
        while progress:
            progress = False
            for e in self.ENG:
                st = self.streams[e]
                while pos[e] < len(st):
                    fn, waits, inc = st[pos[e]]
                    if all(val[s] >= v for (s, v) in waits):
                        if inc is not None:
                            val[inc[0]] += inc[1]
                        pos[e] += 1
                        progress = True
                    else:
                        break
        bad = {e: (pos[e], len(self.streams[e])) for e in self.ENG if pos[e] < len(self.streams[e])}
        if bad:
            msg = []
            for e, (p, n) in bad.items():
                fn, waits, inc = self.streams[e][p]
                msg.append("%s stuck at %d/%d waits=%s have=%s" % (e, p, n, [w for w in waits if val[w[0]] < w[1]], [(w[0], val[w[0]]) for w in waits if val[w[0]] < w[1]]))
            raise RuntimeError("DEADLOCK: " + "; ".join(msg))

    def emit(self):
        self.check()
        nc = self.nc
        engobj = {'pe': 'tensor', 'act': 'scalar', 'dve': 'vector', 'pool': 'gpsimd', 'sp': 'sync'}
        with nc.Block() as block:
            for en in self.ENG:
                stream = self.streams[en]

                def body(engine, stream=stream):
                    waited = {}
                    for fn, waits, inc in stream:
                        for (s, v) in waits:
                            if waited.get(s, 0) >= v:
                                continue
                            engine.wait_ge(self.sem[s], v)
                            waited[s] = v
                        if fn is not None:
                            ins = fn(engine)
                            if inc is not None:
                                ins.then_inc(self.sem[inc[0]], inc[1])
                getattr(block, engobj[en])(body)


class Ring:
    def __init__(self, items):
        self.items = list(items)
        self.free = [[] for _ in self.items]
        self.busy = [False for _ in self.items]
        self.i = 0

    def get(self):
        k = self.i % len(self.items)
        self.i += 1
        assert not self.busy[k], "ring slot re-acquired before its release was emitted"
        self.busy[k] = True
        w = self.free[k]
        self.free[k] = []
        return k, self.items[k], w

    def rel(self, k, toks):
        self.free[k] = [t for t in toks if t is not None]
        self.busy[k] = False


class _Stop(Exception):
    pass


def build(debug=False, stop_after=99):
    nc = bass.Bass("TRN2", target_bir_lowering=False)

    def din(name, shape, dt=F32):
        return nc.dram_tensor(name, shape, dt, kind="ExternalInput").ap()

    xT = din("xT", [1024, S]); x = din("x", [S, 1024]); pos = din("pos", [1, S], I32)
    w_in = din("w_in", [1024, 8352]); w_uq = din("w_uq", [384, 768]); w_ukv = din("w_ukv", [256, 1024])
    w_pm = din("w_pm", [512, 1024]); w_pd = din("w_pd", [512, 1024]); w_out = din("w_out", [1024, 1024])
    ident_d = din("ident", [128, 128]); mask_d = din("maskab", [128, 256]); rot_d = din("rotm", [128, 224])
    freq_d = din("freqs", [128, 2]); gpre_d = din("gpre", [128, 8]); gq_d = din("gq", [128, 3])
    gkv_d = din("gkv", [128, 2]); gpost_d = din("gpost", [1, 1024])
    m01_d = din("mask01", [128, 256])
    y = nc.dram_tensor("y", [S, 1024], F32, kind="ExternalOutput").ap()
    sk = "ExternalOutput" if debug else "Internal"
    QM = nc.dram_tensor("QM", [8, 96, S], BF16, kind=sk).ap()
    KM = nc.dram_tensor("KM", [8, 96, S], BF16, kind=sk).ap()
    VM = nc.dram_tensor("VM", [8, 128, 32 * 128], BF16, kind=sk).ap()
    UM = nc.dram_tensor("UM", [1024, S], BF16, kind=sk).ap()
    WGb = nc.dram_tensor("WGb", [16, 128, 8, 128], BF16, kind="Internal").ap()
    WPMb = nc.dram_tensor("WPMb", [512, 1024], BF16, kind="Internal").ap()
    WPDb = nc.dram_tensor("WPDb", [512, 1024], BF16, kind="Internal").ap()
    WOb = nc.dram_tensor("WOb", [1024, 1024], BF16, kind="Internal").ap()

    w_in_v = w_in.rearrange("(c p) n -> p c n", p=128)
    xT_v = xT.rearrange("(c p) t -> p c t", p=128)

    with ExitStack() as es:
        P = Prog(nc, es)

        uid = [0]

        def sbt(stack, name, shape, dt=F32):
            uid[0] += 1
            return stack.enter_context(nc.sbuf_tensor('s%d_%s' % (uid[0], name), shape, dt))

        banks = [es.enter_context(nc.psum_tensor(f"ps{i}", [128, 512], F32)) for i in range(8)]

        hT = sbt(es, "hT", [128, 8, S], BF16)
        ident = sbt(es, "ident", [128, 128], BF16)
        maskab = sbt(es, "maskab", [128, 256], BF16)
        rotm = sbt(es, "rotm", [128, 224], BF16)
        ones = sbt(es, "ones", [128, 128], F32)
        freqs = sbt(es, "freqs", [128, 2], F32)
        gpre = sbt(es, "gpre", [128, 8], F32)
        gq = sbt(es, "gq", [128, 3], F32)
        gkv = sbt(es, "gkv", [128, 2], F32)
        epsb = sbt(es, "epsb", [128, 1], F32)
        hpib = sbt(es, "hpib", [128, 1], F32)

        dc = P.dsem("dcst")
        dcp = P.dsem("dcstp")
        P.dma('pool', ident[:], ident_d, dcp)
        P.dma('pool', maskab[:], mask_d, dcp)
        k_cstp = P.dma('pool', rotm[:], rot_d, dcp)
        P.dma('sp', freqs[:], freq_d, dc)
        P.dma('sp', gpre[:], gpre_d, dc)
        P.dma('sp', gq[:], gq_d, dc)
        k_cst = P.dma('sp', gkv[:], gkv_d, dc)
        k_ones = P.memset('dve', ones[:], 1.0)
        k_eps = P.memset('dve', epsb[:], EPS)
        k_hpi = P.memset('dve', hpib[:], math.pi / 2)
        CST = [k_cst, k_cstp, k_ones, k_eps, k_hpi]

        def gen_tables_a(stack, fcol, tag, pk_dt):
            CH = 1024
            posi = sbt(stack, "tb_posi", [128, CH], I32)
            ang = sbt(stack, "tb_ang", [128, CH])
            xx = sbt(stack, "tb_x", [128, CH])
            ki = sbt(stack, "tb_ki", [128, CH], I32)
            kf = sbt(stack, "tb_kf", [128, CH])
            gg = [sbt(stack, "tb_g0", [128, CH]), sbt(stack, "tb_g1", [128, CH])]
            pk = [sbt(stack, "tb_pk0", [128, CH], pk_dt), sbt(stack, "tb_pk1", [128, CH], pk_dt)]
            dsm = P.dsem("dtb" + tag)
            kd = None
            for q in range(4):
                kd = P.dma('sp', posi[q * 32:(q + 1) * 32, :], pos[:, q * CH:(q + 1) * CH].broadcast_to([32, CH]), dsm)
            k0 = P.copy('dve', xx[:], posi[:], [kd])
            k1 = P.ts('dve', ang[:], xx[:], freqs[:, fcol:fcol + 1], ALU.mult, waits=[k0] + CST)
            last = [k1]
            khs = []
            for which in (0, 1):
                ka = P.ts('dve', xx[:], ang[:], 1.0 / (2 * math.pi), ALU.mult, 0.25 * which, ALU.add, waits=last)
                kb = P.copy('dve', ki[:], xx[:], [ka])
                kc = P.copy('dve', kf[:], ki[:], [kb])
                kdd = P.tt('dve', xx[:], xx[:], kf[:], ALU.subtract, [kc])
                ke = P.ts('dve', gg[which][:], xx[:], 0.5, ALU.is_gt, waits=[kdd])
                kff = P.tt('dve', kf[:], kf[:], gg[which][:], ALU.add, [ke])
                kg = P.stt(xx[:], kf[:], -C1, ang[:], ALU.mult, ALU.add, [kff])
                kh0 = P.stt(gg[which][:], kf[:], -C2, xx[:], ALU.mult, ALU.add, [kg])
                lim = 3.14159
                lo, hi = (-lim, lim) if which == 0 else (-lim - math.pi / 2, lim - math.pi / 2)
                kh = P.ts('dve', gg[which][:], gg[which][:], lo, ALU.max, hi, ALU.min, waits=[kh0])
                last = [kh]
                khs.append(kh)
            return dict(gg=gg, pk=pk, khs=khs, CH=CH)

        def gen_tables_b(stt_, Ct, St, dst_groups, extra=()):
            CH = stt_['CH']
            gg, pk, khs = stt_['gg'], stt_['pk'], stt_['khs']
            outs = [P.act(pk[0][:], gg[0][:], AF.Sin, waits=[khs[0]]),
                    P.act(pk[1][:], gg[1][:], AF.Sin, bias=hpib[:], waits=[khs[1]] + CST)]
            done = []
            for which, dst in ((0, St), (1, Ct)):
                for g0 in dst_groups:
                    for q in range(4):
                        done.append(P.copy('pool', dst[g0:g0 + 32, q * CH:(q + 1) * CH], pk[which][q * 32:(q + 1) * 32, :], [outs[which]] + list(extra)))
            return done

        st_cd = es.enter_context(ExitStack())
        Cd = sbt(st_cd, "Cd", [128, S], BF16)
        Sd = sbt(st_cd, "Sd", [128, S], BF16)
        st_ma = es.enter_context(ExitStack())
        Cm = sbt(st_ma, "Cm", [128, S], F32)
        Sm = sbt(st_ma, "Sm", [128, S], F32)
        hT_tok = [None] * NT
        with ExitStack() as st0:
            xs = [sbt(st0, f"xs{i}", [128, 8, TT]) for i in range(2)]
            sq = [sbt(st0, f"sq{i}", [128, TT]) for i in range(2)]
            Rb = [sbt(st0, f"Rb{i}", [128, TT]) for i in range(2)]
            xsem = [[P.dsem(f"dxs{i}_{c}") for c in range(8)] for i in range(2)]
            xs_free = [[], []]
            sq_free = [[], []]
            Rb_free = [[], []]
            pr = Ring(banks[0:2])
            gen_tables_b(gen_tables_a(st0, 0, 'm', F32), Cm, Sm, [64])
            for tt in range(NT):
                T = slice(tt * TT, (tt + 1) * TT)
                b = tt % 2
                kxc = [P.dma('sp', xs[b][:, c, :], xT_v[:, c, T], xsem[b][c], xs_free[b]) for c in range(8)]
                kx = kxc[7]
                kb, ps, pw = pr.get()
                kmm = None
                for c in range(8):
                    ks = P.act(sq[c % 2][:], xs[b][:, c, :], AF.Square, waits=[kxc[c]] + sq_free[c % 2])
                    kmm = P.mm(ps[:], ones[:], sq[c % 2][:], start=(c == 0), stop=(c == 7),
                               waits=[ks, k_ones] + (pw if c == 0 else []), sig=True)
                    sq_free[c % 2] = [kmm]
                kr = P.act(Rb[b][:], ps[:], AF.Ln, scale=1.0 / 1024, bias=epsb[:], waits=[kmm, k_eps] + Rb_free[b])
                pr.rel(kb, [kr])
                kr2 = P.act(Rb[b][:], Rb[b][:], AF.Exp, scale=-0.5, waits=[kr])
                kl = None
                for c in range(8):
                    kl = P.stt(hT[:, c, T], xs[b][:, c, :], gpre[:, c:c + 1], Rb[b][:], ALU.mult, ALU.mult,
                               waits=[kr2, kxc[c]] + CST)
                xs_free[b] = [kl]
                Rb_free[b] = [kl]
                hT_tok[tt] = kl
            P.barrier()
        if stop_after == 0:
            P.dead = True

        qst_sem = P.dsem("dqst")
        with ExitStack() as st1:
            wA = sbt(st1, "wA", [128, 8, 672], BF16)
            wkr = sbt(st1, "wkr", [128, 8, 96], BF16)
            wuq = sbt(st1, "wuq", [128, 3, 768], BF16)
            wukv = sbt(st1, "wukv", [128, 2, 1024], BF16)
            dw1 = P.dsem("dw1")
            kz = P.memset('pool', wkr[:], 0.0)
            P.dma('pool', wA[:], w_in_v[:, :, 0:672], dw1)
            P.dma('pool', wkr[:, :, 64:96], w_in_v[:, :, O_KR:O_KR + 32], dw1, [kz])
            P.dma('pool', wuq[:], w_uq.rearrange("(c p) n -> p c n", p=128), dw1)
            k_w1 = P.dma('pool', wukv[:], w_ukv.rearrange("(c p) n -> p c n", p=128), dw1)
            W1 = [k_w1]
            cqn = [sbt(st1, f"cqn{i}", [128, 3, TT], BF16) for i in range(2)]
            ckvn = [sbt(st1, f"ckvn{i}", [128, 2, TT], BF16) for i in range(2)]
            sq = [sbt(st1, f"sq1_{i}", [128, TT]) for i in range(2)]
            Rq = [sbt(st1, f"Rq{i}", [128, TT]) for i in range(2)]
            krb = [sbt(st1, f"krb{i}", [96, TT], BF16) for i in range(2)]
            kro = [sbt(st1, f"kro{i}", [96, TT], BF16) for i in range(2)]
            Qst = sbt(st1, "Qst", [96, 8, TT], BF16)
            Kst = sbt(st1, "Kst", [64, 8, TT], BF16)
            Vst = sbt(st1, "Vst", [128, 8, 4, 128], BF16)
            t1r = Ring([sbt(st1, f"t1_{i}", [128, TT]) for i in range(3)])
            t2r = Ring([sbt(st1, f"t2_{i}", [128, TT]) for i in range(3)])
            k_krb0 = [P.memset('pool', krb[i][:], 0.0) for i in range(2)]
            k_vst1 = P.memset('pool', Vst[:], 1.0)
            krsem = [P.dsem("dkr0"), P.dsem("dkr1")]
            kst_sem = P.dsem("dkst")
            vst_sem = P.dsem("dvst")
            pr = Ring(banks)
            sq_free = [[], []]
            cqn_free = [[], []]
            ckvn_free = [[], []]
            Rq_free = [[], []]
            Rqi = [0]
            krb_free = [[], []]
            kro_free = [[], []]
            Qst_free = [[]]
            Kst_free = [[]]
            Vst_free = [[]]
            rot_mla = rotm[0:96, 0:96]
            LT = {}
            wv = wukv[:, :, :].rearrange("p c (h two d) -> p c h two d", two=2, d=64)

            prL = Ring(banks[0:6])
            prU = Ring(banks[6:8])

            def norm_front(groups, idx, dim):
                kbs, pss, pws = prL.get()
                kmm = None
                for n_, j in enumerate(idx):
                    ks = P.act(sq[n_ % 2][:], groups[j][1][:], AF.Square, waits=[groups[j][2]] + sq_free[n_ % 2])
                    kmm = P.mm(pss[:], ones[:], sq[n_ % 2][:], start=(n_ == 0), stop=(n_ == len(idx) - 1),
                               waits=[ks, k_ones] + (pws if n_ == 0 else []), sig=True)
                    sq_free[n_ % 2] = [kmm]
                ri = Rqi[0] % 2
                Rqi[0] += 1
                kr = P.act(Rq[ri][:], pss[:], AF.Ln, scale=1.0 / dim, bias=epsb[:], waits=[kmm] + Rq_free[ri])
                prL.rel(kbs, [kr])
                kr2 = P.act(Rq[ri][:], Rq[ri][:], AF.Exp, scale=-0.5, waits=[kr])
                return ri, kr2

            def norm_back(groups, idx, dst, dst_free, gcol, ri, kr2):
                kl = None
                for n_, j in enumerate(idx):
                    kl = P.stt(dst[:, n_, :], groups[j][1][:], gcol[:, n_:n_ + 1], Rq[ri][:], ALU.mult, ALU.mult,
                               waits=[kr2, groups[j][2]] + dst_free + CST)
                    prL.rel(groups[j][0], [kl])
                Rq_free[ri] = [kl]
                return kl

            def main_groups(tt, cols):
                T = slice(tt * TT, (tt + 1) * TT)
                hw = [hT_tok[tt]] + W1
                groups = []
                for j in cols:
                    kb, ps, pw = prL.get()
                    km = None
                    for c in range(8):
                        km = P.mm(ps[:], wA[:, c, j * 128:(j + 1) * 128], hT[:, c, T],
                                  start=(c == 0), stop=(c == 7), waits=(hw + pw) if c == 0 else (), sig=(c == 7))
                    groups.append((kb, ps, km))
                return groups

            def Lcq_front(tt):
                groups = main_groups(tt, range(3))
                ri, kr2 = norm_front(groups, [0, 1, 2], 384.0)
                LT[tt] = dict(cq=(groups, ri, kr2))

            def Lcq_back(tt):
                b = tt % 2
                groups, ri, kr2 = LT[tt]['cq']
                LT[tt]['k_cq'] = norm_back(groups, [0, 1, 2], cqn[b], cqn_free[b], gq, ri, kr2)

            def Lkv_front(tt):
                T = slice(tt * TT, (tt + 1) * TT)
                b = tt % 2
                hw = [hT_tok[tt]] + W1
                groups = main_groups(tt, range(3, 5))
                kbk, psk, pwk = prL.get()
                kmk = None
                for c in range(8):
                    kmk = P.mm(psk[0:96, :], wkr[:, c, :], hT[:, c, T], start=(c == 0), stop=(c == 7),
                               waits=(hw + pwk) if c == 0 else (), sig=(c == 7))
                ri, kr2 = norm_front(groups, [0, 1], 256.0)
                ka = P.act(krb[b][64:96, :], psk[64:96, :], AF.Copy, waits=[kmk, k_krb0[b]] + krb_free[b])
                prL.rel(kbk, [ka])
                kb2, pB, pw2 = prL.get()
                kmr = P.mm(pB[0:96, :], rot_mla, krb[b][0:96, :], waits=[ka] + CST + pw2, sig=True)
                LT[tt]['kv'] = (groups, ri, kr2, ka, kb2, pB, kmr)

            def Lkv_back(tt):
                T = slice(tt * TT, (tt + 1) * TT)
                b = tt % 2
                groups, ri, kr2, ka, kb2, pB, kmr = LT[tt]['kv']
                LT[tt]['k_ckv'] = norm_back(groups, [0, 1], ckvn[b], ckvn_free[b], gkv, ri, kr2)
                i1, t1, w1 = t1r.get()
                i2, t2, w2 = t2r.get()
                k1 = P.tt('pool', t1[64:96, :], krb[b][64:96, :], Cm[64:96, T], ALU.mult, [ka] + w1)
                k2 = P.tt('dve', t2[64:96, :], pB[64:96, :], Sm[64:96, T], ALU.mult, [kmr] + w2)
                prL.rel(kb2, [k2])
                krb_free[b] = [kmr, k1]
                k3 = P.tt('dve', kro[b][64:96, :], t1[64:96, :], t2[64:96, :], ALU.add, [k1, k2] + kro_free[b])
                t1r.rel(i1, [k3])
                t2r.rel(i2, [k3])
                kd = None
                for h in range(8):
                    kd = P.dma('sp', KM[h, 64:96, T], kro[b][64:96, :], krsem[b], [k3])
                kro_free[b] = [kd]

            def Uq(tt):
                T = slice(tt * TT, (tt + 1) * TT)
                b = tt % 2
                k_cq = LT[tt]['k_cq']
                pend = []
                Q_done = []

                def qA(h):
                    kb, ps, pw = prU.get()
                    km = None
                    for j in range(3):
                        km = P.mm(ps[0:96, :], wuq[:, j, h * 96:(h + 1) * 96], cqn[b][:, j, :], start=(j == 0), stop=(j == 2),
                                  waits=([k_cq] + W1 + pw) if j == 0 else (), sig=(j == 2))
                    ka = P.act(Qst[0:96, h, :], ps[0:96, :], AF.Copy, waits=[km] + Qst_free[0])
                    prU.rel(kb, [ka])
                    pend.append((h, ka))

                def qB():
                    h, ka = pend.pop(0)
                    kb2, pB, pw2 = prU.get()
                    kmr = P.mm(pB[0:96, :], rot_mla, Qst[0:96, h, :], waits=[ka] + CST + pw2, sig=True)
                    i1, t1, w1 = t1r.get()
                    i2, t2, w2 = t2r.get()
                    k1 = P.tt('pool', t1[64:96, :], Qst[64:96, h, :], Cm[64:96, T], ALU.mult, [ka] + w1)
                    k2 = P.tt('dve', t2[64:96, :], pB[64:96, :], Sm[64:96, T], ALU.mult, [kmr] + w2)
                    prU.rel(kb2, [k2])
                    k3 = P.tt('dve', Qst[64:96, h, :], t1[64:96, :], t2[64:96, :], ALU.add, [k1, k2, kmr])
                    t1r.rel(i1, [k3])
                    t2r.rel(i2, [k3])
                    Q_done.append(k3)
                    return kmr

                last_rot = None
                for h in range(8):
                    qA(h)
                    if len(pend) > 1:
                        last_rot = qB()
                while pend:
                    last_rot = qB()
                cqn_free[b] = [last_rot]
                kqs = P.dma('sp', QM.rearrange("h r t -> r h t")[:, :, T], Qst[:], qst_sem, Q_done)
                Qst_free[0] = [kqs]

            def Ukv(tt):
                T = slice(tt * TT, (tt + 1) * TT)
                b = tt % 2
                k_ckv = LT[tt]['k_ckv']
                K_done = []
                for h in range(8):
                    kb, ps, pw = prU.get()
                    km = None
                    for j in range(2):
                        km = P.mm(ps[0:64, :], wukv[:, j, h * 128:h * 128 + 64], ckvn[b][:, j, :], start=(j == 0), stop=(j == 1),
                                  waits=([k_ckv] + W1 + pw) if j == 0 else (), sig=(j == 1))
                    ka = P.copy('act', Kst[0:64, h, :], ps[0:64, :], [km] + Kst_free[0])
                    prU.rel(kb, [ka])
                    K_done.append(ka)
                kks = P.dma('sp', KM.rearrange("h r t -> r h t")[0:64, :, T], Kst[:], kst_sem, K_done)
                Kst_free[0] = [kks]
                V_done = []
                last_v_mm = None
                for s_ in range(4):
                    kb, ps, pw = prU.get()
                    km = None
                    for j in range(2):
                        km = P.mm(ps[:].rearrange("p (h d) -> p h d", d=64), ckvn[b][:, j, s_ * 128:(s_ + 1) * 128], wv[:, j, :, 1, :],
                                  start=(j == 0), stop=(j == 1), waits=([k_ckv] + W1 + pw) if j == 0 else (), sig=(j == 1))
                    eng = 'act' if s_ % 2 == 0 else 'dve'
                    ka = P.copy(eng, Vst[:, :, s_, 0:64], ps[:].rearrange("p (h d) -> p h d", d=64), [km, k_vst1] + Vst_free[0])
                    prU.rel(kb, [ka])
                    V_done.append(ka)
                    last_v_mm = km
                ckvn_free[b] = [last_v_mm]
                kvs = P.dma('sp', VM.rearrange("h p (n e) -> p h n e", e=128)[:, :, tt * 4:(tt + 1) * 4, :], Vst[:], vst_sem, V_done)
                Vst_free[0] = [kvs]

            Lcq_front(0)
            Lcq_back(0)
            Lkv_front(0)
            Lkv_back(0)
            for tt in range(NT):
                if tt + 1 < NT:
                    Lcq_front(tt + 1)
                Uq(tt)
                if tt + 1 < NT:
                    Lcq_back(tt + 1)
                    Lkv_front(tt + 1)
                Ukv(tt)
                if tt + 1 < NT:
                    Lkv_back(tt + 1)
            P.barrier()

        if stop_after == 1:
            P.dead = True
        st_ma.close()
        ust_sems = [P.dsem("dust0"), P.dsem("dust1")]
        with ExitStack() as st2:
            kc1 = P.memset('pool', Cd[:], 1.0)
            ks0 = P.memset('pool', Sd[:], 0.0)
            tbl_state = gen_tables_a(st2, 1, 'd', BF16)
            Qh = [sbt(st2, f"Qh{i}", [96, S], BF16) for i in range(2)]
            Kh = [sbt(st2, f"Kh{i}", [96, S], BF16) for i in range(2)]
            Vh = [sbt(st2, f"Vh{i}", [128, 32, 128], BF16) for i in range(2)]
            wz = sbt(st2, "wz", [128, 8, 512], BF16)
            dw2 = P.dsem("dw2")
            k_wz = P.dma('pool', wz[:], w_in_v[:, :, O_ZM:O_ZM + 512], dw2)
            dwc = P.dsem("dwcast")
            for o_ in range(16):
                P.dma('pool', WGb[o_], w_in_v[:, :, O_GM + o_ * 128:O_GM + (o_ + 1) * 128], dwc)
            P.dma('pool', WPMb, w_pm, dwc)
            P.dma('pool', WPDb, w_pd, dwc)
            k_wcast = P.dma('pool', WOb, w_out, dwc)
            ptr = Ring([sbt(st2, f"pt{i}", [128, TT], BF16) for i in range(5)])
            szp = sbt(st2, "szp", [128, NT, TT])
            tz = [sbt(st2, f"tz{i}", [128, TT]) for i in range(2)]
            a1 = [sbt(st2, f"a1{i}", [128, TT]) for i in range(2)]
            rd = [sbt(st2, f"rd{i}", [128, TT]) for i in range(2)]
            ust = [sbt(st2, f"ust{i}", [64, TT], BF16) for i in range(2)]
            lsem = [P.dsem("dql0"), P.dsem("dql1")]
            sr = Ring(banks[0:3])
            por = Ring(banks[3:5])
            pzr = Ring(banks[5:7])
            ld_tok = [None, None]
            buf_free = [[], []]
            ep_free = [[], []]
            ust_free = [[], []]
            tz_free = [[], []]
            szp_tok = [None] * NT
            szp_free = [[] for _ in range(NT)]
            epi = 0
            zi = 0

            def load_head(h):
                b = h % 2
                P.dma('sp', Qh[b][:], QM[h], lsem[b], buf_free[b])
                P.dma('sp', Kh[b][:], KM[h], lsem[b], buf_free[b])
                ld_tok[b] = P.dma('sp', Vh[b][:].rearrange("p n e -> p (n e)"), VM[h], lsem[b], buf_free[b])

            load_head(0)
            for h in range(8):
                b = h % 2
                if h + 1 < 8:
                    load_head(h + 1)
                LD = [ld_tok[b]]
                items = []
                for qt in range(NT):
                    nk = 4 * (qt + 1)
                    for kt in range(nk):
                        items.append((qt, kt, nk))
                st = {}
                qstate = {}

                def emit_S(it):
                    nonlocal zi
                    qt, kt, nk = it
                    if kt == 0:
                        if h % 2 == 0:
                            kbz, pz, pwz = pzr.get()
                            kmz = None
                            for c in range(8):
                                kmz = P.mm(pz[:], wz[:, c, (h // 2) * 128:(h // 2 + 1) * 128], hT[:, c, qt * TT:(qt + 1) * TT],
                                           start=(c == 0), stop=(c == 7), waits=([k_wz] + pwz) if c == 0 else (), sig=(c == 7))
                            zz = zi % 2
                            zi += 1
                            kth = P.act(tz[zz][:], pz[:], AF.Tanh, scale=0.5, waits=[kmz] + tz_free[zz])
                            ksz = P.stt(szp[:, qt, :], tz[zz][:], 1.0, pz[:], ALU.add, ALU.mult, [kth] + szp_free[qt])
                            pzr.rel(kbz, [ksz, kth])
                            tz_free[zz] = [ksz]
                            szp_tok[qt] = ksz
                        kbo, po, pwo = por.get()
                        qstate[qt] = dict(kbo=kbo, po=po, pwo=pwo)
                    diag = kt >= 4 * qt
                    j = kt - 4 * qt
                    n0 = 128 * j if diag else 0
                    kb, ps, pw = sr.get()
                    q0 = qt * TT
                    kslc = Kh[b][0:96, kt * 128:(kt + 1) * 128]
                    if not diag:
                        km = P.mm(ps[:, 0:TT], kslc, Qh[b][0:96, q0:q0 + TT], waits=LD + pw, sig=True)
                    else:
                        P.mm(ps[:, n0:n0 + 128], kslc, Qh[b][0:96, q0 + n0:q0 + n0 + 128], start=True, stop=False, waits=LD + pw)
                        km = P.mm(ps[:, n0:n0 + 128], ident[:], maskab[:, 0:128], start=False, stop=True, waits=CST, sig=(j == 3))
                        if j < 3:
                            km = P.mm(ps[:, n0 + 128:TT], kslc, Qh[b][0:96, q0 + n0 + 128:q0 + TT], sig=True)
                    st[it] = dict(kb=kb, ps=ps, km=km, n0=n0)

                def emit_E(it):
                    s_ = st[it]
                    n0 = s_['n0']
                    ki, pt, pw = ptr.get()
                    ke = P.act(pt[:, n0:TT], s_['ps'][:, n0:TT], AF.Exp, scale=SC_MLA, waits=[s_['km']] + pw)
                    sr.rel(s_['kb'], [ke])
                    s_['ki'] = ki; s_['pt'] = pt; s_['ke'] = ke

                def emit_PV(it):
                    nonlocal epi
                    qt, kt, nk = it
                    s_ = st[it]
                    n0 = s_['n0']
                    q_ = qstate[qt]
                    po = q_['po']
                    last = (kt == nk - 1)
                    kpv = P.mm(po[:, n0:TT], Vh[b][:, kt, :], s_['pt'][:, n0:TT], start=(kt == 0), stop=last,
                               waits=[s_['ke']] + (q_['pwo'] if kt == 0 else []), sig=True)
                    ptr.rel(s_['ki'], [kpv])
                    del st[it]
                    if not last:
                        return
                    e = epi % 2
                    epi += 1
                    T = slice(qt * TT, (qt + 1) * TT)
                    hb = (h % 2) * 64
                    kr = P.recip(rd[e][0:64, :], po[64:128, :], [kpv] + ep_free[e])
                    ka1 = P.tt('dve', a1[e][hb:hb + 64, :], po[0:64, :], rd[e][0:64, :], ALU.mult, [kpv, kr] + ep_free[e])
                    por.rel(q_['kbo'], [ka1, kr])
                    ku = P.stt(ust[e][:], a1[e][hb:hb + 64, :], 0.5, szp[hb:hb + 64, qt, :], ALU.mult, ALU.mult, [ka1, szp_tok[qt]] + ust_free[e])
                    ep_free[e] = [ku]
                    if h % 2 == 1:
                        szp_free[qt] = [ku]
                    kus = P.dma('sp', UM[h * 64:(h + 1) * 64, T], ust[e][:], ust_sems[e], [ku])
                    ust_free[e] = [kus]
                    del qstate[qt]

                n_it = len(items)
                for i in range(n_it + 4):
                    if i < n_it:
                        emit_S(items[i])
                    if 0 <= i - 1 < n_it:
                        emit_E(items[i - 1])
                    if 0 <= i - 4 < n_it:
                        emit_PV(items[i - 4])
                buf_free[b] = [('pe', P.cnt['pe'])]
                if h == 0:
                    gen_tables_b(tbl_state, Cd, Sd, [0, 64], [kc1, ks0])
            P.barrier()
        if stop_after == 2:
            P.dead = True

        with ExitStack() as st3:
            wd = [sbt(st3, f"wd{i}", [128, 8, 384], BF16) for i in range(2)]
            wzd = [sbt(st3, f"wzd{i}", [128, 8, 128], BF16) for i in range(2)]
            wdsem = [P.dsem("dwd0"), P.dsem("dwd1")]
            wzsem = [P.dsem("dwz0"), P.dsem("dwz1")]
            Qd = sbt(st3, "Qd", [128, S], BF16)
            Kd = sbt(st3, "Kd", [128, S], BF16)
            Vd = sbt(st3, "Vd", [128, 32, 2, 128], BF16)
            acc = [sbt(st3, f"acc{i}", [128, S]) for i in range(2)]
            abr = Ring([sbt(st3, f"abf{i}", [128, TT], BF16) for i in range(3)])
            t1r = Ring([sbt(st3, f"t1d{i}", [128, TT]) for i in range(2)])
            t2r = Ring([sbt(st3, f"t2d{i}", [128, TT]) for i in range(2)])
            ptr = Ring([sbt(st3, f"ptd{i}", [128, 512], BF16) for i in range(6)])
            rd = [sbt(st3, f"rdd{i}", [128, TT]) for i in range(2)]
            tz = [sbt(st3, f"tzd{i}", [128, TT]) for i in range(2)]
            szd = [sbt(st3, f"szd{i}", [128, TT]) for i in range(2)]
            ustp = [sbt(st3, f"ustd{i}", [128, TT], BF16) for i in range(2)]
            k_vd1 = P.memset('pool', Vd[:], 1.0)
            pr = Ring(banks[0:4])
            pur = [Ring(banks[4:6]), Ring(banks[6:8])]
            mask512 = sbt(st3, "mask512", [128, 512], BF16)
            dmk = P.dsem("dmk")
            P.dma('pool', mask512[:, 0:256], m01_d, dmk)
            k_m512 = P.dma('pool', mask512[:, 256:512], m01_d, dmk)
            wd_free = [[], []]
            wzd_free = [[], []]
            QK_free = []
            Vd_free = []
            acc_free = [[], []]
            ust_free = [[], []]
            tz_free = [[], []]
            szd_free = [[], []]
            rd_free = [[], []]
            epi = 0
            combos = [(hp, g) for hp in range(4) for g in range(3)]
            wtok = {}

            def load_w(ci):
                hp, g = combos[ci]
                sl = ci % 2
                for s3 in range(3):
                    c0 = O_DIL + (s3 * 3 + g) * 512 + hp * 128
                    wtok[ci] = P.dma('pool', wd[sl][:, :, s3 * 128:(s3 + 1) * 128], w_in_v[:, :, c0:c0 + 128], wdsem[sl], wd_free[sl])

            def load_wz(hp):
                sl = hp % 2
                return P.dma('pool', wzd[sl][:], w_in_v[:, :, O_ZD + hp * 128:O_ZD + (hp + 1) * 128], wzsem[sl], wzd_free[sl])

            fin_pending = [None]
            epi_c = [0]

            def make_fin(hp, accw):
                WZ = [wz_tok[hp]]
                zsl = hp % 2

                def step(tt):
                    kmz = None
                    T = slice(tt * TT, (tt + 1) * TT)
                    kbz, pz, pwz = pr.get()
                    for c in range(8):
                        kmz = P.mm(pz[:], wzd[zsl][:, c, :], hT[:, c, T], start=(c == 0), stop=(c == 7),
                                   waits=(WZ + pwz) if c == 0 else (), sig=(c == 7))
                    e = epi_c[0] % 2
                    epi_c[0] += 1
                    kbe, pe_, pwe = pr.get()
                    ke = P.act(pe_[:], pz[:], AF.Exp, scale=-1.0, waits=[kmz] + pwe)
                    kw = None
                    for hh in range(2):
                        kw = P.stt(szd[e][hh * 64:(hh + 1) * 64, :], pe_[hh * 64:(hh + 1) * 64, :], 1.0, acc[hh][64:128, T],
                                   ALU.add, ALU.mult, [ke] + accw + szd_free[e])
                    pr.rel(kbe, [kw])
                    kl = P.act(rd[e][:], szd[e][:], AF.Ln, waits=[kw] + rd_free[e])
                    szd_free[e] = [kl]
                    kx = P.act(rd[e][:], rd[e][:], AF.Exp, scale=-1.0, waits=[kl])
                    i1, t1, w1 = t1r.get()
                    ka1 = None
                    for hh in range(2):
                        ka1 = P.tt('dve', t1[hh * 64:(hh + 1) * 64, :], acc[hh][0:64, T], pz[hh * 64:(hh + 1) * 64, :], ALU.mult,
                                   [kmz] + accw + w1)
                    pr.rel(kbz, [ke, ka1])
                    ku = P.tt('pool', ustp[e][:], t1[:], rd[e][:], ALU.mult, [ka1, kx] + ust_free[e])
                    t1r.rel(i1, [ku])
                    rd_free[e] = [ku]
                    kus = P.dma('sp', UM[512 + hp * 128:512 + (hp + 1) * 128, T], ustp[e][:], ust_sems[e], [ku])
                    ust_free[e] = [kus]
                    acc_free[0] = [ka1, kw]
                    acc_free[1] = [ka1, kw]
                    if tt == NT - 1:
                        wzd_free[zsl] = [kmz]
                return step

            load_w(0)
            wz_tok = {0: load_wz(0)}
            for ci, (hp, g) in enumerate(combos):
                sl = ci % 2
                if ci + 1 < len(combos):
                    load_w(ci + 1)
                WD = [wtok[ci]]
                d = (1, 4, 16)[g]
                L = S // d

                def tokslice(f0, count):
                    r = f0 // L
                    i0 = f0 % L
                    start = r + d * i0
                    return slice(start, start + d * (count - 1) + 1, d)

                pend = []
                lastB = [None]

                def stageA(which, tt):
                    T = slice(tt * TT, (tt + 1) * TT)
                    kb, ps, pw = pr.get()
                    km = None
                    for c in range(8):
                        km = P.mm(ps[:], wd[sl][:, c, which * 128:(which + 1) * 128], hT[:, c, T], start=(c == 0), stop=(c == 7),
                                  waits=(WD + pw) if c == 0 else (), sig=(c == 7))
                    ia, ab, wa = abr.get()
                    ka = P.act(ab[:], ps[:], AF.Copy, waits=[km] + wa)
                    pr.rel(kb, [ka])
                    pend.append((which, tt, ia, ab, ka))

                def stageB():
                    which, tt, ia, ab, ka = pend.pop(0)
                    dst = Qd if which == 0 else Kd
                    T = slice(tt * TT, (tt + 1) * TT)
                    kb2, pB, pw2 = pr.get()
                    kmr = P.mm(pB[:], rotm[:, 96:224], ab[:], waits=[ka] + CST + pw2, sig=True)
                    i1, t1, w1 = t1r.get()
                    i2, t2, w2 = t2r.get()
                    n_i = TT // d
                    nat = "p (i r) -> p i r"
                    t1d = t1[:].rearrange("p (r i) -> p i r", r=d)
                    t2d = t2[:].rearrange("p (r i) -> p i r", r=d)
                    k1 = P.tt('dve', t1d, ab[:].rearrange(nat, r=d), Cd[:, T].rearrange(nat, r=d), ALU.mult, [ka] + w1)
                    k2 = P.tt('dve', t2d, pB[:].rearrange(nat, r=d), Sd[:, T].rearrange(nat, r=d), ALU.mult, [kmr] + w2)
                    pr.rel(kb2, [k2])
                    abr.rel(ia, [kmr, k1])
                    dv = dst[:, :].rearrange("p (r i) -> p r i", r=d)[:, :, tt * n_i:(tt + 1) * n_i]
                    k3 = P.tt('pool', dv, t1[:].rearrange("p (r i) -> p r i", r=d),
                              t2[:].rearrange("p (r i) -> p r i", r=d), ALU.add, [k1, k2] + QK_free)
                    t1r.rel(i1, [k3])
                    t2r.rel(i2, [k3])
                    lastB[0] = k3

                for which in (0, 1):
                    for tt in range(NT):
                        stageA(which, tt)
                        if len(pend) > 1:
                            stageB()
                if stop_after == 2.2:
                    while pend:
                        stageB()
                    P.barrier()
                    P.dead = True
                V_ready = []
                km = None
                for nb4 in range(8):
                    kb, ps, pw = pr.get()
                    for q4 in range(4):
                        n = nb4 * 4 + q4
                        tsl = tokslice(n * 128, 128)
                        for c in range(8):
                            km = P.mm(ps[:, q4 * 128:(q4 + 1) * 128], hT[:, c, tsl], wd[sl][:, c, 256:384], start=(c == 0), stop=(c == 7),
                                      waits=(WD + pw) if (c == 0 and q4 == 0) else (), sig=(c == 7 and q4 == 3))
                    eng = 'act' if nb4 % 2 == 0 else 'dve'
                    ka = P.copy(eng, Vd[:, nb4 * 4:(nb4 + 1) * 4, :, 0:64], ps[:].rearrange("p (n h d) -> p n h d", n=4, h=2),
                                [km, k_vd1] + Vd_free)
                    pr.rel(kb, [ka])
                    V_ready.append(ka)
                    if nb4 == 0:
                        while pend:
                            stageB()
                    if fin_pending[0] is not None:
                        fin_pending[0](nb4)
                        if nb4 == 7:
                            fin_pending[0] = None
                wd_free[sl] = [km]
                if g == 0 and hp + 1 < 4:
                    wz_tok[hp + 1] = load_wz(hp + 1)
                QK_ready = [('pool', P.cnt['pool']), ('dve', P.cnt['dve'])]
                if stop_after == 2.3:
                    P.barrier()
                    P.dead = True
                nbp = L // 128
                pstate = [{}, {}]

                def get_pu(hh, n):
                    bk = n // 4
                    if bk not in pstate[hh]:
                        kbu, pu, pwu = pur[hh].get()
                        pstate[hh][bk] = dict(kb=kbu, pu=pu, pw=pwu, first=True)
                    return pstate[hh][bk]

                sts = {}

                def dS(n):
                    bb = n % nbp
                    N = 256 if bb + 1 < nbp else 128
                    tl = []
                    for hh in range(2):
                        p0 = hh * 64
                        kb, ps, pw = pr.get()
                        km_ = P.mm(ps[:, 0:N], Kd[p0:p0 + 64, n * 128:(n + 1) * 128], Qd[p0:p0 + 64, n * 128:n * 128 + N],
                                   waits=QK_ready + pw, sig=True)
                        tl.append((kb, ps, km_))
                    sts[n] = dict(tl=tl, N=N)

                def dE(n):
                    s_ = sts[n]
                    N = s_['N']
                    ki, pt, pw = ptr.get()
                    kes = []
                    for hh in range(2):
                        kb, ps, km_ = s_['tl'][hh]
                        ke = P.act(pt[:, hh * 256:hh * 256 + N], ps[:, 0:N], AF.Exp, scale=SC_DIL, waits=[km_] + pw)
                        pr.rel(kb, [ke])
                        kes.append(ke)
                    if N == 256:
                        kmk = P.tt('dve', pt[:, 0:512], pt[:, 0:512], mask512[:, 0:512], ALU.mult, kes + [k_m512])
                    else:
                        ptv = pt[:, :].rearrange("p (h c) -> p h c", h=2)[:, :, 0:128]
                        kmk = P.tt('dve', ptv, ptv, mask512[:, :].rearrange("p (h c) -> p h c", h=2)[:, :, 0:128], ALU.mult, kes + [k_m512])
                    s_['ki'] = ki; s_['pt'] = pt; s_['ke'] = kmk

                def dPV(n):
                    s_ = sts[n]
                    N = s_['N']
                    bb = n % nbp
                    kpv = None
                    for hh in range(2):
                        u = get_pu(hh, n)
                        c0 = (n % 4) * 128
                        w = [s_['ke']] + V_ready + (u['pw'] if u['first'] else [])
                        u['first'] = False
                        kpv = P.mm(u['pu'][:, c0:c0 + 128], Vd[:, n, hh, :], s_['pt'][:, hh * 256:hh * 256 + 128], start=(bb == 0), stop=True,
                                   waits=w, sig=True)
                        if N == 256:
                            u2 = get_pu(hh, n + 1)
                            c1 = ((n + 1) % 4) * 128
                            w2 = (u2['pw'] if u2['first'] else [])
                            u2['first'] = False
                            kpv = P.mm(u2['pu'][:, c1:c1 + 128], Vd[:, n, hh, :], s_['pt'][:, hh * 256 + 128:hh * 256 + 256], start=True, stop=False,
                                       waits=w2, sig=True)
                    ptr.rel(s_['ki'], [kpv])
                    del sts[n]
                    if n % 4 == 3:
                        bk = n // 4
                        f0 = bk * 512
                        r = f0 // L
                        i0 = f0 % L
                        for hh in range(2):
                            u = pstate[hh][bk]
                            accv = acc[hh][:, :].rearrange("p (i r) -> p r i", r=d)
                            if L >= 512:
                                dstv = accv[:, r, i0:i0 + 512]
                                srcv = u['pu'][:, :]
                            else:
                                nr = 512 // L
                                dstv = accv[:, r:r + nr, :]
                                srcv = u['pu'][:, :].rearrange("p (r i) -> p r i", r=nr)
                            if g == 0:
                                eng = 'act' if hh == 0 else 'dve'
                                ka_ = P.copy(eng, dstv, srcv, [kpv] + acc_free[hh])
                            else:
                                ka_ = P.tt('dve', dstv, dstv, srcv, ALU.add, [kpv] + acc_free[hh])
                            pur[hh].rel(u['kb'], [ka_])

                for i in range(32 + 4):
                    if i < 32:
                        dS(i)
                    if 0 <= i - 1 < 32:
                        dE(i - 1)
                    if 0 <= i - 4 < 32:
                        dPV(i - 4)
                if g == 0:
                    acc_free = [[], []]
                QK_free = [('pe', P.cnt['pe'])]
                Vd_free = [('pe', P.cnt['pe'])]
                if stop_after == 2.4 or (stop_after == 2.5 and ci == 1) or (stop_after == 2.6 and ci == 2):
                    P.barrier()
                    P.dead = True
                if g == 2:
                    fin_pending[0] = make_fin(hp, [('dve', P.cnt['dve']), ('act', P.cnt['act'])])
            if fin_pending[0] is not None:
                for tt_ in range(NT):
                    fin_pending[0](tt_)
                fin_pending[0] = None
            P.barrier()

        st_cd.close()
        if stop_after == 3:
            P.dead = True
        with ExitStack() as st4:
            wg = sbt(st4, "wg", [128, 16, 8, 128], BF16)
            wpm = sbt(st4, "wpm", [128, 4, 1024], BF16)
            wpd = sbt(st4, "wpd", [128, 4, 1024], BF16)
            wout = sbt(st4, "wout", [128, 8, 1024], BF16)
            gpost = sbt(st4, "gpost", [128, 1024])
            dw3 = [P.dsem(f"dw3_{i}") for i in range(4)]
            k_wg_oc = []
            for oc_ in range(8):
                sm_ = P.dsem(f"dwg{oc_}")
                P.dma('sp', wg[:, oc_, :, :], WGb[oc_], sm_, [k_wcast])
                k_wg_oc.append(P.dma('sp', wg[:, 8 + oc_, :, :], WGb[8 + oc_], sm_, [k_wcast]))
                if oc_ == 0:
                    k_wpm = P.dma('sp', wpm[:], WPMb.rearrange("(c p) n -> p c n", p=128), dw3[1], [k_wcast])
                    k_wpd = P.dma('sp', wpd[:], WPDb.rearrange("(c p) n -> p c n", p=128), dw3[1], [k_wcast])
                    k_wpm = k_wpd
            k_wout = P.dma('sp', wout[:], WOb.rearrange("(c p) n -> p c n", p=128), dw3[2], [k_wcast])
            k_gp = P.dma('sp', gpost[:], gpost_d.broadcast_to([128, 1024]), dw3[3])
            um = [sbt(st4, f"um{i}", [128, 4, TT], BF16) for i in range(2)]
            ud = [sbt(st4, f"ud{i}", [128, 4, TT], BF16) for i in range(2)]
            usem = [P.dsem("dul0"), P.dsem("dul1")]
            sg = [sbt(st4, f"sg{i}", [128, TT]) for i in range(2)]
            m1 = sbt(st4, "m1", [128, TT])
            m2 = sbt(st4, "m2", [128, TT])
            mg = [sbt(st4, f"mg{i}", [128, 8, TT], BF16) for i in range(2)]
            xt = [sbt(st4, f"xt{i}", [128, 1024]) for i in range(3)]
            o1 = [sbt(st4, f"o1_{i}", [128, 1024]) for i in range(2)]
            junk = sbt(st4, "junk", [128, 512])
            ssq = [sbt(st4, f"ssq{i}", [128, 4]) for i in range(2)]
            xsem = [P.dsem(f"dxt{i}") for i in range(3)]
            ysem = [P.dsem(f"dy{i}") for i in range(3)]
            pr = Ring(banks[0:4])
            por = Ring(banks[4:8])
            u_free = [[], []]
            u_tok = [None, None]
            sg_free = [[], []]
            mm_free = [[], []]
            mg_free = [[], []]
            mg_done = [None, None]
            xt_free = [[], [], []]
            o1_free = [[], []]
            ssq_free = [[], []]
            junk_free = [[]]
            out_toks = []
            xi = [0]
            last_umm = [None, None]
            last_omm = [None, None]

            def load_u(tt):
                ub = tt % 2
                T = slice(tt * TT, (tt + 1) * TT)
                P.dma('sp', um[ub][:], UM[0:512, T].rearrange("(c p) t -> p c t", p=128), usem[ub], u_free[ub])
                u_tok[ub] = P.dma('sp', ud[ub][:], UM[512:1024, T].rearrange("(c p) t -> p c t", p=128), usem[ub], u_free[ub])

            def G_oc(tt, oc):
                ub = tt % 2
                T = slice(tt * TT, (tt + 1) * TT)
                grp = []
                for (wt, off, src, nk, kw) in ((wg, 0, None, 8, k_wg_oc[oc]), (wg, 8, None, 8, k_wg_oc[oc]), (wpm, 0, um[ub], 4, k_wpm), (wpd, 0, ud[ub], 4, k_wpd)):
                    kb, ps, pw = pr.get()
                    km = None
                    for c in range(nk):
                        rhs = hT[:, c, T] if src is None else src[:, c, :]
                        lw = wt[:, off + oc, c, :] if src is None else wt[:, c, oc * 128:(oc + 1) * 128]
                        km = P.mm(ps[:], lw, rhs, start=(c == 0), stop=(c == nk - 1),
                                  waits=([kw, u_tok[ub]] + pw) if c == 0 else (), sig=(c == nk - 1))
                    grp.append((kb, ps, km))
                last_umm[ub] = grp[3][2]
                ks1 = P.act(sg[0][:], grp[0][1][:], AF.Sigmoid, waits=[grp[0][2]] + sg_free[0])
                pr.rel(grp[0][0], [ks1])
                ks2 = P.act(sg[1][:], grp[1][1][:], AF.Sigmoid, waits=[grp[1][2]] + sg_free[1])
                pr.rel(grp[1][0], [ks2])
                k1 = P.tt('dve', m1[:], sg[0][:], grp[2][1][:], ALU.mult, [ks1, grp[2][2]] + mm_free[0])
                pr.rel(grp[2][0], [k1])
                k2 = P.tt('dve', m2[:], sg[1][:], grp[3][1][:], ALU.mult, [ks2, grp[3][2]] + mm_free[1])
                pr.rel(grp[3][0], [k2])
                sg_free[0] = [k1]
                sg_free[1] = [k2]
                k3 = P.tt('pool', mg[ub][:, oc, :], m1[:], m2[:], ALU.add, [k1, k2] + (mg_free[ub] if oc == 0 else []))
                mm_free[0] = [k3]
                mm_free[1] = [k3]
                mg_done[ub] = k3
                if oc == 7:
                    u_free[ub] = [last_umm[ub]]

            def O_s(tt, s):
                ub = tt % 2
                r0 = tt * TT + s * 128
                xb = xi[0] % 3
                ob = xi[0] % 2
                xi[0] += 1
                kx = P.dma('sp', xt[xb][:], x[r0:r0 + 128, :], xsem[xb], xt_free[xb])
                halves = []
                for hf in range(2):
                    kb, po, pw = por.get()
                    km = None
                    for oc in range(8):
                        km = P.mm(po[:], mg[ub][:, oc, s * 128:(s + 1) * 128], wout[:, oc, hf * 512:(hf + 1) * 512], start=(oc == 0), stop=(oc == 7),
                                  waits=([k_wout, mg_done[ub]] + pw) if oc == 0 else (), sig=(oc == 7))
                    kq = P.act(junk[:], po[:], AF.Square, accum=ssq[ob][:, hf:hf + 1], waits=[km] + junk_free[0] + ssq_free[ob])
                    junk_free[0] = [kq]
                    halves.append((kb, po, km, kq))
                    last_omm[ub] = km
                ka = P.tt('dve', ssq[ob][:, 2:3], ssq[ob][:, 0:1], ssq[ob][:, 1:2], ALU.add, [halves[0][3], halves[1][3]])
                kr = P.act(ssq[ob][:, 3:4], ssq[ob][:, 2:3], AF.Sqrt, scale=1.0 / 1024, bias=epsb[:], waits=[ka])
                kr2 = P.recip(ssq[ob][:, 3:4], ssq[ob][:, 3:4], [kr])
                ko = None
                for hf in range(2):
                    H = slice(hf * 512, (hf + 1) * 512)
                    ko = P.stt(o1[ob][:, H], halves[hf][1][:], ssq[ob][:, 3:4], gpost[:, H], ALU.mult, ALU.mult,
                               waits=[kr2, k_gp] + o1_free[ob])
                    por.rel(halves[hf][0], [ko])
                ssq_free[ob] = [ko]
                kf = P.tt('pool', xt[xb][:], o1[ob][:], xt[xb][:], ALU.add, [ko, kx])
                o1_free[ob] = [kf]
                kys = P.dma('sp', y[r0:r0 + 128, :], xt[xb][:], ysem[xb], [kf])
                xt_free[xb] = [kys]
                out_toks.append(kys)
                if s == 3:
                    mg_free[ub] = [last_omm[ub]]

            load_u(0)
            for oc in range(8):
                G_oc(0, oc)
            for tt in range(NT):
                if tt + 1 < NT:
                    load_u(tt + 1)
                for s in range(4):
                    O_s(tt, s)
                    if tt + 1 < NT:
                        G_oc(tt + 1, 2 * s)
                        G_oc(tt + 1, 2 * s + 1)
            P.wait('sp', out_toks)
            P.barrier()
        P.emit()
    return nc


_NC_CACHE = {}


def _consts():
    p = np.arange(128)[:, None]
    f = np.arange(128)[None, :]
    ident = np.eye(128, dtype=np.float32)
    maskab = np.zeros((128, 256), np.float32)
    maskab[:, 0:128] = np.where(p > f, NEG, 0.0)
    maskab[:, 128:256] = np.where(f > p, NEG, 0.0)
    rot = np.zeros((128, 224), np.float32)
    for i in range(16):
        rot[80 + i, 64 + i] = -1.0
        rot[64 + i, 80 + i] = 1.0
    for hb in (0, 64):
        for i in range(8):
            rot[hb + 8 + i, 96 + hb + i] = -1.0
            rot[hb + i, 96 + hb + 8 + i] = 1.0
    theta = np.float32(500000.0)
    inv_mla = (theta ** (-np.arange(0, 32, 2, dtype=np.float32) / np.float32(32))).astype(np.float32)
    inv_dil = (theta ** (-np.arange(0, 16, 2, dtype=np.float32) / np.float32(16))).astype(np.float32)
    freqs = np.zeros((128, 2), np.float32)
    for q in range(4):
        for i in range(32):
            freqs[q * 32 + i, 0] = inv_mla[i % 16]
            freqs[q * 32 + i, 1] = inv_dil[i % 8] if i < 16 else 0.0
    mask01 = (maskab == 0.0).astype(np.float32)
    return dict(ident=ident, maskab=maskab, rotm=rot, freqs=freqs, mask01=mask01)


def make_in_maps(x, positions, pre_norm_g, w_in, q_norm_g, w_uq, kv_norm_g, w_ukv,
                 w_proj_mla, w_proj_dil, w_out, post_norm_g):
    cst = _consts()
    f32 = np.float32
    shared = dict(
        w_in=np.ascontiguousarray(w_in[0], f32), w_uq=np.ascontiguousarray(w_uq[0], f32),
        w_ukv=np.ascontiguousarray(w_ukv[0], f32), w_pm=np.ascontiguousarray(w_proj_mla[0], f32),
        w_pd=np.ascontiguousarray(w_proj_dil[0], f32), w_out=np.ascontiguousarray(w_out[0], f32),
        gpre=np.ascontiguousarray(np.asarray(pre_norm_g[0], f32).reshape(8, 128).T),
        gq=np.ascontiguousarray(np.asarray(q_norm_g[0], f32).reshape(3, 128).T),
        gkv=np.ascontiguousarray(np.asarray(kv_norm_g[0], f32).reshape(2, 128).T),
        gpost=np.ascontiguousarray(np.asarray(post_norm_g[0], f32).reshape(1, 1024)),
        **cst)
    maps = []
    for b in range(8):
        xb = np.asarray(x[b], f32)
        m = dict(shared)
        m["x"] = np.ascontiguousarray(xb)
        m["xT"] = np.ascontiguousarray(xb.T)
        m["pos"] = np.ascontiguousarray(np.asarray(positions[b], np.int32).reshape(1, S))
        maps.append(m)
    return maps


def kernel(x, positions, pre_norm_g, w_in, q_norm_g, w_uq, kv_norm_g, w_ukv,
           w_proj_mla, w_proj_dil, w_out, post_norm_g):
    if "nc" not in _NC_CACHE:
        _NC_CACHE["nc"] = build(False)
    nc = _NC_CACHE["nc"]
    maps = make_in_maps(x, positions, pre_norm_g, w_in, q_norm_g, w_uq, kv_norm_g, w_ukv,
                        w_proj_mla, w_proj_dil, w_out, post_norm_g)
    res = run_bass_kernel_spmd(nc, maps, core_ids=list(range(8)))
    out = np.stack([np.asarray(r["y"], np.float32) for r in res.results], axis=0)
    return out
```

```python
import math
import numpy as np
import concourse.bass as bass
import concourse.mybir as mybir
from concourse.bass_utils import run_bass_kernel_spmd
from contextlib import ExitStack

F32 = mybir.dt.float32
BF16 = mybir.dt.bfloat16
I32 = mybir.dt.int32
AF = mybir.ActivationFunctionType
ALU = mybir.AluOpType

S = 4096
TT = 512
NT = S // TT
EPS = 1e-6
NEG = -30000.0
C1 = 6.28125
C2 = 2.0 * math.pi - 6.28125
O_CQ, O_CKV, O_KR, O_DIL, O_ZM, O_ZD, O_GM, O_GD = 0, 384, 640, 672, 5280, 5792, 6304, 7328
SC_MLA = 96.0 ** -0.5
SC_DIL = 64.0 ** -0.5


class Prog:
    ENG = ('pe', 'act', 'dve', 'pool', 'sp')

    def __init__(self, nc, es):
        self.nc = nc
        self.es = es
        self.streams = {e: [] for e in self.ENG}
        self.sem = {}
        self.cnt = {}
        self.dsems = []
        self.dead = False
        for e in self.ENG[:4]:
            self._newsem(e)

    def _newsem(self, name):
        self.sem[name] = self.es.enter_context(self.nc.semaphore(name))
        self.cnt[name] = 0
        return name

    def dsem(self, name):
        self.dsems.append(name)
        return self._newsem(name)

    def op(self, eng, fn, waits=(), sig=True):
        if self.dead:
            return (eng, self.cnt[eng])
        tok = None
        if sig:
            self.cnt[eng] += 1
            tok = (eng, self.cnt[eng])
        self.streams[eng].append((fn, [w for w in waits if w is not None], (eng, 1) if sig else None))
        return tok

    def dma(self, q, out, in_, sem, waits=()):
        if self.dead:
            return (sem, self.cnt[sem])
        self.cnt[sem] += 16
        tok = (sem, self.cnt[sem])
        self.streams[q].append((lambda e: e.dma_start(out=out, in_=in_), [w for w in waits if w is not None], (sem, 16)))
        return tok

    def wait(self, eng, waits):
        if self.dead:
            return
        self.streams[eng].append((None, [w for w in waits if w is not None], None))

    def barrier(self):
        toks = [(s, self.cnt[s]) for s in list(self.ENG[:4]) + self.dsems if self.cnt[s] > 0]
        for e in self.ENG:
            self.wait(e, toks)

    def mm(self, out, lhsT, rhs, start=True, stop=True, waits=(), sig=False):
        return self.op('pe', lambda e: e.matmul(out, lhsT=lhsT, rhs=rhs, start=start, stop=stop), waits, sig)

    def act(self, out, in_, func, scale=None, bias=None, accum=None, waits=(), sig=True):
        kw = {}
        if scale is not None:
            kw['scale'] = scale
        if bias is not None:
            kw['bias'] = bias
        if accum is not None:
            kw['accum_out'] = accum
        return self.op('act', lambda e: e.activation(out=out, in_=in_, func=func, **kw), waits, sig)

    def tt(self, eng, out, in0, in1, op, waits=(), sig=True):
        return self.op(eng, lambda e: e.tensor_tensor(out=out, in0=in0, in1=in1, op=op), waits, sig)

    def ts(self, eng, out, in0, s1, op0, s2=None, op1=None, waits=(), sig=True):
        if op1 is None:
            return self.op(eng, lambda e: e.tensor_scalar(out=out, in0=in0, scalar1=s1, scalar2=None, op0=op0), waits, sig)
        return self.op(eng, lambda e: e.tensor_scalar(out=out, in0=in0, scalar1=s1, scalar2=s2, op0=op0, op1=op1), waits, sig)

    def stt(self, out, in0, scalar, in1, op0, op1, waits=(), sig=True):
        return self.op('dve', lambda e: e.scalar_tensor_tensor(out=out, in0=in0, scalar=scalar, in1=in1, op0=op0, op1=op1), waits, sig)

    def copy(self, eng, out, in_, waits=(), sig=True):
        if eng == 'act':
            return self.act(out, in_, AF.Copy, waits=waits, sig=sig)
        return self.op(eng, lambda e: e.tensor_copy(out=out, in_=in_), waits, sig)

    def recip(self, out, in_, waits=(), sig=True):
        return self.op('dve', lambda e: e.reciprocal(out=out, in_=in_), waits, sig)

    def memset(self, eng, ap, val, waits=(), sig=True):
        return self.op(eng, lambda e: e.memset(ap, val), waits, sig)

    def check(self):
        pos = {e: 0 for e in self.ENG}
        val = {s: 0 for s in self.sem}
        progress = True
        while progress:
            progress = False
            for e in self.ENG:
                st = self.streams[e]
                while pos[e] < len(st):
                    fn, waits, inc = st[pos[e]]
                    if all(val[s] >= v for (s, v) in waits):
                        if inc is not None:
                            val[inc[0]] += inc[1]
                        pos[e] += 1
                        progress = True
                    else:
                        break
        bad = {e: (pos[e], len(self.streams[e])) for e in self.ENG if pos[e] < len(self.streams[e])}
        if bad:
            msg = []
            for e, (p, n) in bad.items():
                fn, waits, inc = self.streams[e][p]
                msg.append("%s stuck at %d/%d waits=%s have=%s" % (e, p, n, [w for w in waits if val[w[0]] < w[1]], [(w[0], val[w[0]]) for w in waits if val[w[0]] < w[1]]))
            raise RuntimeError("DEADLOCK: " + "; ".join(msg))

    def emit(self):
        self.check()
        nc = self.nc
        engobj = {'pe': 'tensor', 'act': 'scalar', 'dve': 'vector', 'pool': 'gpsimd', 'sp': 'sync'}
        with nc.Block() as block:
            for en in self.ENG:
                stream = self.streams[en]

                def body(engine, stream=stream):
                    waited = {}
                    for fn, waits, inc in stream:
                        for (s, v) in waits:
                            if waited.get(s, 0) >= v:
                                continue
                            engine.wait_ge(self.sem[s], v)
                            waited[s] = v
                        if fn is not None:
                            ins = fn(engine)
                            if inc is not None:
                                ins.then_inc(self.sem[inc[0]], inc[1])
                getattr(block, engobj[en])(body)


class Ring:
    def __init__(self, items):
        self.items = list(items)
        self.free = [[] for _ in self.items]
        self.busy = [False for _ in self.items]
        self.i = 0

    def get(self):
        k = self.i % len(self.items)
        self.i += 1
        assert not self.busy[k], "ring slot re-acquired before its release was emitted"
        self.busy[k] = True
        w = self.free[k]
        self.free[k] = []
        return k, self.items[k], w

    def rel(self, k, toks):
        self.free[k] = [t for t in toks if t is not None]
        self.busy[k] = False


class _Stop(Exception):
    pass


def build(debug=False, stop_after=99):
    nc = bass.Bass("TRN2", target_bir_lowering=False)

    def din(name, shape, dt=F32):
        return nc.dram_tensor(name, shape, dt, kind="ExternalInput").ap()

    xT = din("xT", [1024, S]); x = din("x", [S, 1024]); pos = din("pos", [1, S], I32)
    w_in = din("w_in", [1024, 8352]); w_uq = din("w_uq", [384, 768]); w_ukv = din("w_ukv", [256, 1024])
    w_pm = din("w_pm", [512, 1024]); w_pd = din("w_pd", [512, 1024]); w_out = din("w_out", [1024, 1024])
    ident_d = din("ident", [128, 128]); mask_d = din("maskab", [128, 256]); rot_d = din("rotm", [128, 224])
    freq_d = din("freqs", [128, 2]); gpre_d = din("gpre", [128, 8]); gq_d = din("gq", [128, 3])
    gkv_d = din("gkv", [128, 2]); gpost_d = din("gpost", [1, 1024])
    m01_d = din("mask01", [128, 256])
    y = nc.dram_tensor("y", [S, 1024], F32, kind="ExternalOutput").ap()
    sk = "ExternalOutput" if debug else "Internal"
    QM = nc.dram_tensor("QM", [8, 96, S], BF16, kind=sk).ap()
    KM = nc.dram_tensor("KM", [8, 96, S], BF16, kind=sk).ap()
    VM = nc.dram_tensor("VM", [8, 128, 32 * 128], BF16, kind=sk).ap()
    UM = nc.dram_tensor("UM", [1024, S], BF16, kind=sk).ap()
    WGb = nc.dram_tensor("WGb", [16, 128, 8, 128], BF16, kind="Internal").ap()
    WPMb = nc.dram_tensor("WPMb", [512, 1024], BF16, kind="Internal").ap()
    WPDb = nc.dram_tensor("WPDb", [512, 1024], BF16, kind="Internal").ap()
    WOb = nc.dram_tensor("WOb", [1024, 1024], BF16, kind="Internal").ap()

    w_in_v = w_in.rearrange("(c p) n -> p c n", p=128)
    xT_v = xT.rearrange("(c p) t -> p c t", p=128)

    with ExitStack() as es:
        P = Prog(nc, es)

        uid = [0]

        def sbt(stack, name, shape, dt=F32):
            uid[0] += 1
            return stack.enter_context(nc.sbuf_tensor('s%d_%s' % (uid[0], name), shape, dt))

        banks = [es.enter_context(nc.psum_tensor(f"ps{i}", [128, 512], F32)) for i in range(8)]

        hT = sbt(es, "hT", [128, 8, S], BF16)
        ident = sbt(es, "ident", [128, 128], BF16)
        maskab = sbt(es, "maskab", [128, 256], BF16)
        rotm = sbt(es, "rotm", [128, 224], BF16)
        ones = sbt(es, "ones", [128, 128], F32)
        freqs = sbt(es, "freqs", [128, 2], F32)
        gpre = sbt(es, "gpre", [128, 8], F32)
        gq = sbt(es, "gq", [128, 3], F32)
        gkv = sbt(es, "gkv", [128, 2], F32)
        epsb = sbt(es, "epsb", [128, 1], F32)
        hpib = sbt(es, "hpib", [128, 1], F32)

        dc = P.dsem("dcst")
        dcp = P.dsem("dcstp")
        P.dma('pool', ident[:], ident_d, dcp)
        P.dma('pool', maskab[:], mask_d, dcp)
        k_cstp = P.dma('pool', rotm[:], rot_d, dcp)
        P.dma('sp', freqs[:], freq_d, dc)
        P.dma('sp', gpre[:], gpre_d, dc)
        P.dma('sp', gq[:], gq_d, dc)
        k_cst = P.dma('sp', gkv[:], gkv_d, dc)
        k_ones = P.memset('dve', ones[:], 1.0)
        k_eps = P.memset('dve', epsb[:], EPS)
        k_hpi = P.memset('dve', hpib[:], math.pi / 2)
        CST = [k_cst, k_cstp, k_ones, k_eps, k_hpi]

        def gen_tables_a(stack, fcol, tag, pk_dt):
            CH = 1024
            posi = sbt(stack, "tb_posi", [128, CH], I32)
            ang = sbt(stack, "tb_ang", [128, CH])
            xx = sbt(stack, "tb_x", [128, CH])
            ki = sbt(stack, "tb_ki", [128, CH], I32)
            kf = sbt(stack, "tb_kf", [128, CH])
            gg = [sbt(stack, "tb_g0", [128, CH]), sbt(stack, "tb_g1", [128, CH])]
            pk = [sbt(stack, "tb_pk0", [128, CH], pk_dt), sbt(stack, "tb_pk1", [128, CH], pk_dt)]
            dsm = P.dsem("dtb" + tag)
            kd = None
            for q in range(4):
                kd = P.dma('sp', posi[q * 32:(q + 1) * 32, :], pos[:, q * CH:(q + 1) * CH].broadcast_to([32, CH]), dsm)
            k0 = P.copy('dve', xx[:], posi[:], [kd])
            k1 = P.ts('dve', ang[:], xx[:], freqs[:, fcol:fcol + 1], ALU.mult, waits=[k0] + CST)
            last = [k1]
            khs = []
            for which in (0, 1):
                ka = P.ts('dve', xx[:], ang[:], 1.0 / (2 * math.pi), ALU.mult, 0.25 * which, ALU.add, waits=last)
                kb = P.copy('dve', ki[:], xx[:], [ka])
                kc = P.copy('dve', kf[:], ki[:], [kb])
                kdd = P.tt('dve', xx[:], xx[:], kf[:], ALU.subtract, [kc])
                ke = P.ts('dve', gg[which][:], xx[:], 0.5, ALU.is_gt, waits=[kdd])
                kff = P.tt('dve', kf[:], kf[:], gg[which][:], ALU.add, [ke])
                kg = P.stt(xx[:], kf[:], -C1, ang[:], ALU.mult, ALU.add, [kff])
                kh0 = P.stt(gg[which][:], kf[:], -C2, xx[:], ALU.mult, ALU.add, [kg])
                lim = 3.14159
                lo, hi = (-lim, lim) if which == 0 else (-lim - math.pi / 2, lim - math.pi / 2)
                kh = P.ts('dve', gg[which][:], gg[which][:], lo, ALU.max, hi, ALU.min, waits=[kh0])
                last = [kh]
                khs.append(kh)
            return dict(gg=gg, pk=pk, khs=khs, CH=CH)

        def gen_tables_b(stt_, Ct, St, dst_groups, extra=()):
            CH = stt_['CH']
            gg, pk, khs = stt_['gg'], stt_['pk'], stt_['khs']
            outs = [P.act(pk[0][:], gg[0][:], AF.Sin, waits=[khs[0]]),
                    P.act(pk[1][:], gg[1][:], AF.Sin, bias=hpib[:], waits=[khs[1]] + CST)]
            done = []
            for which, dst in ((0, St), (1, Ct)):
                for g0 in dst_groups:
                    for q in range(4):
                        done.append(P.copy('pool', dst[g0:g0 + 32, q * CH:(q + 1) * CH], pk[which][q * 32:(q + 1) * 32, :], [outs[which]] + list(extra)))
            return done

        st_cd = es.enter_context(ExitStack())
        Cd = sbt(st_cd, "Cd", [128, S], BF16)
        Sd = sbt(st_cd, "Sd", [128, S], BF16)
        st_ma = es.enter_context(ExitStack())
        Cm = sbt(st_ma, "Cm", [128, S], F32)
        Sm = sbt(st_ma, "Sm", [128, S], F32)
        hT_tok = [None] * NT
        with ExitStack() as st0:
            xs = [sbt(st0, f"xs{i}", [128, 8, TT]) for i in range(2)]
            sq = [sbt(st0, f"sq{i}", [128, TT]) for i in range(2)]
            Rb = [sbt(st0, f"Rb{i}", [128, TT]) for i in range(2)]
            xsem = [[P.dsem(f"dxs{i}_{c}") for c in range(8)] for i in range(2)]
            xs_free = [[], []]
            sq_free = [[], []]
            Rb_free = [[], []]
            pr = Ring(banks[0:2])
            gen_tables_b(gen_tables_a(st0, 0, 'm', F32), Cm, Sm, [64])
            for tt in range(NT):
                T = slice(tt * TT, (tt + 1) * TT)
                b = tt % 2
                kxc = [P.dma('sp', xs[b][:, c, :], xT_v[:, c, T], xsem[b][c], xs_free[b]) for c in range(8)]
                kx = kxc[7]
                kb, ps, pw = pr.get()
                kmm = None
                for c in range(8):
                    ks = P.act(sq[c % 2][:], xs[b][:, c, :], AF.Square, waits=[kxc[c]] + sq_free[c % 2])
                    kmm = P.mm(ps[:], ones[:], sq[c % 2][:], start=(c == 0), stop=(c == 7),
                               waits=[ks, k_ones] + (pw if c == 0 else []), sig=True)
                    sq_free[c % 2] = [kmm]
                kr = P.act(Rb[b][:], ps[:], AF.Ln, scale=1.0 / 1024, bias=epsb[:], waits=[kmm, k_eps] + Rb_free[b])
                pr.rel(kb, [kr])
                kr2 = P.act(Rb[b][:], Rb[b][:], AF.Exp, scale=-0.5, waits=[kr])
                kl = None
                for c in range(8):
                    kl = P.stt(hT[:, c, T], xs[b][:, c, :], gpre[:, c:c + 1], Rb[b][:], ALU.mult, ALU.mult,
                               waits=[kr2, kxc[c]] + CST)
                xs_free[b] = [kl]
                Rb_free[b] = [kl]
                hT_tok[tt] = kl
            P.barrier()
        if stop_after == 0:
            P.dead = True

        qst_sem = P.dsem("dqst")
        with ExitStack() as st1:
            wA = sbt(st1, "wA", [128, 8, 672], BF16)
            wkr = sbt(st1, "wkr", [128, 8, 96], BF16)
            wuq = sbt(st1, "wuq", [128, 3, 768], BF16)
            wukv = sbt(st1, "wukv", [128, 2, 1024], BF16)
            dw1 = P.dsem("dw1")
            kz = P.memset('pool', wkr[:], 0.0)
            P.dma('pool', wA[:], w_in_v[:, :, 0:672], dw1)
            P.dma('pool', wkr[:, :, 64:96], w_in_v[:, :, O_KR:O_KR + 32], dw1, [kz])
            P.dma('pool', wuq[:], w_uq.rearrange("(c p) n -> p c n", p=128), dw1)
            k_w1 = P.dma('pool', wukv[:], w_ukv.rearrange("(c p) n -> p c n", p=128), dw1)
            W1 = [k_w1]
            cqn = [sbt(st1, f"cqn{i}", [128, 3, TT], BF16) for i in range(2)]
            ckvn = [sbt(st1, f"ckvn{i}", [128, 2, TT], BF16) for i in range(2)]
            sq = [sbt(st1, f"sq1_{i}", [128, TT]) for i in range(2)]
            Rq = [sbt(st1, f"Rq{i}", [128, TT]) for i in range(2)]
            krb = [sbt(st1, f"krb{i}", [96, TT], BF16) for i in range(2)]
            kro = [sbt(st1, f"kro{i}", [96, TT], BF16) for i in range(2)]
            Qst = sbt(st1, "Qst", [96, 8, TT], BF16)
            Kst = sbt(st1, "Kst", [64, 8, TT], BF16)
            Vst = sbt(st1, "Vst", [128, 8, 4, 128], BF16)
            t1r = Ring([sbt(st1, f"t1_{i}", [128, TT]) for i in range(3)])
            t2r = Ring([sbt(st1, f"t2_{i}", [128, TT]) for i in range(3)])
            k_krb0 = [P.memset('pool', krb[i][:], 0.0) for i in range(2)]
            k_vst1 = P.memset('pool', Vst[:], 1.0)
            krsem = [P.dsem("dkr0"), P.dsem("dkr1")]
            kst_sem = P.dsem("dkst")
            vst_sem = P.dsem("dvst")
            pr = Ring(banks)
            sq_free = [[], []]
            cqn_free = [[], []]
            ckvn_free = [[], []]
            Rq_free = [[], []]
            Rqi = [0]
            krb_free = [[], []]
            kro_free = [[], []]
            Qst_free = [[]]
            Kst_free = [[]]
            Vst_free = [[]]
            rot_mla = rotm[0:96, 0:96]
            LT = {}
            wv = wukv[:, :, :].rearrange("p c (h two d) -> p c h two d", two=2, d=64)

            prL = Ring(banks[0:6])
            prU = Ring(banks[6:8])

            def norm_front(groups, idx, dim):
                kbs, pss, pws = prL.get()
                kmm = None
                for n_, j in enumerate(idx):
                    ks = P.act(sq[n_ % 2][:], groups[j][1][:], AF.Square, waits=[groups[j][2]] + sq_free[n_ % 2])
                    kmm = P.mm(pss[:], ones[:], sq[n_ % 2][:], start=(n_ == 0), stop=(n_ == len(idx) - 1),
                               waits=[ks, k_ones] + (pws if n_ == 0 else []), sig=True)
                    sq_free[n_ % 2] = [kmm]
                ri = Rqi[0] % 2
                Rqi[0] += 1
                kr = P.act(Rq[ri][:], pss[:], AF.Ln, scale=1.0 / dim, bias=epsb[:], waits=[kmm] + Rq_free[ri])
                prL.rel(kbs, [kr])
                kr2 = P.act(Rq[ri][:], Rq[ri][:], AF.Exp, scale=-0.5, waits=[kr])
                return ri, kr2

            def norm_back(groups, idx, dst, dst_free, gcol, ri, kr2):
                kl = None
                for n_, j in enumerate(idx):
                    kl = P.stt(dst[:, n_, :], groups[j][1][:], gcol[:, n_:n_ + 1], Rq[ri][:], ALU.mult, ALU.mult,
                               waits=[kr2, groups[j][2]] + dst_free + CST)
                    prL.rel(groups[j][0], [kl])
                Rq_free[ri] = [kl]
                return kl

            def main_groups(tt, cols):
                T = slice(tt * TT, (tt + 1) * TT)
                hw = [hT_tok[tt]] + W1
                groups = []
                for j in cols:
                    kb, ps, pw = prL.get()
                    km = None
                    for c in range(8):
                        km = P.mm(ps[:], wA[:, c, j * 128:(j + 1) * 128], hT[:, c, T],
                                  start=(c == 0), stop=(c == 7), waits=(hw + pw) if c == 0 else (), sig=(c == 7))
                    groups.append((kb, ps, km))
                return groups

            def Lcq_front(tt):
                groups = main_groups(tt, range(3))
                ri, kr2 = norm_front(groups, [0, 1, 2], 384.0)
                LT[tt] = dict(cq=(groups, ri, kr2))

            def Lcq_back(tt):
                b = tt % 2
                groups, ri, kr2 = LT[tt]['cq']
                LT[tt]['k_cq'] = norm_back(groups, [0, 1, 2], cqn[b], cqn_free[b], gq, ri, kr2)

            def Lkv_front(tt):
                T = slice(tt * TT, (tt + 1) * TT)
                b = tt % 2
                hw = [hT_tok[tt]] + W1
                groups = main_groups(tt, range(3, 5))
                kbk, psk, pwk = prL.get()
                kmk = None
                for c in range(8):
                    kmk = P.mm(psk[0:96, :], wkr[:, c, :], hT[:, c, T], start=(c == 0), stop=(c == 7),
                               waits=(hw + pwk) if c == 0 else (), sig=(c == 7))
                ri, kr2 = norm_front(groups, [0, 1], 256.0)
                ka = P.act(krb[b][64:96, :], psk[64:96, :], AF.Copy, waits=[kmk, k_krb0[b]] + krb_free[b])
                prL.rel(kbk, [ka])
                kb2, pB, pw2 = prL.get()
                kmr = P.mm(pB[0:96, :], rot_mla, krb[b][0:96, :], waits=[ka] + CST + pw2, sig=True)
                LT[tt]['kv'] = (groups, ri, kr2, ka, kb2, pB, kmr)

            def Lkv_back(tt):
                T = slice(tt * TT, (tt + 1) * TT)
                b = tt % 2
                groups, ri, kr2, ka, kb2, pB, kmr = LT[tt]['kv']
                LT[tt]['k_ckv'] = norm_back(groups, [0, 1], ckvn[b], ckvn_free[b], gkv, ri, kr2)
                i1, t1, w1 = t1r.get()
                i2, t2, w2 = t2r.get()
                k1 = P.tt('pool', t1[64:96, :], krb[b][64:96, :], Cm[64:96, T], ALU.mult, [ka] + w1)
                k2 = P.tt('dve', t2[64:96, :], pB[64:96, :], Sm[64:96, T], ALU.mult, [kmr] + w2)
                prL.rel(kb2, [k2])
                krb_free[b] = [kmr, k1]
                k3 = P.tt('dve', kro[b][64:96, :], t1[64:96, :], t2[64:96, :], ALU.add, [k1, k2] + kro_free[b])
                t1r.rel(i1, [k3])
                t2r.rel(i2, [k3])
                kd = None
                for h in range(8):
                    kd = P.dma('sp', KM[h, 64:96, T], kro[b][64:96, :], krsem[b], [k3])
                kro_free[b] = [kd]

            def Uq(tt):
                T = slice(tt * TT, (tt + 1) * TT)
                b = tt % 2
                k_cq = LT[tt]['k_cq']
                pend = []
                Q_done = []

                def qA(h):
                    kb, ps, pw = prU.get()
                    km = None
                    for j in range(3):
                        km = P.mm(ps[0:96, :], wuq[:, j, h * 96:(h + 1) * 96], cqn[b][:, j, :], start=(j == 0), stop=(j == 2),
                                  waits=([k_cq] + W1 + pw) if j == 0 else (), sig=(j == 2))
                    ka = P.act(Qst[0:96, h, :], ps[0:96, :], AF.Copy, waits=[km] + Qst_free[0])
                    prU.rel(kb, [ka])
                    pend.append((h, ka))

                def qB():
                    h, ka = pend.pop(0)
                    kb2, pB, pw2 = prU.get()
                    kmr = P.mm(pB[0:96, :], rot_mla, Qst[0:96, h, :], waits=[ka] + CST + pw2, sig=True)
                    i1, t1, w1 = t1r.get()
                    i2, t2, w2 = t2r.get()
                    k1 = P.tt('pool', t1[64:96, :], Qst[64:96, h, :], Cm[64:96, T], ALU.mult, [ka] + w1)
                    k2 = P.tt('dve', t2[64:96, :], pB[64:96, :], Sm[64:96, T], ALU.mult, [kmr] + w2)
                    prU.rel(kb2, [k2])
                    k3 = P.tt('dve', Qst[64:96, h, :], t1[64:96, :], t2[64:96, :], ALU.add, [k1, k2, kmr])
                    t1r.rel(i1, [k3])
                    t2r.rel(i2, [k3])
                    Q_done.append(k3)
                    return kmr

                last_rot = None
                for h in range(8):
                    qA(h)
                    if len(pend) > 1:
                        last_rot = qB()
                while pend:
                    last_rot = qB()
                cqn_free[b] = [last_rot]
                kqs = P.dma('sp', QM.rearrange("h r t -> r h t")[:, :, T], Qst[:], qst_sem, Q_done)
                Qst_free[0] = [kqs]

            def Ukv(tt):
                T = slice(tt * TT, (tt + 1) * TT)
                b = tt % 2
                k_ckv = LT[tt]['k_ckv']
                K_done = []
                for h in range(8):
                    kb, ps, pw = prU.get()
                    km = None
                    for j in range(2):
                        km = P.mm(ps[0:64, :], wukv[:, j, h * 128:h * 128 + 64], ckvn[b][:, j, :], start=(j == 0), stop=(j == 1),
                                  waits=([k_ckv] + W1 + pw) if j == 0 else (), sig=(j == 1))
                    ka = P.copy('act', Kst[0:64, h, :], ps[0:64, :], [km] + Kst_free[0])
                    prU.rel(kb, [ka])
                    K_done.append(ka)
                kks = P.dma('sp', KM.rearrange("h r t -> r h t")[0:64, :, T], Kst[:], kst_sem, K_done)
                Kst_free[0] = [kks]
                V_done = []
                last_v_mm = None
                for s_ in range(4):
                    kb, ps, pw = prU.get()
                    km = None
                    for j in range(2):
                        km = P.mm(ps[:].rearrange("p (h d) -> p h d", d=64), ckvn[b][:, j, s_ * 128:(s_ + 1) * 128], wv[:, j, :, 1, :],
                                  start=(j == 0), stop=(j == 1), waits=([k_ckv] + W1 + pw) if j == 0 else (), sig=(j == 1))
                    eng = 'act' if s_ % 2 == 0 else 'dve'
                    ka = P.copy(eng, Vst[:, :, s_, 0:64], ps[:].rearrange("p (h d) -> p h d", d=64), [km, k_vst1] + Vst_free[0])
                    prU.rel(kb, [ka])
                    V_done.append(ka)
                    last_v_mm = km
                ckvn_free[b] = [last_v_mm]
                kvs = P.dma('sp', VM.rearrange("h p (n e) -> p h n e", e=128)[:, :, tt * 4:(tt + 1) * 4, :], Vst[:], vst_sem, V_done)
                Vst_free[0] = [kvs]

            Lcq_front(0)
            Lcq_back(0)
            Lkv_front(0)
            Lkv_back(0)
            for tt in range(NT):
                if tt + 1 < NT:
                    Lcq_front(tt + 1)
                Uq(tt)
                if tt + 1 < NT:
                    Lcq_back(tt + 1)
                    Lkv_front(tt + 1)
                Ukv(tt)
                if tt + 1 < NT:
                    Lkv_back(tt + 1)
            P.barrier()

        if stop_after == 1:
            P.dead = True
        st_ma.close()
        ust_sems = [P.dsem("dust0"), P.dsem("dust1")]
        with ExitStack() as st2:
            kc1 = P.memset('pool', Cd[:], 1.0)
            ks0 = P.memset('pool', Sd[:], 0.0)
            tbl_state = gen_tables_a(st2, 1, 'd', BF16)
            Qh = [sbt(st2, f"Qh{i}", [96, S], BF16) for i in range(2)]
            Kh = [sbt(st2, f"Kh{i}", [96, S], BF16) for i in range(2)]
            Vh = [sbt(st2, f"Vh{i}", [128, 32, 128], BF16) for i in range(2)]
            wz = sbt(st2, "wz", [128, 8, 512], BF16)
            dw2 = P.dsem("dw2")
            k_wz = P.dma('pool', wz[:], w_in_v[:, :, O_ZM:O_ZM + 512], dw2)
            dwc = P.dsem("dwcast")
            for o_ in range(16):
                P.dma('pool', WGb[o_], w_in_v[:, :, O_GM + o_ * 128:O_GM + (o_ + 1) * 128], dwc)
            P.dma('pool', WPMb, w_pm, dwc)
            P.dma('pool', WPDb, w_pd, dwc)
            k_wcast = P.dma('pool', WOb, w_out, dwc)
            ptr = Ring([sbt(st2, f"pt{i}", [128, TT], BF16) for i in range(6)])
            szp = sbt(st2, "szp", [128, NT, TT])
            tz = [sbt(st2, f"tz{i}", [128, TT]) for i in range(2)]
            a1 = [sbt(st2, f"a1{i}", [128, TT]) for i in range(2)]
            rd = [sbt(st2, f"rd{i}", [128, TT]) for i in range(2)]
            ust = [sbt(st2, f"ust{i}", [64, TT], BF16) for i in range(2)]
            lsem = [P.dsem("dql0"), P.dsem("dql1")]
            sr = Ring(banks[0:4])
            por = Ring(banks[4:6])
            pzr = Ring(banks[6:8])
            ld_tok = [None, None]
            buf_free = [[], []]
            ep_free = [[], []]
            ust_free = [[], []]
            tz_free = [[], []]
            szp_tok = [None] * NT
            szp_free = [[] for _ in range(NT)]
            epi = 0
            zi = 0

            def load_head(h):
                b = h % 2
                P.dma('sp', Qh[b][:], QM[h], lsem[b], buf_free[b])
                P.dma('sp', Kh[b][:], KM[h], lsem[b], buf_free[b])
                ld_tok[b] = P.dma('sp', Vh[b][:].rearrange("p n e -> p (n e)"), VM[h], lsem[b], buf_free[b])

            load_head(0)
            for h in range(8):
                b = h % 2
                if h + 1 < 8:
                    load_head(h + 1)
                LD = [ld_tok[b]]
                items = []
                for qt in range(NT):
                    nk = 4 * (qt + 1)
                    for kt in range(nk):
                        items.append((qt, kt, nk))
                st = {}
                qstate = {}

                def emit_S(it):
                    nonlocal zi
                    qt, kt, nk = it
                    if kt == 0:
                        if h % 2 == 0:
                            kbz, pz, pwz = pzr.get()
                            kmz = None
                            for c in range(8):
                                kmz = P.mm(pz[:], wz[:, c, (h // 2) * 128:(h // 2 + 1) * 128], hT[:, c, qt * TT:(qt + 1) * TT],
                                           start=(c == 0), stop=(c == 7), waits=([k_wz] + pwz) if c == 0 else (), sig=(c == 7))
                            zz = zi % 2
                            zi += 1
                            kth = P.act(tz[zz][:], pz[:], AF.Tanh, scale=0.5, waits=[kmz] + tz_free[zz])
                            ksz = P.stt(szp[:, qt, :], tz[zz][:], 1.0, pz[:], ALU.add, ALU.mult, [kth] + szp_free[qt])
                            pzr.rel(kbz, [ksz, kth])
                            tz_free[zz] = [ksz]
                            szp_tok[qt] = ksz
                        kbo, po, pwo = por.get()
                        qstate[qt] = dict(kbo=kbo, po=po, pwo=pwo)
                    diag = kt >= 4 * qt
                    j = kt - 4 * qt
                    n0 = 128 * j if diag else 0
                    kb, ps, pw = sr.get()
                    q0 = qt * TT
                    kslc = Kh[b][0:96, kt * 128:(kt + 1) * 128]
                    if not diag:
                        km = P.mm(ps[:, 0:TT], kslc, Qh[b][0:96, q0:q0 + TT], waits=LD + pw, sig=True)
                    else:
                        P.mm(ps[:, n0:n0 + 128], kslc, Qh[b][0:96, q0 + n0:q0 + n0 + 128], start=True, stop=False, waits=LD + pw)
                        km = P.mm(ps[:, n0:n0 + 128], ident[:], maskab[:, 0:128], start=False, stop=True, waits=CST, sig=(j == 3))
                        if j < 3:
                            km = P.mm(ps[:, n0 + 128:TT], kslc, Qh[b][0:96, q0 + n0 + 128:q0 + TT], sig=True)
                    st[it] = dict(kb=kb, ps=ps, km=km, n0=n0)

                def emit_E(it):
                    s_ = st[it]
                    n0 = s_['n0']
                    ki, pt, pw = ptr.get()
                    ke = P.act(pt[:, n0:TT], s_['ps'][:, n0:TT], AF.Exp, scale=SC_MLA, waits=[s_['km']] + pw)
                    sr.rel(s_['kb'], [ke])
                    s_['ki'] = ki; s_['pt'] = pt; s_['ke'] = ke

                def emit_PV(it):
                    nonlocal epi
                    qt, kt, nk = it
                    s_ = st[it]
                    n0 = s_['n0']
                    q_ = qstate[qt]
                    po = q_['po']
                    last = (kt == nk - 1)
                    kpv = P.mm(po[:, n0:TT], Vh[b][:, kt, :], s_['pt'][:, n0:TT], start=(kt == 0), stop=last,
                               waits=[s_['ke']] + (q_['pwo'] if kt == 0 else []), sig=True)
                    ptr.rel(s_['ki'], [kpv])
                    del st[it]
                    if not last:
                        return
                    e = epi % 2
                    epi += 1
                    T = slice(qt * TT, (qt + 1) * TT)
                    hb = (h % 2) * 64
                    kr = P.recip(rd[e][0:64, :], po[64:128, :], [kpv] + ep_free[e])
                    ka1 = P.tt('dve', a1[e][hb:hb + 64, :], po[0:64, :], rd[e][0:64, :], ALU.mult, [kpv, kr] + ep_free[e])
                    por.rel(q_['kbo'], [ka1, kr])
                    ku = P.stt(ust[e][:], a1[e][hb:hb + 64, :], 0.5, szp[hb:hb + 64, qt, :], ALU.mult, ALU.mult, [ka1, szp_tok[qt]] + ust_free[e])
                    ep_free[e] = [ku]
                    if h % 2 == 1:
                        szp_free[qt] = [ku]
                    kus = P.dma('sp', UM[h * 64:(h + 1) * 64, T], ust[e][:], ust_sems[e], [ku])
                    ust_free[e] = [kus]
                    del qstate[qt]

                n_it = len(items)
                for i in range(n_it + 5):
                    if i < n_it:
                        emit_S(items[i])
                    if 0 <= i - 1 < n_it:
                        emit_E(items[i - 1])
                    if 0 <= i - 5 < n_it:
                        emit_PV(items[i - 5])
                buf_free[b] = [('pe', P.cnt['pe'])]
                if h == 0:
                    gen_tables_b(tbl_state, Cd, Sd, [0, 64], [kc1, ks0])
            P.barrier()
        if stop_after == 2:
            P.dead = True

        with ExitStack() as st3:
            wd = [sbt(st3, f"wd{i}", [128, 8, 384], BF16) for i in range(2)]
            wzd = [sbt(st3, f"wzd{i}", [128, 8, 128], BF16) for i in range(2)]
            wdsem = [P.dsem("dwd0"), P.dsem("dwd1")]
            wzsem = [P.dsem("dwz0"), P.dsem("dwz1")]
            Qd = sbt(st3, "Qd", [128, S], BF16)
            Kd = sbt(st3, "Kd", [128, S], BF16)
            Vd = sbt(st3, "Vd", [128, 32, 2, 128], BF16)
            acc = [sbt(st3, f"acc{i}", [128, S]) for i in range(2)]
            abr = Ring([sbt(st3, f"abf{i}", [128, TT], BF16) for i in range(3)])
            t1r = Ring([sbt(st3, f"t1d{i}", [128, TT]) for i in range(2)])
            t2r = Ring([sbt(st3, f"t2d{i}", [128, TT]) for i in range(2)])
            ptr = Ring([sbt(st3, f"ptd{i}", [128, 512], BF16) for i in range(8)])
            rd = [sbt(st3, f"rdd{i}", [128, TT]) for i in range(2)]
            tz = [sbt(st3, f"tzd{i}", [128, TT]) for i in range(2)]
            szd = [sbt(st3, f"szd{i}", [128, TT]) for i in range(2)]
            ustp = [sbt(st3, f"ustd{i}", [128, TT], BF16) for i in range(2)]
            k_vd1 = P.memset('pool', Vd[:], 1.0)
            pr = Ring(banks[0:4])
            pur = [Ring(banks[4:6]), Ring(banks[6:8])]
            mask512 = sbt(st3, "mask512", [128, 512], BF16)
            dmk = P.dsem("dmk")
            P.dma('pool', mask512[:, 0:256], m01_d, dmk)
            k_m512 = P.dma('pool', mask512[:, 256:512], m01_d, dmk)
            wd_free = [[], []]
            wzd_free = [[], []]
            QK_free = []
            Vd_free = []
            acc_free = [[], []]
            ust_free = [[], []]
            tz_free = [[], []]
            szd_free = [[], []]
            rd_free = [[], []]
            epi = 0
            combos = [(hp, g) for hp in range(4) for g in range(3)]
            wtok = {}

            def load_w(ci):
                hp, g = combos[ci]
                sl = ci % 2
                for s3 in range(3):
                    c0 = O_DIL + (s3 * 3 + g) * 512 + hp * 128
                    wtok[ci] = P.dma('pool', wd[sl][:, :, s3 * 128:(s3 + 1) * 128], w_in_v[:, :, c0:c0 + 128], wdsem[sl], wd_free[sl])

            def load_wz(hp):
                sl = hp % 2
                return P.dma('pool', wzd[sl][:], w_in_v[:, :, O_ZD + hp * 128:O_ZD + (hp + 1) * 128], wzsem[sl], wzd_free[sl])

            fin_pending = [None]
            epi_c = [0]

            def make_fin(hp, accw):
                WZ = [wz_tok[hp]]
                zsl = hp % 2

                def step(tt):
                    kmz = None
                    T = slice(tt * TT, (tt + 1) * TT)
                    kbz, pz, pwz = pr.get()
                    for c in range(8):
                        kmz = P.mm(pz[:], wzd[zsl][:, c, :], hT[:, c, T], start=(c == 0), stop=(c == 7),
                                   waits=(WZ + pwz) if c == 0 else (), sig=(c == 7))
                    e = epi_c[0] % 2
                    epi_c[0] += 1
                    kbe, pe_, pwe = pr.get()
                    ke = P.act(pe_[:], pz[:], AF.Exp, scale=-1.0, waits=[kmz] + pwe)
                    kw = None
                    for hh in range(2):
                        kw = P.stt(szd[e][hh * 64:(hh + 1) * 64, :], pe_[hh * 64:(hh + 1) * 64, :], 1.0, acc[hh][64:128, T],
                                   ALU.add, ALU.mult, [ke] + accw + szd_free[e])
                    pr.rel(kbe, [kw])
                    kl = P.act(rd[e][:], szd[e][:], AF.Ln, waits=[kw] + rd_free[e])
                    szd_free[e] = [kl]
                    kx = P.act(rd[e][:], rd[e][:], AF.Exp, scale=-1.0, waits=[kl])
                    i1, t1, w1 = t1r.get()
                    ka1 = None
                    for hh in range(2):
                        ka1 = P.tt('dve', t1[hh * 64:(hh + 1) * 64, :], acc[hh][0:64, T], pz[hh * 64:(hh + 1) * 64, :], ALU.mult,
                                   [kmz] + accw + w1)
                    pr.rel(kbz, [ke, ka1])
                    ku = P.tt('pool', ustp[e][:], t1[:], rd[e][:], ALU.mult, [ka1, kx] + ust_free[e])
                    t1r.rel(i1, [ku])
                    rd_free[e] = [ku]
                    kus = P.dma('sp', UM[512 + hp * 128:512 + (hp + 1) * 128, T], ustp[e][:], ust_sems[e], [ku])
                    ust_free[e] = [kus]
                    acc_free[0] = [ka1, kw]
                    acc_free[1] = [ka1, kw]
                    if tt == NT - 1:
                        wzd_free[zsl] = [kmz]
                return step

            load_w(0)
            wz_tok = {0: load_wz(0)}
            for ci, (hp, g) in enumerate(combos):
                sl = ci % 2
                if ci + 1 < len(combos):
                    load_w(ci + 1)
                WD = [wtok[ci]]
                d = (1, 4, 16)[g]
                L = S // d

                def tokslice(f0, count):
                    r = f0 // L
                    i0 = f0 % L
                    start = r + d * i0
                    return slice(start, start + d * (count - 1) + 1, d)

                pend = []
                lastB = [None]

                def stageA(which, tt):
                    T = slice(tt * TT, (tt + 1) * TT)
                    kb, ps, pw = pr.get()
                    km = None
                    for c in range(8):
                        km = P.mm(ps[:], wd[sl][:, c, which * 128:(which + 1) * 128], hT[:, c, T], start=(c == 0), stop=(c == 7),
                                  waits=(WD + pw) if c == 0 else (), sig=(c == 7))
                    ia, ab, wa = abr.get()
                    ka = P.act(ab[:], ps[:], AF.Copy, waits=[km] + wa)
                    pr.rel(kb, [ka])
                    pend.append((which, tt, ia, ab, ka))

                def stageB():
                    which, tt, ia, ab, ka = pend.pop(0)
                    dst = Qd if which == 0 else Kd
                    T = slice(tt * TT, (tt + 1) * TT)
                    kb2, pB, pw2 = pr.get()
                    kmr = P.mm(pB[:], rotm[:, 96:224], ab[:], waits=[ka] + CST + pw2, sig=True)
                    i1, t1, w1 = t1r.get()
                    i2, t2, w2 = t2r.get()
                    n_i = TT // d
                    nat = "p (i r) -> p i r"
                    t1d = t1[:].rearrange("p (r i) -> p i r", r=d)
                    t2d = t2[:].rearrange("p (r i) -> p i r", r=d)
                    k1 = P.tt('dve', t1d, ab[:].rearrange(nat, r=d), Cd[:, T].rearrange(nat, r=d), ALU.mult, [ka] + w1)
                    k2 = P.tt('dve', t2d, pB[:].rearrange(nat, r=d), Sd[:, T].rearrange(nat, r=d), ALU.mult, [kmr] + w2)
                    pr.rel(kb2, [k2])
                    abr.rel(ia, [kmr, k1])
                    dv = dst[:, :].rearrange("p (r i) -> p r i", r=d)[:, :, tt * n_i:(tt + 1) * n_i]
                    k3 = P.tt('pool', dv, t1[:].rearrange("p (r i) -> p r i", r=d),
                              t2[:].rearrange("p (r i) -> p r i", r=d), ALU.add, [k1, k2] + QK_free)
                    t1r.rel(i1, [k3])
                    t2r.rel(i2, [k3])
                    lastB[0] = k3

                for which in (0, 1):
                    for tt in range(NT):
                        stageA(which, tt)
                        if len(pend) > 1:
                            stageB()
                if stop_after == 2.2:
                    while pend:
                        stageB()
                    P.barrier()
                    P.dead = True
                V_ready = []
                km = None
                for nb4 in range(8):
                    kb, ps, pw = pr.get()
                    for q4 in range(4):
                        n = nb4 * 4 + q4
                        tsl = tokslice(n * 128, 128)
                        for c in range(8):
                            km = P.mm(ps[:, q4 * 128:(q4 + 1) * 128], hT[:, c, tsl], wd[sl][:, c, 256:384], start=(c == 0), stop=(c == 7),
                                      waits=(WD + pw) if (c == 0 and q4 == 0) else (), sig=(c == 7 and q4 == 3))
                    eng = 'act' if nb4 % 2 == 0 else 'dve'
                    ka = P.copy(eng, Vd[:, nb4 * 4:(nb4 + 1) * 4, :, 0:64], ps[:].rearrange("p (n h d) -> p n h d", n=4, h=2),
                                [km, k_vd1] + Vd_free)
                    pr.rel(kb, [ka])
                    V_ready.append(ka)
                    if nb4 == 0:
                        while pend:
                            stageB()
                    if fin_pending[0] is not None:
                        fin_pending[0](nb4)
                        if nb4 == 7:
                            fin_pending[0] = None
                wd_free[sl] = [km]
                if g == 0 and hp + 1 < 4:
                    wz_tok[hp + 1] = load_wz(hp + 1)
                QK_ready = [('pool', P.cnt['pool']), ('dve', P.cnt['dve'])]
                if stop_after == 2.3:
                    P.barrier()
                    P.dead = True
                nbp = L // 128
                pstate = [{}, {}]

                def get_pu(hh, n):
                    bk = n // 4
                    if bk not in pstate[hh]:
                        kbu, pu, pwu = pur[hh].get()
                        pstate[hh][bk] = dict(kb=kbu, pu=pu, pw=pwu, first=True)
                    return pstate[hh][bk]

                sts = {}

                def dS(n):
                    bb = n % nbp
                    N = 256 if bb + 1 < nbp else 128
                    tl = []
                    for hh in range(2):
                        p0 = hh * 64
                        kb, ps, pw = pr.get()
                        km_ = P.mm(ps[:, 0:N], Kd[p0:p0 + 64, n * 128:(n + 1) * 128], Qd[p0:p0 + 64, n * 128:n * 128 + N],
                                   waits=QK_ready + pw, sig=True)
                        tl.append((kb, ps, km_))
                    sts[n] = dict(tl=tl, N=N)

                def dE(n):
                    s_ = sts[n]
                    N = s_['N']
                    ki, pt, pw = ptr.get()
                    kes = []
                    for hh in range(2):
                        kb, ps, km_ = s_['tl'][hh]
                        ke = P.act(pt[:, hh * 256:hh * 256 + N], ps[:, 0:N], AF.Exp, scale=SC_DIL, waits=[km_] + pw)
                        pr.rel(kb, [ke])
                        kes.append(ke)
                    if N == 256:
                        kmk = P.tt('dve', pt[:, 0:512], pt[:, 0:512], mask512[:, 0:512], ALU.mult, kes + [k_m512])
                    else:
                        ptv = pt[:, :].rearrange("p (h c) -> p h c", h=2)[:, :, 0:128]
                        kmk = P.tt('dve', ptv, ptv, mask512[:, :].rearrange("p (h c) -> p h c", h=2)[:, :, 0:128], ALU.mult, kes + [k_m512])
                    s_['ki'] = ki; s_['pt'] = pt; s_['ke'] = kmk

                def dPV(n):
                    s_ = sts[n]
                    N = s_['N']
                    bb = n % nbp
                    kpv = None
                    for hh in range(2):
                        u = get_pu(hh, n)
                        c0 = (n % 4) * 128
                        w = [s_['ke']] + V_ready + (u['pw'] if u['first'] else [])
                        u['first'] = False
                        kpv = P.mm(u['pu'][:, c0:c0 + 128], Vd[:, n, hh, :], s_['pt'][:, hh * 256:hh * 256 + 128], start=(bb == 0), stop=True,
                                   waits=w, sig=True)
                        if N == 256:
                            u2 = get_pu(hh, n + 1)
                            c1 = ((n + 1) % 4) * 128
                            w2 = (u2['pw'] if u2['first'] else [])
                            u2['first'] = False
                            kpv = P.mm(u2['pu'][:, c1:c1 + 128], Vd[:, n, hh, :], s_['pt'][:, hh * 256 + 128:hh * 256 + 256], start=True, stop=False,
                                       waits=w2, sig=True)
                    ptr.rel(s_['ki'], [kpv])
                    del sts[n]
                    if n % 4 == 3:
                        bk = n // 4
                        f0 = bk * 512
                        r = f0 // L
                        i0 = f0 % L
                        for hh in range(2):
                            u = pstate[hh][bk]
                            accv = acc[hh][:, :].rearrange("p (i r) -> p r i", r=d)
                            if L >= 512:
                                dstv = accv[:, r, i0:i0 + 512]
                                srcv = u['pu'][:, :]
                            else:
                                nr = 512 // L
                                dstv = accv[:, r:r + nr, :]
                                srcv = u['pu'][:, :].rearrange("p (r i) -> p r i", r=nr)
                            if g == 0:
                                eng = 'act' if hh == 0 else 'dve'
                                ka_ = P.copy(eng, dstv, srcv, [kpv] + acc_free[hh])
                            else:
                                ka_ = P.tt('dve', dstv, dstv, srcv, ALU.add, [kpv] + acc_free[hh])
                            pur[hh].rel(u['kb'], [ka_])

                for i in range(32 + 6):
                    if i < 32:
                        dS(i)
                    if 0 <= i - 1 < 32:
                        dE(i - 1)
                    if 0 <= i - 6 < 32:
                        dPV(i - 6)
                if g == 0:
                    acc_free = [[], []]
                QK_free = [('pe', P.cnt['pe'])]
                Vd_free = [('pe', P.cnt['pe'])]
                if stop_after == 2.4 or (stop_after == 2.5 and ci == 1) or (stop_after == 2.6 and ci == 2):
                    P.barrier()
                    P.dead = True
                if g == 2:
                    fin_pending[0] = make_fin(hp, [('dve', P.cnt['dve']), ('act', P.cnt['act'])])
            if fin_pending[0] is not None:
                for tt_ in range(NT):
                    fin_pending[0](tt_)
                fin_pending[0] = None
            P.barrier()

        st_cd.close()
        if stop_after == 3:
            P.dead = True
        with ExitStack() as st4:
            wg = sbt(st4, "wg", [128, 16, 8, 128], BF16)
            wpm = sbt(st4, "wpm", [128, 4, 1024], BF16)
            wpd = sbt(st4, "wpd", [128, 4, 1024], BF16)
            wout = sbt(st4, "wout", [128, 8, 1024], BF16)
            gpost = sbt(st4, "gpost", [128, 1024])
            dw3 = [P.dsem(f"dw3_{i}") for i in range(4)]
            k_wg_oc = []
            for oc_ in range(8):
                sm_ = P.dsem(f"dwg{oc_}")
                P.dma('sp', wg[:, oc_, :, :], WGb[oc_], sm_, [k_wcast])
                k_wg_oc.append(P.dma('sp', wg[:, 8 + oc_, :, :], WGb[8 + oc_], sm_, [k_wcast]))
                if oc_ == 0:
                    k_wpm = P.dma('sp', wpm[:], WPMb.rearrange("(c p) n -> p c n", p=128), dw3[1], [k_wcast])
                    k_wpd = P.dma('sp', wpd[:], WPDb.rearrange("(c p) n -> p c n", p=128), dw3[1], [k_wcast])
                    k_wpm = k_wpd
            k_wout = P.dma('sp', wout[:], WOb.rearrange("(c p) n -> p c n", p=128), dw3[2], [k_wcast])
            k_gp = P.dma('sp', gpost[:], gpost_d.broadcast_to([128, 1024]), dw3[3])
            um = [sbt(st4, f"um{i}", [128, 4, TT], BF16) for i in range(2)]
            ud = [sbt(st4, f"ud{i}", [128, 4, TT], BF16) for i in range(2)]
            usem = [P.dsem("dul0"), P.dsem("dul1")]
            sg = [sbt(st4, f"sg{i}", [128, TT]) for i in range(2)]
            m1 = sbt(st4, "m1", [128, TT])
            m2 = sbt(st4, "m2", [128, TT])
            mg = [sbt(st4, f"mg{i}", [128, 8, TT], BF16) for i in range(2)]
            xt = [sbt(st4, f"xt{i}", [128, 1024]) for i in range(3)]
            o1 = [sbt(st4, f"o1_{i}", [128, 1024]) for i in range(2)]
            junk = sbt(st4, "junk", [128, 512])
            ssq = [sbt(st4, f"ssq{i}", [128, 4]) for i in range(2)]
            xsem = [P.dsem(f"dxt{i}") for i in range(3)]
            ysem = [P.dsem(f"dy{i}") for i in range(3)]
            pr = Ring(banks[0:4])
            por = Ring(banks[4:8])
            u_free = [[], []]
            u_tok = [None, None]
            sg_free = [[], []]
            mm_free = [[], []]
            mg_free = [[], []]
            mg_done = [None, None]
            xt_free = [[], [], []]
            o1_free = [[], []]
            ssq_free = [[], []]
            junk_free = [[]]
            out_toks = []
            xi = [0]
            last_umm = [None, None]
            last_omm = [None, None]

            def load_u(tt):
                ub = tt % 2
                T = slice(tt * TT, (tt + 1) * TT)
                P.dma('sp', um[ub][:], UM[0:512, T].rearrange("(c p) t -> p c t", p=128), usem[ub], u_free[ub])
                u_tok[ub] = P.dma('sp', ud[ub][:], UM[512:1024, T].rearrange("(c p) t -> p c t", p=128), usem[ub], u_free[ub])

            def G_oc(tt, oc):
                ub = tt % 2
                T = slice(tt * TT, (tt + 1) * TT)
                grp = []
                for (wt, off, src, nk, kw) in ((wg, 0, None, 8, k_wg_oc[oc]), (wg, 8, None, 8, k_wg_oc[oc]), (wpm, 0, um[ub], 4, k_wpm), (wpd, 0, ud[ub], 4, k_wpd)):
                    kb, ps, pw = pr.get()
                    km = None
                    for c in range(nk):
                        rhs = hT[:, c, T] if src is None else src[:, c, :]
                        lw = wt[:, off + oc, c, :] if src is None else wt[:, c, oc * 128:(oc + 1) * 128]
                        km = P.mm(ps[:], lw, rhs, start=(c == 0), stop=(c == nk - 1),
                                  waits=([kw, u_tok[ub]] + pw) if c == 0 else (), sig=(c == nk - 1))
                    grp.append((kb, ps, km))
                last_umm[ub] = grp[3][2]
                ks1 = P.act(sg[0][:], grp[0][1][:], AF.Sigmoid, waits=[grp[0][2]] + sg_free[0])
                pr.rel(grp[0][0], [ks1])
                ks2 = P.act(sg[1][:], grp[1][1][:], AF.Sigmoid, waits=[grp[1][2]] + sg_free[1])
                pr.rel(grp[1][0], [ks2])
                k1 = P.tt('dve', m1[:], sg[0][:], grp[2][1][:], ALU.mult, [ks1, grp[2][2]] + mm_free[0])
                pr.rel(grp[2][0], [k1])
                k2 = P.tt('dve', m2[:], sg[1][:], grp[3][1][:], ALU.mult, [ks2, grp[3][2]] + mm_free[1])
                pr.rel(grp[3][0], [k2])
                sg_free[0] = [k1]
                sg_free[1] = [k2]
                k3 = P.tt('pool', mg[ub][:, oc, :], m1[:], m2[:], ALU.add, [k1, k2] + (mg_free[ub] if oc == 0 else []))
                mm_free[0] = [k3]
                mm_free[1] = [k3]
                mg_done[ub] = k3
                if oc == 7:
                    u_free[ub] = [last_umm[ub]]

            def O_s(tt, s):
                ub = tt % 2
                r0 = tt * TT + s * 128
                xb = xi[0] % 3
                ob = xi[0] % 2
                xi[0] += 1
                kx = P.dma('sp', xt[xb][:], x[r0:r0 + 128, :], xsem[xb], xt_free[xb])
                halves = []
                for hf in range(2):
                    kb, po, pw = por.get()
                    km = None
                    for oc in range(8):
                        km = P.mm(po[:], mg[ub][:, oc, s * 128:(s + 1) * 128], wout[:, oc, hf * 512:(hf + 1) * 512], start=(oc == 0), stop=(oc == 7),
                                  waits=([k_wout, mg_done[ub]] + pw) if oc == 0 else (), sig=(oc == 7))
                    kq = P.act(junk[:], po[:], AF.Square, accum=ssq[ob][:, hf:hf + 1], waits=[km] + junk_free[0] + ssq_free[ob])
                    junk_free[0] = [kq]
                    halves.append((kb, po, km, kq))
                    last_omm[ub] = km
                ka = P.tt('dve', ssq[ob][:, 2:3], ssq[ob][:, 0:1], ssq[ob][:, 1:2], ALU.add, [halves[0][3], halves[1][3]])
                kr = P.act(ssq[ob][:, 3:4], ssq[ob][:, 2:3], AF.Sqrt, scale=1.0 / 1024, bias=epsb[:], waits=[ka])
                kr2 = P.recip(ssq[ob][:, 3:4], ssq[ob][:, 3:4], [kr])
                ko = None
                for hf in range(2):
                    H = slice(hf * 512, (hf + 1) * 512)
                    ko = P.stt(o1[ob][:, H], halves[hf][1][:], ssq[ob][:, 3:4], gpost[:, H], ALU.mult, ALU.mult,
                               waits=[kr2, k_gp] + o1_free[ob])
                    por.rel(halves[hf][0], [ko])
                ssq_free[ob] = [ko]
                kf = P.tt('pool', xt[xb][:], o1[ob][:], xt[xb][:], ALU.add, [ko, kx])
                o1_free[ob] = [kf]
                kys = P.dma('sp', y[r0:r0 + 128, :], xt[xb][:], ysem[xb], [kf])
                xt_free[xb] = [kys]
                out_toks.append(kys)
                if s == 3:
                    mg_free[ub] = [last_omm[ub]]

            load_u(0)
            for oc in range(8):
                G_oc(0, oc)
            for tt in range(NT):
                if tt + 1 < NT:
                    load_u(tt + 1)
                for s in range(4):
                    O_s(tt, s)
                    if tt + 1 < NT:
                        G_oc(tt + 1, 2 * s)
                        G_oc(tt + 1, 2 * s + 1)
            P.wait('sp', out_toks)
            P.barrier()
        P.emit()
    return nc


_NC_CACHE = {}


def _consts():
    p = np.arange(128)[:, None]
    f = np.arange(128)[None, :]
    ident = np.eye(128, dtype=np.float32)
    maskab = np.zeros((128, 256), np.float32)
    maskab[:, 0:128] = np.where(p > f, NEG, 0.0)
    maskab[:, 128:256] = np.where(f > p, NEG, 0.0)
    rot = np.zeros((128, 224), np.float32)
    for i in range(16):
        rot[80 + i, 64 + i] = -1.0
        rot[64 + i, 80 + i] = 1.0
    for hb in (0, 64):
        for i in range(8):
            rot[hb + 8 + i, 96 + hb + i] = -1.0
            rot[hb + i, 96 + hb + 8 + i] = 1.0
    theta = np.float32(500000.0)
    inv_mla = (theta ** (-np.arange(0, 32, 2, dtype=np.float32) / np.float32(32))).astype(np.float32)
    inv_dil = (theta ** (-np.arange(0, 16, 2, dtype=np.float32) / np.float32(16))).astype(np.float32)
    freqs = np.zeros((128, 2), np.float32)
    for q in range(4):
        for i in range(32):
            freqs[q * 32 + i, 0] = inv_mla[i % 16]
            freqs[q * 32 + i, 1] = inv_dil[i % 8] if i < 16 else 0.0
    mask01 = (maskab == 0.0).astype(np.float32)
    return dict(ident=ident, maskab=maskab, rotm=rot, freqs=freqs, mask01=mask01)


def make_in_maps(x, positions, pre_norm_g, w_in, q_norm_g, w_uq, kv_norm_g, w_ukv,
                 w_proj_mla, w_proj_dil, w_out, post_norm_g):
    cst = _consts()
    f32 = np.float32
    shared = dict(
        w_in=np.ascontiguousarray(w_in[0], f32), w_uq=np.ascontiguousarray(w_uq[0], f32),
        w_ukv=np.ascontiguousarray(w_ukv[0], f32), w_pm=np.ascontiguousarray(w_proj_mla[0], f32),
        w_pd=np.ascontiguousarray(w_proj_dil[0], f32), w_out=np.ascontiguousarray(w_out[0], f32),
        gpre=np.ascontiguousarray(np.asarray(pre_norm_g[0], f32).reshape(8, 128).T),
        gq=np.ascontiguousarray(np.asarray(q_norm_g[0], f32).reshape(3, 128).T),
        gkv=np.ascontiguousarray(np.asarray(kv_norm_g[0], f32).reshape(2, 128).T),
        gpost=np.ascontiguousarray(np.asarray(post_norm_g[0], f32).reshape(1, 1024)),
        **cst)
    maps = []
    for b in range(8):
        xb = np.asarray(x[b], f32)
        m = dict(shared)
        m["x"] = np.ascontiguousarray(xb)
        m["xT"] = np.ascontiguousarray(xb.T)
        m["pos"] = np.ascontiguousarray(np.asarray(positions[b], np.int32).reshape(1, S))
        maps.append(m)
    return maps


def kernel(x, positions, pre_norm_g, w_in, q_norm_g, w_uq, kv_norm_g, w_ukv,
           w_proj_mla, w_proj_dil, w_out, post_norm_g):
    if "nc" not in _NC_CACHE:
        _NC_CACHE["nc"] = build(False)
    nc = _NC_CACHE["nc"]
    maps = make_in_maps(x, positions, pre_norm_g, w_in, q_norm_g, w_uq, kv_norm_g, w_ukv,
                        w_proj_mla, w_proj_dil, w_out, post_norm_g)
    res = run_bass_kernel_spmd(nc, maps, core_ids=list(range(8)))
    out = np.stack([np.asarray(r["y"], np.float32) for r in res.results], axis=0)
    return out
```
